# Optimizing a Trainium2 kernel written in Bass

```python
import jax, jax.numpy as jnp
from jax import lax
import numpy as np

D_MODEL = 1024
BATCH = 4
SEQ = 4096
DEPTH = 2

GRID_W = 64
HEAD_DIM = 64
NA_HEADS = (D_MODEL // 2) // HEAD_DIM
GQA_Q_HEADS = (D_MODEL // 2) // HEAD_DIM
GQA_KV_HEADS = GQA_Q_HEADS // 4
NA_KH_MAX = 8
NA_KW = 16
ROPE_THETA = 10000.0
Q_BLOCK = 128
NA_WIDTH = NA_HEADS * HEAD_DIM
GQA_WIDTH = GQA_Q_HEADS * HEAD_DIM
KV_WIDTH = GQA_KV_HEADS * HEAD_DIM
MIX_WIDTH = NA_WIDTH + GQA_WIDTH
IN_SPLITS = [NA_WIDTH, NA_WIDTH, NA_WIDTH, GQA_WIDTH, KV_WIDTH, KV_WIDTH]
IN_WIDTH = sum(IN_SPLITS)
IN_SPLIT_POINTS = [int(p) for p in np.cumsum(IN_SPLITS)[:-1]]
MEM_LEN = 256
MEM_HEADS = 4
MEM_HEAD_DIM = 128
MEM_WIDTH = MEM_HEADS * MEM_HEAD_DIM
D_FF = ((8 * D_MODEL // 3 + 127) // 128) * 128
CONV_W = 3
EPS = 1e-6
NEG_INF = -1e30

kernel_name = "hybrid_na_gqa_memxattn_convffn_encoder"


def rms_norm(x, g):
    x32 = x.astype(jnp.float32)
    y = x32 * lax.rsqrt(jnp.mean(x32 * x32, axis=-1, keepdims=True) + EPS)
    return (y * g.astype(jnp.float32)).astype(x.dtype)


def rope_tables(s):
    t = jnp.arange(s)
    pos = jnp.stack([t // GRID_W, t % GRID_W], axis=-1).astype(jnp.float32)
    n_f = HEAD_DIM // 4
    inv_freq = ROPE_THETA ** (-jnp.arange(n_f, dtype=jnp.float32) / n_f)
    ang = pos[:, :, None] * inv_freq
    return jnp.cos(ang), jnp.sin(ang)


def apply_axial_rope(x, cos, sin):
    b, s, h, dh = x.shape
    xr = x.astype(jnp.float32).reshape(b, s, h, 2, 2, dh // 4)
    x1, x2 = xr[..., 0, :], xr[..., 1, :]
    c = cos[None, :, None]
    sn = sin[None, :, None]
    out = jnp.stack([x1 * c - x2 * sn, x1 * sn + x2 * c], axis=-2)
    return out.reshape(b, s, h, dh).astype(x.dtype)


def neighbourhood_attention(q, k, v, rpb, rows):
    b, s, h, dh = q.shape
    kh = min(NA_KH_MAX, rows)
    q = q.reshape(b, rows, GRID_W, h, dh)
    k = k.reshape(b, rows, GRID_W, h, dh)
    v = v.reshape(b, rows, GRID_W, h, dh)
    r = jnp.arange(rows)
    row_start = jnp.clip(r - kh // 2, 0, rows - kh)
    row_idx = row_start[:, None] + jnp.arange(kh)[None, :]
    k_rows = k[:, row_idx]
    v_rows = v[:, row_idx]
    scores = jnp.einsum('brqhd,brikhd->bhrqik', q, k_rows,
                        preferred_element_type=jnp.float32) * (dh ** -0.5)
    c = jnp.arange(GRID_W)
    col_start = jnp.clip(c - NA_KW // 2, 0, GRID_W - NA_KW)
    col_valid = (c[None, :] >= col_start[:, None]) & (c[None, :] < col_start[:, None] + NA_KW)
    row_off = row_idx - r[:, None] + (NA_KH_MAX - 1)
    col_off = jnp.clip(c[None, :] - c[:, None], -(NA_KW - 1), NA_KW - 1) + (NA_KW - 1)
    bias = rpb[:, row_off[:, None, :, None], col_off[None, :, None, :]]
    scores = scores + bias[None].astype(jnp.float32)
    scores = jnp.where(col_valid[None, None, None, :, None, :], scores, NEG_INF)
    p = jax.nn.softmax(scores, axis=(-2, -1))
    o = jnp.einsum('bhrqik,brikhd->brqhd', p.astype(v.dtype), v_rows)
    return o.reshape(b, s, h * dh)


def gqa_block_attention(q, k, v):
    b, s, hq, dh = q.shape
    hkv = k.shape[2]
    g = hq // hkv
    nblk = s // Q_BLOCK
    qb = q.reshape(b, nblk, Q_BLOCK, hkv, g, dh).transpose(1, 0, 2, 3, 4, 5)
    scale = dh ** -0.5

    def one_block(qi):
        sc = jnp.einsum('bqkgd,bskd->bkgqs', qi, k, preferred_element_type=jnp.float32) * scale
        p = jax.nn.softmax(sc, axis=-1)
        return jnp.einsum('bkgqs,bskd->bqkgd', p.astype(v.dtype), v)

    o = lax.map(one_block, qb)
    return o.transpose(1, 0, 2, 3, 4, 5).reshape(b, s, hq * dh)


def hybrid_mixer(h, w_in, rpb, q_norm, k_norm, w_out, rows, cos, sin):
    b, s, _ = h.shape
    proj = h @ w_in
    na_q, na_k, na_v, g_q, g_k, g_v = jnp.split(proj, IN_SPLIT_POINTS, axis=-1)
    na_shape = (b, s, NA_HEADS, HEAD_DIM)
    na_out = neighbourhood_attention(na_q.reshape(na_shape), na_k.reshape(na_shape),
                                     na_v.reshape(na_shape), rpb, rows)
    g_q = apply_axial_rope(rms_norm(g_q.reshape(b, s, GQA_Q_HEADS, HEAD_DIM), q_norm), cos, sin)
    g_k = apply_axial_rope(rms_norm(g_k.reshape(b, s, GQA_KV_HEADS, HEAD_DIM), k_norm), cos, sin)
    g_v = g_v.reshape(b, s, GQA_KV_HEADS, HEAD_DIM)
    gqa_out = gqa_block_attention(g_q, g_k, g_v)
    return jnp.concatenate([na_out, gqa_out], axis=-1) @ w_out


def memory_cross_attention(h, mem_n, w_q, w_kv, w_o):
    b, s, _ = h.shape
    m = mem_n.shape[1]
    q = (h @ w_q).reshape(b, s, MEM_HEADS, MEM_HEAD_DIM)
    kv = (mem_n @ w_kv).reshape(b, m, 2, MEM_HEADS, MEM_HEAD_DIM)
    k, v = kv[:, :, 0], kv[:, :, 1]
    sc = jnp.einsum('bqhd,bmhd->bhqm', q, k, preferred_element_type=jnp.float32) * (MEM_HEAD_DIM ** -0.5)
    p = jax.nn.softmax(sc, axis=-1)
    o = jnp.einsum('bhqm,bmhd->bqhd', p.astype(v.dtype), v).reshape(b, s, MEM_WIDTH)
    return o @ w_o


def conv_ffn(h, w_up, conv_w, conv_b, w_down):
    u = h @ w_up
    up = jnp.pad(u, ((0, 0), (1, 1), (0, 0)))
    u = up[:, :-2] * conv_w[0] + up[:, 1:-1] * conv_w[1] + up[:, 2:] * conv_w[2] + conv_b
    gate, val = jnp.split(u, 2, axis=-1)
    return (jax.nn.silu(gate) * val) @ w_down


def setup_inputs(seed: int = 0) -> dict:
    key = jax.random.key(seed)
    ks = jax.random.split(key, 24)
    f32 = jnp.float32

    def normal(k, shape, scale):
        return jax.random.normal(k, shape, f32) * scale

    def gain(k, shape):
        return 1.0 + 0.02 * jax.random.normal(k, shape, f32)

    L = DEPTH
    return {
        "x": normal(ks[0], (BATCH, SEQ, D_MODEL), 1.0),
        "mem": normal(ks[1], (BATCH, MEM_LEN, D_MODEL), 1.0),
        "norm_mix": gain(ks[2], (L, D_MODEL)),
        "w_in": normal(ks[3], (L, D_MODEL, IN_WIDTH), D_MODEL ** -0.5),
        "na_rpb": normal(ks[4], (L, NA_HEADS, 2 * NA_KH_MAX - 1, 2 * NA_KW - 1), 0.1),
        "gqa_q_norm": gain(ks[5], (L, HEAD_DIM)),
        "gqa_k_norm": gain(ks[6], (L, HEAD_DIM)),
        "w_out": normal(ks[7], (L, MIX_WIDTH, D_MODEL), MIX_WIDTH ** -0.5),
        "norm_mem_q": gain(ks[8], (L, D_MODEL)),
        "norm_mem_kv": gain(ks[9], (L, D_MODEL)),
        "w_mem_q": normal(ks[10], (L, D_MODEL, MEM_WIDTH), D_MODEL ** -0.5),
        "w_mem_kv": normal(ks[11], (L, D_MODEL, 2 * MEM_WIDTH), D_MODEL ** -0.5),
        "w_mem_o": normal(ks[12], (L, MEM_WIDTH, D_MODEL), MEM_WIDTH ** -0.5),
        "norm_ffn": gain(ks[13], (L, D_MODEL)),
        "w_up": normal(ks[14], (L, D_MODEL, 2 * D_FF), D_MODEL ** -0.5),
        "conv_w": normal(ks[15], (L, CONV_W, 2 * D_FF), CONV_W ** -0.5),
        "conv_b": normal(ks[16], (L, 2 * D_FF), 0.02),
        "w_down": normal(ks[17], (L, D_FF, D_MODEL), D_FF ** -0.5),
        "norm_final": gain(ks[18], (D_MODEL,)),
    }


def reference(x, mem, norm_mix, w_in, na_rpb, gqa_q_norm, gqa_k_norm, w_out,
              norm_mem_q, norm_mem_kv, w_mem_q, w_mem_kv, w_mem_o,
              norm_ffn, w_up, conv_w, conv_b, w_down, norm_final):
    s = x.shape[1]
    rows = s // GRID_W
    cos, sin = rope_tables(s)
    for l in range(DEPTH):
        x = x + hybrid_mixer(rms_norm(x, norm_mix[l]), w_in[l], na_rpb[l],
                             gqa_q_norm[l], gqa_k_norm[l], w_out[l], rows, cos, sin)
        x = x + memory_cross_attention(rms_norm(x, norm_mem_q[l]), rms_norm(mem, norm_mem_kv[l]),
                                       w_mem_q[l], w_mem_kv[l], w_mem_o[l])
        x = x + conv_ffn(rms_norm(x, norm_ffn[l]), w_up[l], conv_w[l], conv_b[l], w_down[l])
    return rms_norm(x, norm_final)
```

```python
import numpy as np
import concourse.bass as bass
import concourse.mybir as mybir
from concourse.bass_utils import run_bass_kernel_spmd

F32 = mybir.dt.float32
BF16 = mybir.dt.bfloat16
AF = mybir.ActivationFunctionType
ALU = mybir.AluOpType

L = 2
D = 1024
T = 2048
NCH = 8
DFF = 2816
NJ = 44
EPS = 1e-6
PAIRS = [[0, 1], [2, 3], [4, 5], [6, 7]]

PV_NORM = 0
PV_FINAL = 64
PV_GQ = 72
PV_GK = 74
PV_FLAGL = 76
PV_FLAGR = 77
PV_CW = 80
PV_N = 80 + 2 * 176


class Buf:
    __slots__ = ("w", "r")

    def __init__(self):
        self.w = []
        self.r = []


class Sem:
    def __init__(self, h):
        self.h = h
        self.cnt = 0


class Eng:
    def __init__(self, K, name):
        self.K = K
        self.name = name
        self.sem = K.new_sem(name)
        self.items = []
        self.seen = {}
        self.pend_r = []
        self.pend_w = []
        self.dsems = []
        self.dsi = 0

    def _wait(self, dep):
        s, v = dep
        if self.seen.get(s, 0) >= v:
            return
        self.seen[s] = v
        self.items.append(("w", s.h, v))

    def _deps(self, reads, writes):
        for b in reads:
            for d in b.w:
                self._wait(d)
        for b in writes:
            for d in b.w:
                self._wait(d)
            for d in b.r:
                self._wait(d)

    def _register(self, dep, reads, writes):
        for b in reads:
            b.r.append(dep)
            if len(b.r) > 12:
                m = {}
                for s, v in b.r:
                    if m.get(s, 0) < v:
                        m[s] = v
                b.r = list(m.items())
        for b in writes:
            b.w = [dep]
            b.r = []

    def op(self, fn, reads=(), writes=(), signal=True):
        self._deps(reads, writes)
        if signal:
            self.sem.cnt += 1
            dep = (self.sem, self.sem.cnt)
            self.items.append(("o", fn, self.sem.h, 1))
            self._register(dep, list(reads) + self.pend_r, list(writes) + self.pend_w)
            self.pend_r = []
            self.pend_w = []
            return dep
        self.items.append(("o", fn, None, 0))
        self.pend_r += list(reads)
        self.pend_w += list(writes)
        return None

    def dma(self, out, in_, reads=(), writes=()):
        self._deps(reads, writes)
        if not self.dsems:
            self.dsems = [self.K.new_sem(self.name + "_d%d" % i) for i in range(6)]
        s = self.dsems[self.dsi % len(self.dsems)]
        self.dsi += 1
        s.cnt += 16
        dep = (s, s.cnt)
        self.items.append(("o", lambda e, o=out, i=in_: e.dma_start(out=o, in_=i), s.h, 16))
        self._register(dep, reads, writes)
        return dep

    def replay(self, e):
        for it in self.items:
            if it[0] == "w":
                e.wait_ge(it[1], it[2])
            else:
                ins = it[1](e)
                if it[2] is not None:
                    ins.then_inc(it[2], it[3])


class K:
    def __init__(self, nc, stack):
        self.nc = nc
        self.stack = stack
        self.sems = []
        self.pe = Eng(self, "pe")
        self.act = Eng(self, "act")
        self.dve = Eng(self, "dve")
        self.pool = Eng(self, "pool")
        self.sp = Eng(self, "sp")
        self.engs = [self.pe, self.act, self.dve, self.pool, self.sp]

    def new_sem(self, name):
        h = self.stack.enter_context(self.nc.semaphore(name))
        s = Sem(h)
        self.sems.append(s)
        return s

    def barrier(self):
        for e in self.engs:
            if e.pend_r or e.pend_w:
                raise RuntimeError("pending unsignaled ops at barrier on " + e.name)
        for e in self.engs:
            if e is self.pool:
                continue
            for s in self.sems:
                if s.cnt > 0:
                    e._wait((s, s.cnt))


from contextlib import ExitStack


def build_nc(nlayers=L, final_norm=True, stop=None):
    nc = bass.Bass("TRN2", target_bir_lowering=False)
    stack = ExitStack()
    k = K(nc, stack)
    pe, act, dve, pool, sp = k.pe, k.act, k.dve, k.pool, k.sp

    def din(name, shape, dt=F32):
        return nc.dram_tensor(name, list(shape), dt, kind="ExternalInput").ap()

    x_d = din("x", [T, D])
    mem_d = din("mem", [256, D])
    w_in_d = din("w_in", [L, D, 2304])
    w_out_d = din("w_out", [L, D, D])
    w_mq_d = din("w_mem_q", [L, D, 512])
    w_mkv_d = din("w_mem_kv", [L, D, 1024])
    w_mo_d = din("w_mem_o", [L, 512, D])
    w_up_d = din("w_up", [L, D, 2 * DFF])
    w_dn_d = din("w_down", [L, DFF, D])
    pv_d = din("pv", [128, PV_N])
    ident_d = din("ident", [128, 128])
    rt_d = din("rmatT", [128, 128])
    cos_d = din("cosT", [128, T])
    sin_d = din("sinT", [128, T])
    nag_d = din("nag", [L, 8, 128, 896])
    namask_d = din("namask", [3, 128, 896])
    out_d = nc.dram_tensor("out", [T, D], F32, kind="ExternalOutput").ap()

    e1_in = [nc.dram_tensor("e1in%d" % l, [384, 2048], BF16) for l in range(L)]
    e1_out = [nc.dram_tensor("e1out%d" % l, [768, 2048], BF16) for l in range(L)]
    e2_in = [nc.dram_tensor("e2in%d" % l, [512, 1536], BF16) for l in range(L)]
    e2_out = [nc.dram_tensor("e2out%d" % l, [1024, 1536], BF16) for l in range(L)]
    e3_in = [nc.dram_tensor("e3in%d" % l, [128, 16], BF16) for l in range(L)]
    e3_out = [nc.dram_tensor("e3out%d" % l, [256, 16], BF16) for l in range(L)]
    cc_sems = [k.new_sem("cc%d" % i) for i in range(3 * L)]

    import os
    ARENA_B = int(os.environ.get('ARENA_KB', '196')) * 1024
    arena = stack.enter_context(nc.sbuf_tensor("arena", [128, ARENA_B // 2], BF16))
    psum = stack.enter_context(nc.psum_tensor("psum", [128, 8, 512], F32))

    def view(off_bytes, shape, dt):
        esz = 4 if dt == F32 else 2
        n = int(np.prod(shape))
        assert off_bytes % 4 == 0 and off_bytes + n * esz <= ARENA_B, (off_bytes, shape)
        ap = arena[:, off_bytes // 2: off_bytes // 2 + n * esz // 2]
        if dt == F32:
            ap = ap.bitcast(F32)
        if len(shape) == 2:
            return ap.rearrange("p (a b) -> p a b", a=shape[0])
        if len(shape) == 3:
            return ap.rearrange("p (a b c) -> p a b c", a=shape[0], b=shape[1])
        return ap

    KB = 1024
    off = 0

    def take(nbytes):
        nonlocal off
        o = off
        off += (nbytes + 31) // 32 * 32
        return o

    xT = view(take(64 * KB), [8, T], F32)
    hT = view(take(8 * 2050 * 2), [8, 2050], BF16)
    HT_OFF = off - (8 * 2050 * 2 + 31) // 32 * 32
    NW = 8
    wring = [view(take(2 * KB), [8, 128], BF16) for _ in range(NW)]
    wbuf = [Buf() for _ in range(NW)]
    ident = view(take(512), [128], F32)
    onesblk = view(take(256), [128], BF16)
    ones128 = view(take(256), [128], BF16)
    rmt = view(take(256), [128], BF16)
    pv = view(take(PV_N * 4), [PV_N], F32)
    A0 = off
    assert stop == 'load' or ARENA_B - A0 >= 80 * KB, (ARENA_B - A0)

    def av(o, shape, dt):
        return view(A0 + o, shape, dt)

    def hv(o, shape, dt):
        return view(HT_OFF + o, shape, dt)

    bank = [Buf() for _ in range(8)]

    def PS(b, n=512):
        return psum[:, b, 0:n]

    def PS2(b):
        return psum[:, b:b + 2, :]

    B_xT = [[Buf() for _ in range(4)] for _ in range(8)]
    B_hT = [Buf() for _ in range(4)]
    B_hhalo = Buf()
    B_const = Buf()

    wstate = {"i": 0}

    def wload(src_ap):
        i = wstate["i"] % NW
        wstate["i"] += 1
        pool.dma(wring[i], src_ap, writes=[wbuf[i]])
        return wring[i], wbuf[i]

    def wload_cols(w_l, c0, ncols=128):
        src = w_l[:, c0:c0 + ncols].rearrange("(kc p) n -> p kc n", p=128)
        i = wstate["i"] % NW
        wstate["i"] += 1
        dst = wring[i][:, :, 0:ncols]
        pool.dma(dst, src, writes=[wbuf[i]])
        return wring[i], wbuf[i]

    sp.dma(ident, ident_d, writes=[B_const])
    sp.dma(pv, pv_d, writes=[B_const])
    pool.dma(rmt, rt_d, writes=[B_const])
    dve.op(lambda e: e.memset(ones128, 1.0), writes=[B_const])
    dve.op(lambda e: e.memset(onesblk, 0.0), writes=[B_const])
    dve.op(lambda e: e.memset(onesblk[0:64, 0:64], 1.0), writes=[B_const])
    dve.op(lambda e: e.memset(onesblk[64:128, 64:128], 1.0), writes=[B_const])
    dve.op(lambda e: e.memset(hT[:, :, 0:1], 0.0), writes=[B_hhalo])
    dve.op(lambda e: e.memset(hT[:, :, 2049:2050], 0.0), writes=[B_hhalo])
    k.barrier()

    def pvc(c, n=1):
        return pv[:, c:c + n]

    def load_transpose(src_d, ntiles, dstT, dstbufs, stage_off):
        stg = [av(stage_off + i * 4 * KB, [1024], F32) for i in range(2)]
        sb = [Buf(), Buf()]
        for t in range(ntiles):
            s = stg[t % 2]
            sp.dma(s, src_d[t * 128:(t + 1) * 128, :], writes=[sb[t % 2]])
            for half in range(2):
                b = (2 * t + half) % 8
                for j in range(4):
                    c = half * 4 + j
                    pe.op(lambda e, o=psum[:, b, j * 128:(j + 1) * 128], i=s[:, c * 128:(c + 1) * 128]:
                          e.transpose(o, i, ident),
                          reads=[sb[t % 2], B_const], writes=[bank[b]], signal=(j == 3))
                wb = dstbufs(t, half)
                dst = dstT[:, half * 4:half * 4 + 4, t * 128:(t + 1) * 128]
                src = psum[:, b, :].rearrange("p (a q) -> p a q", a=4)
                eng = dve if half == 0 else act
                if eng is dve:
                    dve.op(lambda e, o=dst, i=src: e.tensor_copy(o, i), reads=[bank[b]], writes=wb)
                else:
                    act.op(lambda e, o=dst, i=src: e.copy(o, i), reads=[bank[b]], writes=wb)

    load_transpose(x_d, 16, xT, lambda t, half: [B_xT[half * 4 + j][t // 4] for j in range(4)], 0)
    k.barrier()

    def rmsnorm_T(src, ncols, gcol, dst_fn, tmp_off, src_bufs, dst_bufs, nblk=None, after_blk=None):
        sq = [av(tmp_off + i * KB, [512], BF16) for i in range(4)]
        sqb = [Buf() for _ in range(4)]
        rs = [av(tmp_off + 4 * KB + i * 2 * KB, [512], F32) for i in range(2)]
        rsb = [Buf(), Buf()]
        nb = ncols // 512 if nblk is None else nblk
        w = min(512, ncols)
        for blk in range(nb):
            pb = blk % 2
            for c in range(8):
                i = (blk * 8 + c) % 4
                act.op(lambda e, o=sq[i][:, 0:w], s=src[:, c, blk * w:(blk + 1) * w]: e.activation(o, s, AF.Square),
                       reads=src_bufs(c, blk), writes=[sqb[i]])
                pe.op(lambda e, o=PS(pb, w), r=sq[i][:, 0:w], st=(c == 0), sp_=(c == 7):
                      e.matmul(o, ones128, r, start=st, stop=sp_),
                      reads=[sqb[i], B_const], writes=[bank[pb]], signal=(c == 7))
            r = rs[blk % 2]
            act.op(lambda e, o=r[:, 0:w], s=PS(pb, w): e.activation(o, s, AF.Ln, bias=EPS, scale=1.0 / D),
                   reads=[bank[pb]], writes=[rsb[blk % 2]])
            act.op(lambda e, o=r[:, 0:w]: e.activation(o, o, AF.Exp, scale=-0.5),
                   reads=[rsb[blk % 2]], writes=[rsb[blk % 2]])
            for c in range(8):
                dve.op(lambda e, o=dst_fn(c, blk), s=src[:, c, blk * w:(blk + 1) * w], g=pvc(gcol + c), rr=r[:, 0:w]:
                       e.scalar_tensor_tensor(o, s, g, rr, ALU.mult, ALU.mult),
                       reads=src_bufs(c, blk) + [rsb[blk % 2], B_const], writes=dst_bufs(c, blk))
            if after_blk is not None:
                after_blk(blk)

    def norm_x_to_hT(gcol, tmp_off):
        rmsnorm_T(xT, T, gcol, lambda c, blk: hT[:, c, 1 + blk * 512:1 + (blk + 1) * 512], tmp_off,
                  lambda c, blk: [B_xT[c][blk]], lambda c, blk: [B_hT[blk]])

    def hblk(c, blk):
        return hT[:, c, 1 + blk * 512:1 + (blk + 1) * 512]

    cci = {"i": 0}

    def allgather(src_t, dst_t, reads):
        s = cc_sems[cci["i"]]
        cci["i"] += 1
        pool._deps(reads, [])
        s.cnt += 1
        pool.items.append(("o", lambda e: e.collective_compute(
            "AllGather", ALU.bypass, replica_groups=PAIRS,
            ins=[src_t.ap().opt()], outs=[dst_t.ap().opt()]), s.h, 1))
        b = Buf()
        b.w = [(s, 1)]
        return b

    def proj_residual(w_l, nk, rhs_fn, rhs_bufs):
        for oc in range(8):
            i = wstate["i"] % NW
            wstate["i"] += 1
            pool.dma(wring[i][:, 0:nk, :], w_l[:, oc * 128:(oc + 1) * 128].rearrange("(kc p) n -> p kc n", p=128), writes=[wbuf[i]])
            wt, wb = wring[i], wbuf[i]
            for blk in range(4):
                pb = (oc * 4 + blk) % 4
                for c in range(nk):
                    pe.op(lambda e, o=PS(pb), a=wt[:, c, :], r=rhs_fn(c, blk), st=(c == 0), sp_=(c == nk - 1):
                          e.matmul(o, a, r, start=st, stop=sp_),
                          reads=[wb] + rhs_bufs(c, blk), writes=[bank[pb]], signal=(c == nk - 1))
                xs = xT[:, oc, blk * 512:(blk + 1) * 512]
                dve.op(lambda e, o=xs, p_=PS(pb): e.tensor_tensor(o, o, p_, ALU.add), reads=[bank[pb], B_xT[oc][blk]], writes=[B_xT[oc][blk]])


    def layer_body(l):
        nbase = PV_NORM + l * 32
        w_in_l = w_in_d[l]
        norm_x_to_hT(nbase + 0, 64 * KB)
        k.barrier()
        if stop == (l, 'norm'):
            return True

        qT = av(0, [4, T], BF16)
        ropeC = av(16 * KB, [T], F32)
        ropeS = av(24 * KB, [T], F32)
        kfull = av(16 * KB, [2, 4096], BF16)
        vfull = av(32 * KB, [32, 320], BF16)
        TMP = 52 * KB
        B_rope = Buf()
        sp.dma(ropeC, cos_d, writes=[B_rope])
        sp.dma(ropeS, sin_d, writes=[B_rope])
        B_qT = [[Buf() for _ in range(4)] for _ in range(4)]
        nr_tb = [{n_: Buf() for n_ in ('sq', 'gv', 'rs', 'aa', 'bb')} for _ in range(2)]

        def normrope(ps_b, gcol, dst_ap, dst_bufs, blk, tmpo, ti):
            sl = ti % 2
            o = tmpo + sl * 8 * KB
            sqv = av(o, [512], BF16)
            gv = av(o + KB, [512], BF16)
            rsv = av(o + 2 * KB, [512], F32)
            aa = av(o + 4 * KB, [512], F32)
            bb = av(o + 6 * KB, [512], F32)
            B = nr_tb[sl]
            b2 = 4 + 2 * sl
            b3 = b2 + 1
            act.op(lambda e: e.activation(sqv, PS(ps_b), AF.Square), reads=[bank[ps_b]], writes=[B["sq"]])
            act.op(lambda e: e.activation(gv, PS(ps_b), AF.Identity, scale=pvc(gcol)), reads=[bank[ps_b], B_const], writes=[B["gv"]])
            pe.op(lambda e: e.matmul(PS(b2), onesblk, sqv, start=True, stop=True), reads=[B["sq"], B_const], writes=[bank[b2]])
            pe.op(lambda e: e.matmul(PS(b3), rmt, gv, start=True, stop=True), reads=[B["gv"], B_const], writes=[bank[b3]])
            act.op(lambda e: e.activation(rsv, PS(b2), AF.Ln, bias=EPS, scale=1.0 / 64), reads=[bank[b2]], writes=[B["rs"]])
            act.op(lambda e: e.activation(rsv, rsv, AF.Exp, scale=-0.5), reads=[B["rs"]], writes=[B["rs"]])
            cs = ropeC[:, blk * 512:(blk + 1) * 512]
            sn = ropeS[:, blk * 512:(blk + 1) * 512]
            dve.op(lambda e: e.tensor_tensor(aa, gv, cs, ALU.mult), reads=[B["gv"], B_rope], writes=[B["aa"]])
            dve.op(lambda e: e.tensor_tensor(bb, PS(b3), sn, ALU.mult), reads=[bank[b3], B_rope], writes=[B["bb"]])
            dve.op(lambda e: e.tensor_tensor(aa, aa, bb, ALU.add), reads=[B["aa"], B["bb"]], writes=[B["aa"]])
            dve.op(lambda e: e.tensor_tensor(dst_ap, aa, rsv, ALU.mult), reads=[B["aa"], B["rs"]], writes=dst_bufs)

        kst = [av(TMP + 16 * KB + i * KB, [512], BF16) for i in range(2)]
        kstb = [Buf(), Buf()]
        e1_deps = []
        ti = 0
        for g in range(2):
            i = wstate["i"] % NW
            wstate["i"] += 1
            for hf in range(2):
                pool.dma(wring[i][:, :, hf * 64:(hf + 1) * 64],
                         w_in_l[:, 2048 + g * 64:2048 + (g + 1) * 64].rearrange("(kc p) n -> p kc n", p=128),
                         writes=[wbuf[i]])
            wt, wb = wring[i], wbuf[i]
            for blk in range(4):
                pb = blk % 2
                for c in range(8):
                    pe.op(lambda e, o=PS(pb), a=wt[:, c, :], r=hblk(c, blk), st=(c == 0), sp_=(c == 7):
                          e.matmul(o, a, r, start=st, stop=sp_),
                          reads=[wb, B_hT[blk]], writes=[bank[pb]], signal=(c == 7))
                si = ti % 2
                normrope(pb, PV_GK + l, kst[si], [kstb[si]], blk, TMP, ti)
                ti += 1
                d = sp.dma(e1_in[l][g * 128:(g + 1) * 128, blk * 512:(blk + 1) * 512], kst[si], reads=[kstb[si]])
                e1_deps.append(d)
        wt, wb = wload_cols(w_in_l, 2176)
        vst = av(TMP + 18 * KB, [16, 128], BF16)
        vstb = Buf()
        for t4 in range(4):
            pb = 2 + t4 % 2
            for tt in range(4):
                t = t4 * 4 + tt
                for c in range(8):
                    pe.op(lambda e, o=psum[:, pb, tt * 128:(tt + 1) * 128], a=hT[:, c, 1 + t * 128:1 + (t + 1) * 128], r=wt[:, c, :],
                          st=(c == 0), sp_=(c == 7): e.matmul(o, a, r, start=st, stop=sp_),
                          reads=[wb, B_hT[t // 4]], writes=[bank[pb]], signal=(c == 7 and tt == 3))
            act.op(lambda e, o=vst[:, t4 * 4:(t4 + 1) * 4, :], i=psum[:, pb, :].rearrange("p (a q) -> p a q", a=4): e.copy(o, i),
                   reads=[bank[pb]], writes=[vstb])
        vdst = e1_in[l][256:384, :].rearrange("p (t n) -> p t n", t=16)
        d = sp.dma(vdst, vst, reads=[vstb])
        e1_deps.append(d)
        eb = Buf()
        eb.w = e1_deps
        e1_done = allgather(e1_in[l], e1_out[l], [eb])

        for c4 in range(4):
            wt, wb = wload_cols(w_in_l, 1536 + c4 * 128)
            for blk in range(4):
                pb = blk % 2
                for c in range(8):
                    pe.op(lambda e, o=PS(pb), a=wt[:, c, :], r=hblk(c, blk), st=(c == 0), sp_=(c == 7):
                          e.matmul(o, a, r, start=st, stop=sp_),
                          reads=[wb, B_hT[blk]], writes=[bank[pb]], signal=(c == 7))
                normrope(pb, PV_GQ + l, qT[:, c4, blk * 512:(blk + 1) * 512], [B_qT[c4][blk]], blk, TMP, ti)
                ti += 1
        k.barrier()
        if stop == (l, 'gqaproj'):
            return True

        B_kf = Buf()
        B_vf = Buf()
        vf5 = vfull.rearrange("p t (s d) -> p t s d", d=64)
        dve.op(lambda e: e.memset(vf5[:, :, 0:5:2, :], 1.0), writes=[B_vf])
        for r in range(2):
            for g in range(2):
                sp.dma(kfull[:, g, r * 2048:(r + 1) * 2048], e1_out[l][384 * r + 128 * g:384 * r + 128 * (g + 1), :],
                       reads=[e1_done], writes=[B_kf])
            vsrc = e1_out[l][384 * r + 256:384 * r + 384, :].rearrange("p (t n) -> p t n", t=16)
            sp.dma(vfull[:, r * 16:(r + 1) * 16, 64:128], vsrc[:, :, 0:64], reads=[e1_done], writes=[B_vf])
            sp.dma(vfull[:, r * 16:(r + 1) * 16, 192:256], vsrc[:, :, 64:128], reads=[e1_done], writes=[B_vf])

        PT = [av(TMP + i * 2 * KB, [1024], BF16) for i in range(2)]
        ptb = [Buf(), Buf()]
        rden = av(TMP + 4 * KB, [1024], F32)
        rdb = Buf()
        it = 0
        rden2 = av(TMP + 4 * KB, [512], F32)
        for hp in range(4):
            g = hp // 2
            c4 = hp
            for qb in range(4):
                oe = 4 + 2 * ((hp * 4 + qb) % 2)
                qbufs = [B_qT[c4][qb]]
                qs = slice(qb * 512, (qb + 1) * 512)

                def s_mm(kt, sb):
                    for odd in range(2):
                        r0 = 64 * odd
                        pe.op(lambda e, o=PS(sb + odd), a=kfull[r0:r0 + 64, g, kt * 128:(kt + 1) * 128], r=qT[r0:r0 + 64, c4, qs]:
                              e.matmul(o, a, r, start=True, stop=True),
                              reads=[B_kf] + qbufs, writes=[bank[sb], bank[sb + 1]], signal=(odd == 1))

                s_mm(0, 0)
                for kt in range(32):
                    sb = 2 * (kt % 2)
                    if kt + 1 < 32:
                        s_mm(kt + 1, 2 * ((kt + 1) % 2))
                    p = PT[it % 2]
                    pbf = ptb[it % 2]
                    it += 1
                    act.op(lambda e, o=p, s_=psum[:, sb:sb + 2, :].rearrange("p a q -> p (a q)"): e.activation(o, s_, AF.Exp, scale=0.125),
                           reads=[bank[sb], bank[sb + 1]], writes=[pbf])
                    for odd in range(2):
                        vc0 = (64 if not odd else 0) + 128 * g
                        pe.op(lambda e, o=PS(oe + odd), a=vfull[:, kt, vc0:vc0 + 128], r=p[:, odd * 512:(odd + 1) * 512], st=(kt == 0), sp_=(kt == 31):
                              e.matmul(o, a, r, start=st, stop=sp_),
                              reads=[pbf, B_vf], writes=[bank[oe], bank[oe + 1]], signal=(odd == 1))
                obs = [bank[oe], bank[oe + 1]]
                dve.op(lambda e, o=rden2[64:128, :], i=psum[64:128, oe, :]: e.reciprocal(o, i), reads=obs, writes=[rdb], signal=False)
                dve.op(lambda e, o=rden2[0:64, :], i=psum[0:64, oe + 1, :]: e.reciprocal(o, i), reads=obs, writes=[rdb])
                dve.op(lambda e, o=qT[0:64, c4, qs], a=psum[0:64, oe, :], b=rden2[64:128, :]:
                       e.tensor_tensor(o, a, b, ALU.mult), reads=obs + [rdb], writes=qbufs, signal=False)
                dve.op(lambda e, o=qT[64:128, c4, qs], a=psum[64:128, oe + 1, :], b=rden2[0:64, :]:
                       e.tensor_tensor(o, a, b, ALU.mult), reads=obs + [rdb], writes=qbufs)
        pass
        proj_residual(w_out_d[l][512:1024, :], 4,
                      lambda c, blk: qT[:, c, blk * 512:(blk + 1) * 512],
                      lambda c, blk: [B_qT[c][blk]])
        k.barrier()
        if stop == (l, 'gqa'):
            return True

        naq = av(0, [4, T], BF16)
        nakT = av(16 * KB, [4, 2560], BF16)
        naV = av(36 * KB, [20, 768], BF16)
        nav5 = naV.rearrange("p t (c s d) -> p t c s d", c=4, s=3)
        NTMP = 66 * KB
        B_naq = [[Buf() for _ in range(8)] for _ in range(4)]
        B_nak = Buf()
        B_nav = Buf()
        def na_qk_proj(which, base_c, dstT, coff):
            for c4 in range(4):
                wt, wb = wload_cols(w_in_l, base_c + c4 * 128)
                for blk in range(4):
                    pb = blk % 2
                    for c in range(8):
                        pe.op(lambda e, o=PS(pb), a=wt[:, c, :], r=hblk(c, blk), st=(c == 0), sp_=(c == 7):
                              e.matmul(o, a, r, start=st, stop=sp_),
                              reads=[wb, B_hT[blk]], writes=[bank[pb]], signal=(c == 7))
                    dst = dstT[:, c4, coff + blk * 512:coff + (blk + 1) * 512]
                    wbs = [B_naq[c4][2 * blk], B_naq[c4][2 * blk + 1]] if which == 0 else [B_nak]
                    if (c4 + blk) % 2 == 0:
                        act.op(lambda e, o=dst, i=PS(pb): e.copy(o, i), reads=[bank[pb]], writes=wbs)
                    else:
                        dve.op(lambda e, o=dst, i=PS(pb): e.tensor_copy(o, i), reads=[bank[pb]], writes=wbs)
        na_qk_proj(1, 512, nakT, 256)
        wv = []
        for c4 in range(4):
            wv.append(wload_cols(w_in_l, 1024 + c4 * 128))
        for t in range(16):
            pb = 2 + t % 2
            for c4 in range(4):
                wt, wb = wv[c4]
                for c in range(8):
                    pe.op(lambda e, o=psum[:, pb, c4 * 128:(c4 + 1) * 128], a=hT[:, c, 1 + t * 128:1 + (t + 1) * 128], r=wt[:, c, :],
                          st=(c == 0), sp_=(c == 7): e.matmul(o, a, r, start=st, stop=sp_),
                          reads=[wb, B_hT[t // 4]], writes=[bank[pb]], signal=(c == 7 and c4 == 3))
            vdst_ = nav5[:, 2 + t, :, 0:3:2, :]
            vsrc_ = psum[:, pb, :].rearrange("p (c e d) -> p c e d", c=4, e=2)
            if t % 2 == 0:
                act.op(lambda e, o=vdst_, i=vsrc_: e.copy(o, i), reads=[bank[pb]], writes=[B_nav])
            else:
                dve.op(lambda e, o=vdst_, i=vsrc_: e.tensor_copy(o, i), reads=[bank[pb]], writes=[B_nav])
        dve.op(lambda e: e.memset(nav5[:, 2:18, :, 1, :], 1.0), writes=[B_nav])
        e2d = []
        kview = lambda t_, r0_: t_[r0_:r0_ + 128, 0:1024].rearrange("p (c n) -> p c n", c=4)
        vview = lambda t_, r0_: t_[r0_:r0_ + 128, :].rearrange("p (s n) -> p s n", s=2)
        e2d.append(sp.dma(kview(e2_in[l], 0), nakT[:, :, 256:512], reads=[B_nak]))
        e2d.append(sp.dma(kview(e2_in[l], 128), nakT[:, :, 2048:2304], reads=[B_nak]))
        e2d.append(sp.dma(vview(e2_in[l], 256), naV[:, 2:4, :], reads=[B_nav]))
        e2d.append(sp.dma(vview(e2_in[l], 384), naV[:, 16:18, :], reads=[B_nav]))
        eb2 = Buf()
        eb2.w = e2d
        e2_done = allgather(e2_in[l], e2_out[l], [eb2])
        sp.dma(nakT[:, :, 0:256], kview(e2_out[l], 128), reads=[e2_done], writes=[B_nak])
        sp.dma(nakT[:, :, 2304:2560], kview(e2_out[l], 512), reads=[e2_done], writes=[B_nak])
        sp.dma(naV[:, 0:2, :], vview(e2_out[l], 384), reads=[e2_done], writes=[B_nav])
        sp.dma(naV[:, 18:20, :], vview(e2_out[l], 512 + 256), reads=[e2_done], writes=[B_nav])
        na_qk_proj(0, 0, naq, 0)
        dve.op(lambda e: e.tensor_scalar(naV[:, 0:2, :], naV[:, 0:2, :], pvc(PV_FLAGL), None, ALU.mult), reads=[B_nav, B_const], writes=[B_nav])
        dve.op(lambda e: e.tensor_scalar(naV[:, 18:20, :], naV[:, 18:20, :], pvc(PV_FLAGR), None, ALU.mult), reads=[B_nav, B_const], writes=[B_nav])
        k.barrier()

        nmask = [hv(i * 1792, [896], BF16) for i in range(3)]
        B_nmask = Buf()
        for i in range(3):
            pool.dma(nmask[i], namask_d[i], writes=[B_nmask] + B_hT)
        gst = [hv(6 * KB + i * 3584, [896], F32) for i in range(2)]
        gstb = [Buf(), Buf()]
        ttf = [hv(14 * KB + i * 1792, [896], BF16) for i in range(2)]
        ttfb = [Buf(), Buf()]
        ttab = [[hv(18 * KB + (i * 3 + m) * 1792, [896], BF16) for m in range(3)] for i in range(2)]
        ttabb = [Buf(), Buf()]
        NPT = [av(NTMP + i * KB, [2, 256], BF16) for i in range(3)]
        nptb = [[Buf(), Buf()] for _ in range(3)]
        NE = [av(NTMP + 3 * KB + i * KB, [2, 256], BF16) for i in range(3)]
        neb = [Buf() for _ in range(3)]
        nrd = av(NTMP + 6 * KB, [256], F32)
        nrdb = Buf()
        def na_tables(h):
            hi = h % 2
            sp.dma(gst[hi], nag_d[l, h], writes=[gstb[hi]])
            act.op(lambda e, o=ttf[hi], i=gst[hi]: e.activation(o, i, AF.Exp), reads=[gstb[hi]], writes=[ttfb[hi]])
            for m in range(3):
                dve.op(lambda e, o=ttab[hi][m], a=ttf[hi], b=nmask[m]: e.tensor_tensor(o, a, b, ALU.mult),
                       reads=[ttfb[hi], B_nmask], writes=[ttabb[hi]])

        steps = [(hp, b, w) for hp in range(4) for b in range(8) for w in range(6)]
        NS_ = len(steps)
        LAG = 1
        NE2 = [av(NTMP + i * KB, [2, 256], BF16) for i in range(3)]
        NP2 = [av(NTMP + 3 * KB + i * KB, [2, 256], BF16) for i in range(3)]
        ne2b = [Buf() for _ in range(3)]
        np2b = [[Buf(), Buf()] for _ in range(3)]
        nrd2 = av(NTMP + 6 * KB, [256], F32)

        def na_front(n):
            hp, b, w = steps[n]
            c4 = hp
            if b == 0 and w == 0:
                na_tables(2 * hp)
                na_tables(2 * hp + 1)
            ti_ = 0 if b == 0 else (2 if b == 7 else 1)
            qb_ = [B_naq[c4][b]]
            sA = 2 * (n % 2)
            s_ = 2 * b + w
            for odd in range(2):
                r0 = 64 * odd
                pe.op(lambda e, o=psum[:, sA + odd, 0:256], a=nakT[r0:r0 + 64, c4, s_ * 128:(s_ + 1) * 128],
                      r=naq[r0:r0 + 64, c4, b * 256:(b + 1) * 256]: e.matmul(o, a, r, start=True, stop=True),
                      reads=[B_nak] + qb_, writes=[bank[sA], bank[sA + 1]], signal=(odd == 1))
            ii = n % 3
            act.op(lambda e, o=NE2[ii], s2=psum[:, sA:sA + 2, 0:256]: e.activation(o, s2, AF.Exp, scale=0.125),
                   reads=[bank[sA], bank[sA + 1]], writes=[ne2b[ii]])
            c0 = (10 - 2 * w) * 64
            dve.op(lambda e, o=NP2[ii][:, 0, :], a=NE2[ii][:, 0, :], b_=ttab[0][ti_][:, c0:c0 + 256]: e.tensor_tensor(o, a, b_, ALU.mult),
                   reads=[ne2b[ii], ttabb[0]], writes=[np2b[ii][0]])
            pool.op(lambda e, o=NP2[ii][:, 1, :], a=NE2[ii][:, 1, :], b_=ttab[1][ti_][:, c0:c0 + 256]: e.tensor_tensor(o, a, b_, ALU.mult),
                    reads=[ne2b[ii], ttabb[1]], writes=[np2b[ii][1]])

        def na_back(n):
            hp, b, w = steps[n]
            c4 = hp
            oe = 4 + 2 * ((hp * 8 + b) % 2)
            ii = n % 3
            qb_ = [B_naq[c4][b]]
            s_ = 2 * b + w
            for odd in range(2):
                nvc0 = c4 * 192 + (64 if odd else 0)
                pe.op(lambda e, o=psum[:, oe + odd, 0:256], a=naV[:, s_, nvc0:nvc0 + 128], r=NP2[ii][:, odd, :], st=(w == 0), sp_=(w == 5):
                      e.matmul(o, a, r, start=st, stop=sp_),
                      reads=[np2b[ii][odd], B_nav], writes=[bank[oe], bank[oe + 1]], signal=True)
            if w == 5:
                obs = [bank[oe], bank[oe + 1]]
                dve.op(lambda e, o=nrd2[64:128, :], i=psum[64:128, oe, 0:256]: e.reciprocal(o, i), reads=obs, writes=[nrdb], signal=False)
                dve.op(lambda e, o=nrd2[0:64, :], i=psum[0:64, oe + 1, 0:256]: e.reciprocal(o, i), reads=obs, writes=[nrdb])
                dve.op(lambda e, o=naq[0:64, c4, b * 256:(b + 1) * 256], a=psum[0:64, oe, 0:256], b_=nrd2[64:128, :]:
                       e.tensor_tensor(o, a, b_, ALU.mult), reads=obs + [nrdb], writes=qb_, signal=False)
                dve.op(lambda e, o=naq[64:128, c4, b * 256:(b + 1) * 256], a=psum[64:128, oe + 1, 0:256], b_=nrd2[0:64, :]:
                       e.tensor_tensor(o, a, b_, ALU.mult), reads=obs + [nrdb], writes=qb_)

        for n in range(NS_ + LAG):
            if n < NS_:
                na_front(n)
            if n - LAG >= 0:
                na_back(n - LAG)
        pass
        if stop == (l, 'na'):
            return True

        proj_residual(w_out_d[l][0:512, :], 4,
                      lambda c, blk: naq[:, c, blk * 512:(blk + 1) * 512],
                      lambda c, blk: [B_naq[c][2 * blk], B_naq[c][2 * blk + 1]])
        k.barrier()
        if stop == (l, 'mixer'):
            return True

        qm = av(0, [4, T], BF16)
        memT = av(16 * KB, [8, 256], F32)
        memn = av(24 * KB, [8, 256], BF16)
        kmT = av(28 * KB, [4, 256], BF16)
        Vm = av(30 * KB, [2, 512], BF16)
        MT = 32 * KB
        B_memT = Buf()
        B_memn = Buf()
        B_km = Buf()
        B_vm = Buf()
        load_transpose(mem_d, 2, memT, lambda t, half: [B_memT], 40 * KB)
        pass
        rmsnorm_T(memT, 256, nbase + 16, lambda c, blk: memn[:, c, :], MT, lambda c, blk: [B_memT], lambda c, blk: [B_memn], nblk=1)
        norm_x_to_hT(nbase + 8, 56 * KB)
        k.barrier()
        w_kv_l = w_mkv_d[l]
        for hd in range(4):
            wt, wb = wload_cols(w_kv_l, hd * 128)
            for c in range(8):
                pe.op(lambda e, o=PS(hd % 2, 256), a=wt[:, c, :], r=memn[:, c, :], st=(c == 0), sp_=(c == 7):
                      e.matmul(o, a, r, start=st, stop=sp_), reads=[wb, B_memn], writes=[bank[hd % 2]], signal=(c == 7))
            act.op(lambda e, o=kmT[:, hd, :], i=PS(hd % 2, 256): e.copy(o, i), reads=[bank[hd % 2]], writes=[B_km])
        for c4 in range(4):
            wt, wb = wload_cols(w_kv_l, 512 + c4 * 128)
            for mt in range(2):
                pb = 2 + (c4 * 2 + mt) % 2
                for c in range(8):
                    pe.op(lambda e, o=PS(pb, 128), a=memn[:, c, mt * 128:(mt + 1) * 128], r=wt[:, c, :], st=(c == 0), sp_=(c == 7):
                          e.matmul(o, a, r, start=st, stop=sp_), reads=[wb, B_memn], writes=[bank[pb]], signal=(c == 7))
                dve.op(lambda e, o=Vm[:, mt, c4 * 128:(c4 + 1) * 128], i=PS(pb, 128): e.tensor_copy(o, i), reads=[bank[pb]], writes=[B_vm])
        B_qm = [[Buf() for _ in range(4)] for _ in range(4)]
        w_q_l = w_mq_d[l]
        for hd in range(4):
            wt, wb = wload_cols(w_q_l, hd * 128)
            for blk in range(4):
                pb = 4 + blk % 2
                for c in range(8):
                    pe.op(lambda e, o=PS(pb), a=wt[:, c, :], r=hblk(c, blk), st=(c == 0), sp_=(c == 7):
                          e.matmul(o, a, r, start=st, stop=sp_), reads=[wb, B_hT[blk]], writes=[bank[pb]], signal=(c == 7))
                dst = qm[:, hd, blk * 512:(blk + 1) * 512]
                if blk % 2 == 0:
                    act.op(lambda e, o=dst, i=PS(pb): e.copy(o, i), reads=[bank[pb]], writes=[B_qm[hd][blk]])
                else:
                    dve.op(lambda e, o=dst, i=PS(pb): e.tensor_copy(o, i), reads=[bank[pb]], writes=[B_qm[hd][blk]])
        MPT = [av(MT + i * 2 * KB, [2, 512], BF16) for i in range(2)]
        mptb = [Buf(), Buf()]
        mrd = av(MT + 4 * KB, [512], F32)
        mrdb = Buf()
        msc = float(128 ** -0.5)
        it = 0
        for blk in range(4):
            for hd in range(4):
                sbk = 2 * (it % 2)
                for mt in range(2):
                    pe.op(lambda e, o=PS(sbk + mt), a=kmT[:, hd, mt * 128:(mt + 1) * 128], r=qm[:, hd, blk * 512:(blk + 1) * 512]:
                          e.matmul(o, a, r, start=True, stop=True),
                          reads=[B_km, B_qm[hd][blk]], writes=[bank[sbk], bank[sbk + 1]], signal=(mt == 1))
                p = MPT[it % 2]
                pb_ = mptb[it % 2]
                act.op(lambda e, o=p, s_=psum[:, sbk:sbk + 2, :]: e.activation(o, s_, AF.Exp, scale=msc),
                       reads=[bank[sbk], bank[sbk + 1]], writes=[pb_])
                ob = 4 + 2 * (it % 2)
                it += 1
                for mt in range(2):
                    pe.op(lambda e, o=PS(ob), a=Vm[:, mt, hd * 128:(hd + 1) * 128], r=p[:, mt, :], st=(mt == 0), sp_=(mt == 1):
                          e.matmul(o, a, r, start=st, stop=sp_), reads=[pb_, B_vm], writes=[bank[ob]], signal=(mt == 1))
                for mt in range(2):
                    pe.op(lambda e, o=PS(ob + 1), r=p[:, mt, :], st=(mt == 0), sp_=(mt == 1):
                          e.matmul(o, ones128, r, start=st, stop=sp_), reads=[pb_, B_const], writes=[bank[ob + 1]], signal=(mt == 1))
                dve.op(lambda e, i=PS(ob + 1): e.reciprocal(mrd, i), reads=[bank[ob + 1]], writes=[mrdb])
                dve.op(lambda e, o=qm[:, hd, blk * 512:(blk + 1) * 512], a=PS(ob): e.tensor_tensor(o, a, mrd, ALU.mult),
                       reads=[bank[ob], mrdb], writes=[B_qm[hd][blk]])
        pass
        proj_residual(w_mo_d[l], 4, lambda c, blk: qm[:, c, blk * 512:(blk + 1) * 512], lambda c, blk: [B_qm[c][blk]])
        k.barrier()
        if stop == (l, 'mem'):
            return True

        norm_x_to_hT(nbase + 24, 64 * KB)
        hst = av(0, [2, 8], BF16)
        hstb = Buf()
        dve.op(lambda e: e.tensor_copy(hst[:, 0, :], hT[:, :, 1]), reads=[B_hT[0]], writes=[hstb])
        dve.op(lambda e: e.tensor_copy(hst[:, 1, :], hT[:, :, 2048]), reads=[B_hT[3]], writes=[hstb])
        d = sp.dma(e3_in[l][:, :].rearrange("p (a b) -> p a b", a=2), hst, reads=[hstb])
        eb3 = Buf()
        eb3.w = [d]
        e3_done = allgather(e3_in[l], e3_out[l], [eb3])
        hrx = av(64, [2, 8], BF16)
        hrxb = Buf()
        sp.dma(hrx[:, 0, :], e3_out[l][0:128, 8:16], reads=[e3_done], writes=[hrxb])
        sp.dma(hrx[:, 1, :], e3_out[l][128:256, 0:8], reads=[e3_done], writes=[hrxb])
        dve.op(lambda e: e.tensor_scalar(hT[:, :, 0], hrx[:, 0, :], pvc(PV_FLAGL), None, ALU.mult), reads=[hrxb, B_const], writes=[B_hhalo])
        dve.op(lambda e: e.tensor_scalar(hT[:, :, 2049], hrx[:, 1, :], pvc(PV_FLAGR), None, ALU.mult), reads=[hrxb, B_const], writes=[B_hhalo])
        k.barrier()
        if stop == (l, 'ffnhalo'):
            return True

        actT = av(1 * KB, [22, 1024], BF16)
        TG = [av(45 * KB + i * 4 * KB, [1024], F32) for i in range(2)]
        TV = [av(53 * KB + i * 4 * KB, [1024], F32) for i in range(2)]
        UC = [av(61 * KB + i * 4128, [1032], F32) for i in range(2)]
        tgb = [Buf(), Buf()]
        tvb = [Buf(), Buf()]
        ucb = [Buf(), Buf()]
        WD = [av(61 * KB + 8256 + i * 5632, [22, 128], BF16) for i in range(2)]
        wdb = [Buf(), Buf()]
        B_act = [Buf() for _ in range(22)]
        cwb = PV_CW + l * 176
        w_up_l = w_up_d[l]
        w_dn_l = w_dn_d[l]
        UHB = 7
        uhb = [bank[6], bank[7]]
        for hh in range(int(os.environ.get('FFN_HH', '2'))):
            base = 1 + hh * 1024
            uslot = 0
            order_ = [cp_ + 22 * g_ for cp_ in range(int(os.environ.get('FFN_CP', '22'))) for g_ in range(2)]
            loaded_ = {}
            PF_ = 5
            for s0 in range(min(PF_, len(order_))):
                loaded_[s0] = wload_cols(w_up_l, order_[s0] * 128)
            for cp in range(int(os.environ.get('FFN_CP', '22'))):
                for gv_ in range(2):
                    j = cp + 22 * gv_
                    si_ = 2 * cp + gv_
                    if si_ + PF_ < len(order_):
                        loaded_[si_ + PF_] = wload_cols(w_up_l, order_[si_ + PF_] * 128)
                    wt, wb = loaded_.pop(si_)
                    ub = 2 * (uslot % 3)
                    uc = UC[uslot % 2]
                    ucb_ = ucb[uslot % 2]
                    uslot += 1
                    for sb_ in range(2):
                        for c in range(8):
                            pe.op(lambda e, o=PS(ub + sb_), a=wt[:, c, :], r=hT[:, c, base + sb_ * 512:base + (sb_ + 1) * 512],
                                  st=(c == 0), sp_=(c == 7): e.matmul(o, a, r, start=st, stop=sp_),
                                  reads=[wb, B_hT[hh * 2 + sb_]], writes=[bank[ub], bank[ub + 1]], signal=(c == 7 and sb_ == 1))
                    uhi_ = si_ % 2
                    uh = psum[:, 6 + uhi_, 0:2]
                    for c in range(8):
                        pe.op(lambda e, o=uh, a=wt[:, c, :], r=hT[:, c, base - 1:base + 1025:1025], st=(c == 0), sp_=(c == 7):
                              e.matmul(o, a, r, start=st, stop=sp_),
                              reads=[wb, B_hT[0], B_hT[1], B_hT[2], B_hT[3], B_hhalo], writes=[uhb[uhi_]], signal=(c == 7))
                    U = psum[:, ub:ub + 2, :].rearrange("p a q -> p (a q)")
                    tt_ = (TG if gv_ == 0 else TV)[cp % 2]
                    ttb_ = (tgb if gv_ == 0 else tvb)[cp % 2]
                    ub_ = [bank[ub], bank[ub + 1]]
                    act.op(lambda e, o=uc[:, 1:1025], u=U: e.copy(o, u), reads=ub_, writes=[ucb_])
                    act.op(lambda e, o=uc[:, 0:1026:1025], u=uh: e.copy(o, u), reads=[uhb[uhi_]], writes=[ucb_])
                    act.op(lambda e, o=tt_, u=U, s_=pvc(cwb + 44 + j), b_=pvc(cwb + 132 + j):
                           e.activation(o, u, AF.Identity, bias=b_, scale=s_), reads=ub_ + [B_const], writes=[ttb_])
                    dve.op(lambda e, o=tt_, u=uc[:, 0:1024], s_=pvc(cwb + j):
                           e.scalar_tensor_tensor(o, u, s_, o, ALU.mult, ALU.add), reads=[ucb_, ttb_, B_const], writes=[ttb_])
                    dve.op(lambda e, o=tt_, u=uc[:, 2:1026], s_=pvc(cwb + 88 + j):
                           e.scalar_tensor_tensor(o, u, s_, o, ALU.mult, ALU.add), reads=[ucb_, ttb_, B_const], writes=[ttb_])
                act.op(lambda e, o=TG[cp % 2]: e.activation(o, o, AF.Silu), reads=[tgb[cp % 2]], writes=[tgb[cp % 2]])
                pool.op(lambda e, o=actT[:, cp, :], a=TG[cp % 2], b_=TV[cp % 2]: e.tensor_tensor(o, a, b_, ALU.mult),
                        reads=[tgb[cp % 2], tvb[cp % 2]], writes=[B_act[cp]])
            for oc in range(int(os.environ.get('FFN_OC', '8'))):
                wi = oc % 2
                pool.dma(WD[wi], w_dn_l[:, oc * 128:(oc + 1) * 128].rearrange("(kc p) n -> p kc n", p=128), writes=[wdb[wi]])
                for sb_ in range(2):
                    pb = (oc * 2 + sb_) % 6
                    for c in range(22):
                        pe.op(lambda e, o=PS(pb), a=WD[wi][:, c, :], r=actT[:, c, sb_ * 512:(sb_ + 1) * 512], st=(c == 0), sp_=(c == 21):
                              e.matmul(o, a, r, start=st, stop=sp_), reads=[wdb[wi], B_act[c]], writes=[bank[pb]], signal=(c == 21))
                    blk = hh * 2 + sb_
                    xs = xT[:, oc, blk * 512:(blk + 1) * 512]
                    dve.op(lambda e, o=xs, p_=PS(pb): e.tensor_tensor(o, o, p_, ALU.add), reads=[bank[pb], B_xT[oc][blk]], writes=[B_xT[oc][blk]])
        k.barrier()
        return False

    if stop != 'load':
        for l in range(nlayers):
            if layer_body(l):
                break

    yT = av(0, [8, 512], F32)
    yT2 = [av(i * 16 * KB, [8, 512], F32) for i in range(2)]
    ytb = [Buf(), Buf()]
    ost = [av(32 * KB + i * 4 * KB, [1024], F32) for i in range(4)]
    ostb = [Buf() for _ in range(4)]
    out_deps = []
    cnt = {"n": 0}

    def emit_out(blk, src_of, src_b):
        for tt in range(4):
            t = blk * 4 + tt
            oi = t % 4
            for half in range(2):
                pb = cnt["n"] % 8
                cnt["n"] += 1
                for j in range(4):
                    c = half * 4 + j
                    pe.op(lambda e, o=psum[:, pb, j * 128:(j + 1) * 128], i=src_of(c, blk, tt): e.transpose(o, i, ident),
                          reads=src_b(c, blk) + [B_const], writes=[bank[pb]], signal=(j == 3))
                dst = ost[oi][:, half * 512:(half + 1) * 512]
                if half == 0:
                    dve.op(lambda e, o=dst, i=PS(pb): e.tensor_copy(o, i), reads=[bank[pb]], writes=[ostb[oi]])
                else:
                    act.op(lambda e, o=dst, i=PS(pb): e.copy(o, i), reads=[bank[pb]], writes=[ostb[oi]])
            out_deps.append(sp.dma(out_d[t * 128:(t + 1) * 128, :], ost[oi], reads=[ostb[oi]]))

    if final_norm:
        rmsnorm_T(xT, T, PV_FINAL, lambda c, blk: yT2[blk % 2][:, c, :], 56 * KB,
                  lambda c, blk: [B_xT[c][blk]], lambda c, blk: [ytb[blk % 2]],
                  after_blk=lambda blk: emit_out(blk, lambda c, b_, tt: yT2[b_ % 2][:, c, tt * 128:(tt + 1) * 128],
                                                 lambda c, b_: [ytb[b_ % 2]]))
    else:
        for blk in range(4):
            emit_out(blk, lambda c, b_, tt: xT[:, c, b_ * 512 + tt * 128:b_ * 512 + (tt + 1) * 128],
                     lambda c, b_: [B_xT[c][b_]])
    for d in out_deps:
        sp._wait(d)
    k.barrier()

    if os.environ.get('SEMDBG'):
        print('SEMS', [(e.name, e.sem.cnt, len(e.items)) for e in k.engs], [(e.name, [d.cnt for d in e.dsems]) for e in k.engs])
    with nc.Block() as block:
        @block.sync
        def _(e):
            sp.replay(e)

        @block.gpsimd
        def _(e):
            pool.replay(e)

        @block.tensor
        def _(e):
            pe.replay(e)

        @block.vector
        def _(e):
            dve.replay(e)

        @block.scalar
        def _(e):
            act.replay(e)
    stack.close()
    return nc


def _consts(p):
    ident = np.eye(128, dtype=np.float32)
    R = np.zeros((128, 128), np.float32)
    for blk in range(2):
        for a in range(2):
            o = blk * 64 + a * 32
            for f in range(16):
                R[o + f, o + 16 + f] = -1.0
                R[o + 16 + f, o + f] = 1.0
    rmatT = np.ascontiguousarray(R.T)
    t = np.arange(T) + p * T
    pos = np.stack([t // 64, t % 64], 0).astype(np.float32)
    inv = (10000.0 ** (-np.arange(16, dtype=np.float32) / 16)).astype(np.float32)
    cosT = np.zeros((128, T), np.float32)
    sinT = np.zeros((128, T), np.float32)
    for d in range(128):
        dd = d % 64
        a = dd // 32
        f = dd % 16
        ang = (pos[a] * inv[f]).astype(np.float32)
        cosT[d] = np.cos(ang)
        sinT[d] = np.sin(ang)
    kr = (np.arange(128) // 64)[:, None, None]
    kc = (np.arange(128) % 64)[:, None, None]
    idx = np.arange(14)[None, :, None]
    qc = np.arange(64)[None, None, :]
    dr = 6 + kr - idx + 0 * qc
    cs = np.clip(qc - 8, 0, 48)
    colvalid = (kc >= cs) & (kc < cs + 16)
    band = (dr >= -4) & (dr <= 3)
    full = colvalid & (dr >= -7) & (dr <= 7)
    bandm = colvalid & band
    mA = full if p == 0 else bandm
    mC = bandm if p == 0 else full
    namask = np.stack([mA, bandm, mC], 0).astype(np.float32).reshape(3, 128, 896)
    return ident, rmatT, cosT, sinT, namask


def _na_gather(na_rpb):
    kr = (np.arange(128) // 64)[:, None, None]
    kc = (np.arange(128) % 64)[:, None, None]
    idx = np.arange(14)[None, :, None]
    qc = np.arange(64)[None, None, :]
    dr = np.clip(6 + kr - idx + 0 * qc, -7, 7) + 7
    dc = np.clip(kc - qc + 0 * idx, -15, 15) + 15
    g = na_rpb[:, :, dr, dc]
    return np.ascontiguousarray(g.reshape(L, 8, 128, 896)).astype(np.float32)


def _pv(p, inp):
    pv = np.zeros((128, PV_N), np.float32)

    def col8(v):
        return np.asarray(v, np.float32).reshape(8, 128).T

    for l in range(L):
        b = PV_NORM + l * 32
        pv[:, b + 0:b + 8] = col8(inp["norm_mix"][l])
        pv[:, b + 8:b + 16] = col8(inp["norm_mem_q"][l])
        pv[:, b + 16:b + 24] = col8(inp["norm_mem_kv"][l])
        pv[:, b + 24:b + 32] = col8(inp["norm_ffn"][l])
        pv[:, PV_GQ + l] = np.tile(np.asarray(inp["gqa_q_norm"][l], np.float32), 2)
        pv[:, PV_GK + l] = np.tile(np.asarray(inp["gqa_k_norm"][l], np.float32), 2)
        cw = np.asarray(inp["conv_w"][l], np.float32)
        cb = np.asarray(inp["conv_b"][l], np.float32)
        o = PV_CW + l * 176
        for kk in range(3):
            pv[:, o + kk * 44:o + (kk + 1) * 44] = cw[kk].reshape(44, 128).T
        pv[:, o + 132:o + 176] = cb.reshape(44, 128).T
    pv[:, PV_FINAL:PV_FINAL + 8] = col8(inp["norm_final"])
    pv[:, PV_FLAGL] = 1.0 if p == 1 else 0.0
    pv[:, PV_FLAGR] = 1.0 if p == 0 else 0.0
    return pv


_CACHE = {}


def kernel(**inputs):
    inp = {k_: np.asarray(v) for k_, v in inputs.items()}
    if "nc" not in _CACHE:
        _CACHE["nc"] = build_nc()
    nc = _CACHE["nc"]
    nag = _na_gather(np.asarray(inp["na_rpb"], np.float32))
    shared = {n_: np.ascontiguousarray(inp[n_], dtype=np.float32) for n_ in
              ("w_in", "w_out", "w_mem_q", "w_mem_kv", "w_mem_o", "w_up", "w_down")}
    in_maps = []
    for c in range(8):
        b, p = c // 2, c % 2
        ident, rmatT, cosT, sinT, namask = _consts(p)
        m = dict(shared)
        m["x"] = np.ascontiguousarray(inp["x"][b, p * T:(p + 1) * T, :], dtype=np.float32)
        m["mem"] = np.ascontiguousarray(inp["mem"][b], dtype=np.float32)
        m["pv"] = _pv(p, inp)
        m["ident"] = ident
        m["rmatT"] = rmatT
        m["cosT"] = cosT
        m["sinT"] = sinT
        m["nag"] = nag
        m["namask"] = namask
        in_maps.append(m)
    res = run_bass_kernel_spmd(nc, in_maps, core_ids=list(range(8)))
    out = np.zeros((4, 4096, D), np.float32)
    for c in range(8):
        b, p = c // 2, c % 2
        out[b, p * T:(p + 1) * T, :] = np.asarray(res.results[c]["out"], dtype=np.float32)
    return out
```

```python
import numpy as np
import concourse.bass as bass
import concourse.mybir as mybir
from concourse.bass_utils import run_bass_kernel_spmd

F32 = mybir.dt.float32
BF16 = mybir.dt.bfloat16
AF = mybir.ActivationFunctionType
ALU = mybir.AluOpType

L = 2
D = 1024
T = 2048
NCH = 8
DFF = 2816
NJ = 44
EPS = 1e-6
PAIRS = [[0, 1], [2, 3], [4, 5], [6, 7]]

PV_NORM = 0
PV_FINAL = 64
PV_GQ = 72
PV_GK = 74
PV_FLAGL = 76
PV_FLAGR = 77
PV_CW = 80
PV_N = 80 + 2 * 176


class Buf:
    __slots__ = ("w", "r")

    def __init__(self):
        self.w = []
        self.r = []


class Sem:
    def __init__(self, h):
        self.h = h
        self.cnt = 0


class Eng:
    def __init__(self, K, name):
        self.K = K
        self.name = name
        self.sem = K.new_sem(name)
        self.items = []
        self.seen = {}
        self.pend_r = []
        self.pend_w = []
        self.dsems = []
        self.dsi = 0

    def _wait(self, dep):
        s, v = dep
        if self.seen.get(s, 0) >= v:
            return
        self.seen[s] = v
        self.items.append(("w", s.h, v))

    def _deps(self, reads, writes):
        for b in reads:
            for d in b.w:
                self._wait(d)
        for b in writes:
            for d in b.w:
                self._wait(d)
            for d in b.r:
                self._wait(d)

    def _register(self, dep, reads, writes):
        for b in reads:
            b.r.append(dep)
            if len(b.r) > 12:
                m = {}
                for s, v in b.r:
                    if m.get(s, 0) < v:
                        m[s] = v
                b.r = list(m.items())
        for b in writes:
            b.w = [dep]
            b.r = []

    def op(self, fn, reads=(), writes=(), signal=True):
        self._deps(reads, writes)
        if signal:
            self.sem.cnt += 1
            dep = (self.sem, self.sem.cnt)
            self.items.append(("o", fn, self.sem.h, 1))
            self._register(dep, list(reads) + self.pend_r, list(writes) + self.pend_w)
            self.pend_r = []
            self.pend_w = []
            return dep
        self.items.append(("o", fn, None, 0))
        self.pend_r += list(reads)
        self.pend_w += list(writes)
        return None

    def dma(self, out, in_, reads=(), writes=()):
        self._deps(reads, writes)
        if not self.dsems:
            self.dsems = [self.K.new_sem(self.name + "_d%d" % i) for i in range(6)]
        s = self.dsems[self.dsi % len(self.dsems)]
        self.dsi += 1
        s.cnt += 16
        dep = (s, s.cnt)
        self.items.append(("o", lambda e, o=out, i=in_: e.dma_start(out=o, in_=i), s.h, 16))
        self._register(dep, reads, writes)
        return dep

    def replay(self, e):
        for it in self.items:
            if it[0] == "w":
                e.wait_ge(it[1], it[2])
            else:
                ins = it[1](e)
                if it[2] is not None:
                    ins.then_inc(it[2], it[3])


class K:
    def __init__(self, nc, stack):
        self.nc = nc
        self.stack = stack
        self.sems = []
        self.pe = Eng(self, "pe")
        self.act = Eng(self, "act")
        self.dve = Eng(self, "dve")
        self.pool = Eng(self, "pool")
        self.sp = Eng(self, "sp")
        self.engs = [self.pe, self.act, self.dve, self.pool, self.sp]

    def new_sem(self, name):
        h = self.stack.enter_context(self.nc.semaphore(name))
        s = Sem(h)
        self.sems.append(s)
        return s

    def barrier(self):
        for e in self.engs:
            if e.pend_r or e.pend_w:
                raise RuntimeError("pending unsignaled ops at barrier on " + e.name)
        for e in self.engs:
            if e is self.pool:
                continue
            for s in self.sems:
                if s.cnt > 0:
                    e._wait((s, s.cnt))


from contextlib import ExitStack


def build_nc(nlayers=L, final_norm=True, stop=None):
    nc = bass.Bass("TRN2", target_bir_lowering=False)
    stack = ExitStack()
    k = K(nc, stack)
    pe, act, dve, pool, sp = k.pe, k.act, k.dve, k.pool, k.sp

    def din(name, shape, dt=F32):
        return nc.dram_tensor(name, list(shape), dt, kind="ExternalInput").ap()

    x_d = din("x", [T, D])
    mem_d = din("mem", [256, D])
    w_in_d = din("w_in", [L, D, 2304])
    w_out_d = din("w_out", [L, D, D])
    w_mq_d = din("w_mem_q", [L, D, 512])
    w_mkv_d = din("w_mem_kv", [L, D, 1024])
    w_mo_d = din("w_mem_o", [L, 512, D])
    w_up_d = din("w_up", [L, D, 2 * DFF])
    w_dn_d = din("w_down", [L, DFF, D])
    pv_d = din("pv", [128, PV_N])
    ident_d = din("ident", [128, 128])
    rt_d = din("rmatT", [128, 128])
    cos_d = din("cosT", [128, T])
    sin_d = din("sinT", [128, T])
    nag_d = din("nag", [L, 8, 128, 896])
    namask_d = din("namask", [3, 128, 896])
    out_d = nc.dram_tensor("out", [T, D], F32, kind="ExternalOutput").ap()

    e1_in = [nc.dram_tensor("e1in%d" % l, [384, 2048], BF16) for l in range(L)]
    e1_out = [nc.dram_tensor("e1out%d" % l, [768, 2048], BF16) for l in range(L)]
    e2_in = [nc.dram_tensor("e2in%d" % l, [512, 1536], BF16) for l in range(L)]
    e2_out = [nc.dram_tensor("e2out%d" % l, [1024, 1536], BF16) for l in range(L)]
    e3_in = [nc.dram_tensor("e3in%d" % l, [128, 16], BF16) for l in range(L)]
    e3_out = [nc.dram_tensor("e3out%d" % l, [256, 16], BF16) for l in range(L)]
    cc_sems = [k.new_sem("cc%d" % i) for i in range(3 * L)]

    import os
    ARENA_B = int(os.environ.get('ARENA_KB', '196')) * 1024
    arena = stack.enter_context(nc.sbuf_tensor("arena", [128, ARENA_B // 2], BF16))
    psum = stack.enter_context(nc.psum_tensor("psum", [128, 8, 512], F32))

    def view(off_bytes, shape, dt):
        esz = 4 if dt == F32 else 2
        n = int(np.prod(shape))
        assert off_bytes % 4 == 0 and off_bytes + n * esz <= ARENA_B, (off_bytes, shape)
        ap = arena[:, off_bytes // 2: off_bytes // 2 + n * esz // 2]
        if dt == F32:
            ap = ap.bitcast(F32)
        if len(shape) == 2:
            return ap.rearrange("p (a b) -> p a b", a=shape[0])
        if len(shape) == 3:
            return ap.rearrange("p (a b c) -> p a b c", a=shape[0], b=shape[1])
        return ap

    KB = 1024
    off = 0

    def take(nbytes):
        nonlocal off
        o = off
        off += (nbytes + 31) // 32 * 32
        return o

    xT = view(take(64 * KB), [8, T], F32)
    hT = view(take(8 * 2050 * 2), [8, 2050], BF16)
    HT_OFF = off - (8 * 2050 * 2 + 31) // 32 * 32
    NW = 8
    wring = [view(take(2 * KB), [8, 128], BF16) for _ in range(NW)]
    wbuf = [Buf() for _ in range(NW)]
    ident = view(take(512), [128], F32)
    onesblk = view(take(256), [128], BF16)
    ones128 = view(take(256), [128], BF16)
    rmt = view(take(256), [128], BF16)
    pv = view(take(PV_N * 4), [PV_N], F32)
    A0 = off
    assert stop == 'load' or ARENA_B - A0 >= 80 * KB, (ARENA_B - A0)

    def av(o, shape, dt):
        return view(A0 + o, shape, dt)

    def hv(o, shape, dt):
        return view(HT_OFF + o, shape, dt)

    bank = [Buf() for _ in range(8)]

    def PS(b, n=512):
        return psum[:, b, 0:n]

    def PS2(b):
        return psum[:, b:b + 2, :]

    B_xT = [[Buf() for _ in range(4)] for _ in range(8)]
    B_hT = [Buf() for _ in range(4)]
    B_hhalo = Buf()
    B_const = Buf()

    wstate = {"i": 0}

    def wload(src_ap):
        i = wstate["i"] % NW
        wstate["i"] += 1
        pool.dma(wring[i], src_ap, writes=[wbuf[i]])
        return wring[i], wbuf[i]

    def wload_cols(w_l, c0, ncols=128):
        src = w_l[:, c0:c0 + ncols].rearrange("(kc p) n -> p kc n", p=128)
        i = wstate["i"] % NW
        wstate["i"] += 1
        dst = wring[i][:, :, 0:ncols]
        pool.dma(dst, src, writes=[wbuf[i]])
        return wring[i], wbuf[i]

    sp.dma(ident, ident_d, writes=[B_const])
    sp.dma(pv, pv_d, writes=[B_const])
    pool.dma(rmt, rt_d, writes=[B_const])
    dve.op(lambda e: e.memset(ones128, 1.0), writes=[B_const])
    dve.op(lambda e: e.memset(onesblk, 0.0), writes=[B_const])
    dve.op(lambda e: e.memset(onesblk[0:64, 0:64], 1.0), writes=[B_const])
    dve.op(lambda e: e.memset(onesblk[64:128, 64:128], 1.0), writes=[B_const])
    dve.op(lambda e: e.memset(hT[:, :, 0:1], 0.0), writes=[B_hhalo])
    dve.op(lambda e: e.memset(hT[:, :, 2049:2050], 0.0), writes=[B_hhalo])
    k.barrier()

    def pvc(c, n=1):
        return pv[:, c:c + n]

    def load_transpose(src_d, ntiles, dstT, dstbufs, stage_off):
        stg = [av(stage_off + i * 4 * KB, [1024], F32) for i in range(2)]
        sb = [Buf(), Buf()]
        for t in range(ntiles):
            s = stg[t % 2]
            sp.dma(s, src_d[t * 128:(t + 1) * 128, :], writes=[sb[t % 2]])
            for half in range(2):
                b = (2 * t + half) % 8
                for j in range(4):
                    c = half * 4 + j
                    pe.op(lambda e, o=psum[:, b, j * 128:(j + 1) * 128], i=s[:, c * 128:(c + 1) * 128]:
                          e.transpose(o, i, ident),
                          reads=[sb[t % 2], B_const], writes=[bank[b]], signal=(j == 3))
                wb = dstbufs(t, half)
                dst = dstT[:, half * 4:half * 4 + 4, t * 128:(t + 1) * 128]
                src = psum[:, b, :].rearrange("p (a q) -> p a q", a=4)
                eng = dve if half == 0 else act
                if eng is dve:
                    dve.op(lambda e, o=dst, i=src: e.tensor_copy(o, i), reads=[bank[b]], writes=wb)
                else:
                    act.op(lambda e, o=dst, i=src: e.copy(o, i), reads=[bank[b]], writes=wb)

    load_transpose(x_d, 16, xT, lambda t, half: [B_xT[half * 4 + j][t // 4] for j in range(4)], 0)
    k.barrier()

    def rmsnorm_T(src, ncols, gcol, dst_fn, tmp_off, src_bufs, dst_bufs, nblk=None, after_blk=None):
        sq = [av(tmp_off + i * KB, [512], BF16) for i in range(4)]
        sqb = [Buf() for _ in range(4)]
        rs = [av(tmp_off + 4 * KB + i * 2 * KB, [512], F32) for i in range(2)]
        rsb = [Buf(), Buf()]
        nb = ncols // 512 if nblk is None else nblk
        w = min(512, ncols)
        for blk in range(nb):
            pb = blk % 2
            for c in range(8):
                i = (blk * 8 + c) % 4
                act.op(lambda e, o=sq[i][:, 0:w], s=src[:, c, blk * w:(blk + 1) * w]: e.activation(o, s, AF.Square),
                       reads=src_bufs(c, blk), writes=[sqb[i]])
                pe.op(lambda e, o=PS(pb, w), r=sq[i][:, 0:w], st=(c == 0), sp_=(c == 7):
                      e.matmul(o, ones128, r, start=st, stop=sp_),
                      reads=[sqb[i], B_const], writes=[bank[pb]], signal=(c == 7))
            r = rs[blk % 2]
            act.op(lambda e, o=r[:, 0:w], s=PS(pb, w): e.activation(o, s, AF.Ln, bias=EPS, scale=1.0 / D),
                   reads=[bank[pb]], writes=[rsb[blk % 2]])
            act.op(lambda e, o=r[:, 0:w]: e.activation(o, o, AF.Exp, scale=-0.5),
                   reads=[rsb[blk % 2]], writes=[rsb[blk % 2]])
            for c in range(8):
                dve.op(lambda e, o=dst_fn(c, blk), s=src[:, c, blk * w:(blk + 1) * w], g=pvc(gcol + c), rr=r[:, 0:w]:
                       e.scalar_tensor_tensor(o, s, g, rr, ALU.mult, ALU.mult),
                       reads=src_bufs(c, blk) + [rsb[blk % 2], B_const], writes=dst_bufs(c, blk))
            if after_blk is not None:
                after_blk(blk)

    def norm_x_to_hT(gcol, tmp_off):
        rmsnorm_T(xT, T, gcol, lambda c, blk: hT[:, c, 1 + blk * 512:1 + (blk + 1) * 512], tmp_off,
                  lambda c, blk: [B_xT[c][blk]], lambda c, blk: [B_hT[blk]])

    def hblk(c, blk):
        return hT[:, c, 1 + blk * 512:1 + (blk + 1) * 512]

    cci = {"i": 0}

    def allgather(src_t, dst_t, reads):
        s = cc_sems[cci["i"]]
        cci["i"] += 1
        pool._deps(reads, [])
        s.cnt += 1
        pool.items.append(("o", lambda e: e.collective_compute(
            "AllGather", ALU.bypass, replica_groups=PAIRS,
            ins=[src_t.ap().opt()], outs=[dst_t.ap().opt()]), s.h, 1))
        b = Buf()
        b.w = [(s, 1)]
        return b

    def proj_residual(w_l, nk, rhs_fn, rhs_bufs):
        for oc in range(8):
            i = wstate["i"] % NW
            wstate["i"] += 1
            pool.dma(wring[i][:, 0:nk, :], w_l[:, oc * 128:(oc + 1) * 128].rearrange("(kc p) n -> p kc n", p=128), writes=[wbuf[i]])
            wt, wb = wring[i], wbuf[i]
            for blk in range(4):
                pb = (oc * 4 + blk) % 4
                for c in range(nk):
                    pe.op(lambda e, o=PS(pb), a=wt[:, c, :], r=rhs_fn(c, blk), st=(c == 0), sp_=(c == nk - 1):
                          e.matmul(o, a, r, start=st, stop=sp_),
                          reads=[wb] + rhs_bufs(c, blk), writes=[bank[pb]], signal=(c == nk - 1))
                xs = xT[:, oc, blk * 512:(blk + 1) * 512]
                dve.op(lambda e, o=xs, p_=PS(pb): e.tensor_tensor(o, o, p_, ALU.add), reads=[bank[pb], B_xT[oc][blk]], writes=[B_xT[oc][blk]])


    def layer_body(l):
        nbase = PV_NORM + l * 32
        w_in_l = w_in_d[l]
        norm_x_to_hT(nbase + 0, 64 * KB)
        k.barrier()
        if stop == (l, 'norm'):
            return True

        qT = av(0, [4, T], BF16)
        ropeC = av(16 * KB, [T], F32)
        ropeS = av(24 * KB, [T], F32)
        kfull = av(16 * KB, [2, 4096], BF16)
        vfull = av(32 * KB, [32, 320], BF16)
        TMP = 52 * KB
        B_rope = Buf()
        sp.dma(ropeC, cos_d, writes=[B_rope])
        sp.dma(ropeS, sin_d, writes=[B_rope])
        B_qT = [[Buf() for _ in range(4)] for _ in range(4)]
        nr_tb = [{n_: Buf() for n_ in ('sq', 'gv', 'rs', 'aa', 'bb')} for _ in range(2)]

        def normrope(ps_b, gcol, dst_ap, dst_bufs, blk, tmpo, ti):
            sl = ti % 2
            o = tmpo + sl * 8 * KB
            sqv = av(o, [512], BF16)
            gv = av(o + KB, [512], BF16)
            rsv = av(o + 2 * KB, [512], F32)
            aa = av(o + 4 * KB, [512], F32)
            bb = av(o + 6 * KB, [512], F32)
            B = nr_tb[sl]
            b2 = 4 + 2 * sl
            b3 = b2 + 1
            act.op(lambda e: e.activation(sqv, PS(ps_b), AF.Square), reads=[bank[ps_b]], writes=[B["sq"]])
            act.op(lambda e: e.activation(gv, PS(ps_b), AF.Identity, scale=pvc(gcol)), reads=[bank[ps_b], B_const], writes=[B["gv"]])
            pe.op(lambda e: e.matmul(PS(b2), onesblk, sqv, start=True, stop=True), reads=[B["sq"], B_const], writes=[bank[b2]])
            pe.op(lambda e: e.matmul(PS(b3), rmt, gv, start=True, stop=True), reads=[B["gv"], B_const], writes=[bank[b3]])
            act.op(lambda e: e.activation(rsv, PS(b2), AF.Ln, bias=EPS, scale=1.0 / 64), reads=[bank[b2]], writes=[B["rs"]])
            act.op(lambda e: e.activation(rsv, rsv, AF.Exp, scale=-0.5), reads=[B["rs"]], writes=[B["rs"]])
            cs = ropeC[:, blk * 512:(blk + 1) * 512]
            sn = ropeS[:, blk * 512:(blk + 1) * 512]
            dve.op(lambda e: e.tensor_tensor(aa, gv, cs, ALU.mult), reads=[B["gv"], B_rope], writes=[B["aa"]])
            dve.op(lambda e: e.tensor_tensor(bb, PS(b3), sn, ALU.mult), reads=[bank[b3], B_rope], writes=[B["bb"]])
            dve.op(lambda e: e.tensor_tensor(aa, aa, bb, ALU.add), reads=[B["aa"], B["bb"]], writes=[B["aa"]])
            dve.op(lambda e: e.tensor_tensor(dst_ap, aa, rsv, ALU.mult), reads=[B["aa"], B["rs"]], writes=dst_bufs)

        kst = [av(TMP + 16 * KB + i * KB, [512], BF16) for i in range(2)]
        kstb = [Buf(), Buf()]
        e1_deps = []
        ti = 0
        for g in range(2):
            i = wstate["i"] % NW
            wstate["i"] += 1
            for hf in range(2):
                pool.dma(wring[i][:, :, hf * 64:(hf + 1) * 64],
                         w_in_l[:, 2048 + g * 64:2048 + (g + 1) * 64].rearrange("(kc p) n -> p kc n", p=128),
                         writes=[wbuf[i]])
            wt, wb = wring[i], wbuf[i]
            for blk in range(4):
                pb = blk % 2
                for c in range(8):
                    pe.op(lambda e, o=PS(pb), a=wt[:, c, :], r=hblk(c, blk), st=(c == 0), sp_=(c == 7):
                          e.matmul(o, a, r, start=st, stop=sp_),
                          reads=[wb, B_hT[blk]], writes=[bank[pb]], signal=(c == 7))
                si = ti % 2
                normrope(pb, PV_GK + l, kst[si], [kstb[si]], blk, TMP, ti)
                ti += 1
                d = sp.dma(e1_in[l][g * 128:(g + 1) * 128, blk * 512:(blk + 1) * 512], kst[si], reads=[kstb[si]])
                e1_deps.append(d)
        wt, wb = wload_cols(w_in_l, 2176)
        vst = av(TMP + 18 * KB, [16, 128], BF16)
        vstb = Buf()
        for t4 in range(4):
            pb = 2 + t4 % 2
            for tt in range(4):
                t = t4 * 4 + tt
                for c in range(8):
                    pe.op(lambda e, o=psum[:, pb, tt * 128:(tt + 1) * 128], a=hT[:, c, 1 + t * 128:1 + (t + 1) * 128], r=wt[:, c, :],
                          st=(c == 0), sp_=(c == 7): e.matmul(o, a, r, start=st, stop=sp_),
                          reads=[wb, B_hT[t // 4]], writes=[bank[pb]], signal=(c == 7 and tt == 3))
            act.op(lambda e, o=vst[:, t4 * 4:(t4 + 1) * 4, :], i=psum[:, pb, :].rearrange("p (a q) -> p a q", a=4): e.copy(o, i),
                   reads=[bank[pb]], writes=[vstb])
        vdst = e1_in[l][256:384, :].rearrange("p (t n) -> p t n", t=16)
        d = sp.dma(vdst, vst, reads=[vstb])
        e1_deps.append(d)
        eb = Buf()
        eb.w = e1_deps
        e1_done = allgather(e1_in[l], e1_out[l], [eb])

        for c4 in range(4):
            wt, wb = wload_cols(w_in_l, 1536 + c4 * 128)
            for blk in range(4):
                pb = blk % 2
                for c in range(8):
                    pe.op(lambda e, o=PS(pb), a=wt[:, c, :], r=hblk(c, blk), st=(c == 0), sp_=(c == 7):
                          e.matmul(o, a, r, start=st, stop=sp_),
                          reads=[wb, B_hT[blk]], writes=[bank[pb]], signal=(c == 7))
                normrope(pb, PV_GQ + l, qT[:, c4, blk * 512:(blk + 1) * 512], [B_qT[c4][blk]], blk, TMP, ti)
                ti += 1
        k.barrier()
        if stop == (l, 'gqaproj'):
            return True

        B_kf = Buf()
        B_vf = Buf()
        vf5 = vfull.rearrange("p t (s d) -> p t s d", d=64)
        dve.op(lambda e: e.memset(vf5[:, :, 0:5:2, :], 1.0), writes=[B_vf])
        for r in range(2):
            for g in range(2):
                sp.dma(kfull[:, g, r * 2048:(r + 1) * 2048], e1_out[l][384 * r + 128 * g:384 * r + 128 * (g + 1), :],
                       reads=[e1_done], writes=[B_kf])
            vsrc = e1_out[l][384 * r + 256:384 * r + 384, :].rearrange("p (t n) -> p t n", t=16)
            sp.dma(vfull[:, r * 16:(r + 1) * 16, 64:128], vsrc[:, :, 0:64], reads=[e1_done], writes=[B_vf])
            sp.dma(vfull[:, r * 16:(r + 1) * 16, 192:256], vsrc[:, :, 64:128], reads=[e1_done], writes=[B_vf])

        PT = [av(TMP + i * 2 * KB, [1024], BF16) for i in range(2)]
        ptb = [Buf(), Buf()]
        rden = av(TMP + 4 * KB, [1024], F32)
        rdb = Buf()
        it = 0
        rden2 = av(TMP + 4 * KB, [512], F32)
        for hp in range(4):
            g = hp // 2
            c4 = hp
            for qb in range(4):
                oe = 4 + 2 * ((hp * 4 + qb) % 2)
                qbufs = [B_qT[c4][qb]]
                qs = slice(qb * 512, (qb + 1) * 512)

                def s_mm(kt, sb):
                    for odd in range(2):
                        r0 = 64 * odd
                        pe.op(lambda e, o=PS(sb + odd), a=kfull[r0:r0 + 64, g, kt * 128:(kt + 1) * 128], r=qT[r0:r0 + 64, c4, qs]:
                              e.matmul(o, a, r, start=True, stop=True),
                              reads=[B_kf] + qbufs, writes=[bank[sb], bank[sb + 1]], signal=(odd == 1))

                s_mm(0, 0)
                for kt in range(32):
                    sb = 2 * (kt % 2)
                    if kt + 1 < 32:
                        s_mm(kt + 1, 2 * ((kt + 1) % 2))
                    p = PT[it % 2]
                    pbf = ptb[it % 2]
                    it += 1
                    act.op(lambda e, o=p, s_=psum[:, sb:sb + 2, :].rearrange("p a q -> p (a q)"): e.activation(o, s_, AF.Exp, scale=0.125),
                           reads=[bank[sb], bank[sb + 1]], writes=[pbf])
                    for odd in range(2):
                        vc0 = (64 if not odd else 0) + 128 * g
                        pe.op(lambda e, o=PS(oe + odd), a=vfull[:, kt, vc0:vc0 + 128], r=p[:, odd * 512:(odd + 1) * 512], st=(kt == 0), sp_=(kt == 31):
                              e.matmul(o, a, r, start=st, stop=sp_),
                              reads=[pbf, B_vf], writes=[bank[oe], bank[oe + 1]], signal=(odd == 1))
                obs = [bank[oe], bank[oe + 1]]
                dve.op(lambda e, o=rden2[64:128, :], i=psum[64:128, oe, :]: e.reciprocal(o, i), reads=obs, writes=[rdb], signal=False)
                dve.op(lambda e, o=rden2[0:64, :], i=psum[0:64, oe + 1, :]: e.reciprocal(o, i), reads=obs, writes=[rdb])
                dve.op(lambda e, o=qT[0:64, c4, qs], a=psum[0:64, oe, :], b=rden2[64:128, :]:
                       e.tensor_tensor(o, a, b, ALU.mult), reads=obs + [rdb], writes=qbufs, signal=False)
                dve.op(lambda e, o=qT[64:128, c4, qs], a=psum[64:128, oe + 1, :], b=rden2[0:64, :]:
                       e.tensor_tensor(o, a, b, ALU.mult), reads=obs + [rdb], writes=qbufs)
        pass
        proj_residual(w_out_d[l][512:1024, :], 4,
                      lambda c, blk: qT[:, c, blk * 512:(blk + 1) * 512],
                      lambda c, blk: [B_qT[c][blk]])
        k.barrier()
        if stop == (l, 'gqa'):
            return True

        naq = av(0, [4, T], BF16)
        nakT = av(16 * KB, [4, 2560], BF16)
        naV = av(36 * KB, [20, 768], BF16)
        nav5 = naV.rearrange("p t (c s d) -> p t c s d", c=4, s=3)
        NTMP = 66 * KB
        B_naq = [[Buf() for _ in range(8)] for _ in range(4)]
        B_nak = Buf()
        B_nav = Buf()
        def na_qk_proj(which, base_c, dstT, coff):
            for c4 in range(4):
                wt, wb = wload_cols(w_in_l, base_c + c4 * 128)
                for blk in range(4):
                    pb = blk % 2
                    for c in range(8):
                        pe.op(lambda e, o=PS(pb), a=wt[:, c, :], r=hblk(c, blk), st=(c == 0), sp_=(c == 7):
                              e.matmul(o, a, r, start=st, stop=sp_),
                              reads=[wb, B_hT[blk]], writes=[bank[pb]], signal=(c == 7))
                    dst = dstT[:, c4, coff + blk * 512:coff + (blk + 1) * 512]
                    wbs = [B_naq[c4][2 * blk], B_naq[c4][2 * blk + 1]] if which == 0 else [B_nak]
                    if (c4 + blk) % 2 == 0:
                        act.op(lambda e, o=dst, i=PS(pb): e.copy(o, i), reads=[bank[pb]], writes=wbs)
                    else:
                        dve.op(lambda e, o=dst, i=PS(pb): e.tensor_copy(o, i), reads=[bank[pb]], writes=wbs)
        na_qk_proj(1, 512, nakT, 256)
        wv = []
        for c4 in range(4):
            wv.append(wload_cols(w_in_l, 1024 + c4 * 128))
        for t in range(16):
            pb = 2 + t % 2
            for c4 in range(4):
                wt, wb = wv[c4]
                for c in range(8):
                    pe.op(lambda e, o=psum[:, pb, c4 * 128:(c4 + 1) * 128], a=hT[:, c, 1 + t * 128:1 + (t + 1) * 128], r=wt[:, c, :],
                          st=(c == 0), sp_=(c == 7): e.matmul(o, a, r, start=st, stop=sp_),
                          reads=[wb, B_hT[t // 4]], writes=[bank[pb]], signal=(c == 7 and c4 == 3))
            vdst_ = nav5[:, 2 + t, :, 0:3:2, :]
            vsrc_ = psum[:, pb, :].rearrange("p (c e d) -> p c e d", c=4, e=2)
            if t % 2 == 0:
                act.op(lambda e, o=vdst_, i=vsrc_: e.copy(o, i), reads=[bank[pb]], writes=[B_nav])
            else:
                dve.op(lambda e, o=vdst_, i=vsrc_: e.tensor_copy(o, i), reads=[bank[pb]], writes=[B_nav])
        dve.op(lambda e: e.memset(nav5[:, 2:18, :, 1, :], 1.0), writes=[B_nav])
        e2d = []
        kview = lambda t_, r0_: t_[r0_:r0_ + 128, 0:1024].rearrange("p (c n) -> p c n", c=4)
        vview = lambda t_, r0_: t_[r0_:r0_ + 128, :].rearrange("p (s n) -> p s n", s=2)
        e2d.append(sp.dma(kview(e2_in[l], 0), nakT[:, :, 256:512], reads=[B_nak]))
        e2d.append(sp.dma(kview(e2_in[l], 128), nakT[:, :, 2048:2304], reads=[B_nak]))
        e2d.append(sp.dma(vview(e2_in[l], 256), naV[:, 2:4, :], reads=[B_nav]))
        e2d.append(sp.dma(vview(e2_in[l], 384), naV[:, 16:18, :], reads=[B_nav]))
        eb2 = Buf()
        eb2.w = e2d
        e2_done = allgather(e2_in[l], e2_out[l], [eb2])
        sp.dma(nakT[:, :, 0:256], kview(e2_out[l], 128), reads=[e2_done], writes=[B_nak])
        sp.dma(nakT[:, :, 2304:2560], kview(e2_out[l], 512), reads=[e2_done], writes=[B_nak])
        sp.dma(naV[:, 0:2, :], vview(e2_out[l], 384), reads=[e2_done], writes=[B_nav])
        sp.dma(naV[:, 18:20, :], vview(e2_out[l], 512 + 256), reads=[e2_done], writes=[B_nav])
        na_qk_proj(0, 0, naq, 0)
        dve.op(lambda e: e.tensor_scalar(naV[:, 0:2, :], naV[:, 0:2, :], pvc(PV_FLAGL), None, ALU.mult), reads=[B_nav, B_const], writes=[B_nav])
        dve.op(lambda e: e.tensor_scalar(naV[:, 18:20, :], naV[:, 18:20, :], pvc(PV_FLAGR), None, ALU.mult), reads=[B_nav, B_const], writes=[B_nav])
        k.barrier()

        nmask = [hv(i * 1792, [896], BF16) for i in range(3)]
        B_nmask = Buf()
        for i in range(3):
            pool.dma(nmask[i], namask_d[i], writes=[B_nmask] + B_hT)
        gst = [hv(6 * KB + i * 3584, [896], F32) for i in range(2)]
        gstb = [Buf(), Buf()]
        ttf = [hv(14 * KB + i * 1792, [896], BF16) for i in range(2)]
        ttfb = [Buf(), Buf()]
        ttab = [[hv(18 * KB + (i * 3 + m) * 1792, [896], BF16) for m in range(3)] for i in range(2)]
        ttabb = [Buf(), Buf()]
        NPT = [av(NTMP + i * KB, [2, 256], BF16) for i in range(4)]
        nptb = [[Buf(), Buf()] for _ in range(4)]
        NE = [av(NTMP + 4 * KB + i * KB, [2, 256], BF16) for i in range(4)]
        neb = [Buf() for _ in range(4)]
        nrd = av(NTMP + 8 * KB, [256], F32)
        nrdb = Buf()
        def na_tables(h):
            hi = h % 2
            sp.dma(gst[hi], nag_d[l, h], writes=[gstb[hi]])
            act.op(lambda e, o=ttf[hi], i=gst[hi]: e.activation(o, i, AF.Exp), reads=[gstb[hi]], writes=[ttfb[hi]])
            for m in range(3):
                dve.op(lambda e, o=ttab[hi][m], a=ttf[hi], b=nmask[m]: e.tensor_tensor(o, a, b, ALU.mult),
                       reads=[ttfb[hi], B_nmask], writes=[ttabb[hi]])

        steps = [(h, b, wp) for h in range(8) for b in range(8) for wp in range(3)]
        NS_ = len(steps)
        LAG = 3

        def na_front(n):
            h, b, wp = steps[n]
            c4 = h // 2
            r0 = 64 * (h % 2)
            hi = h % 2
            if b == 0 and wp == 0:
                na_tables(h)
            tab = ttab[hi][0 if b == 0 else (2 if b == 7 else 1)]
            qb_ = [B_naq[c4][b]]
            sbk = n % 4
            for j in range(2):
                w = 2 * wp + 1 - j
                s_ = 2 * b + w
                pe.op(lambda e, o=psum[:, sbk, j * 256:(j + 1) * 256], a=nakT[r0:r0 + 64, c4, s_ * 128:(s_ + 1) * 128],
                      r=naq[r0:r0 + 64, c4, b * 256:(b + 1) * 256]: e.matmul(o, a, r, start=True, stop=True),
                      reads=[B_nak] + qb_, writes=[bank[sbk]], signal=(j == 1))
            ii = n % 4
            act.op(lambda e, o=NE[ii], s2=psum[:, sbk, :].rearrange("p (a q) -> p a q", a=2): e.activation(o, s2, AF.Exp, scale=0.125),
                   reads=[bank[sbk]], writes=[neb[ii]])
            w_hi = 2 * wp + 1
            c0 = (10 - 2 * w_hi) * 64
            tsl = tab[:, c0:c0 + 256]
            tsl2 = tab[:, c0 + 128:c0 + 384]
            dve.op(lambda e, o=NPT[ii][:, 0, :], a=NE[ii][:, 0, :], b_=tsl: e.tensor_tensor(o, a, b_, ALU.mult),
                   reads=[neb[ii], ttabb[hi]], writes=[nptb[ii][0]])
            pool.op(lambda e, o=NPT[ii][:, 1, :], a=NE[ii][:, 1, :], b_=tsl2: e.tensor_tensor(o, a, b_, ALU.mult),
                    reads=[neb[ii], ttabb[hi]], writes=[nptb[ii][1]])

        def na_back(n):
            h, b, wp = steps[n]
            c4 = h // 2
            odd = h % 2
            r0 = 64 * odd
            nvc0 = c4 * 192 + (64 if odd else 0)
            ob = 4 + (h * 8 + b) % 4
            ii = n % 4
            qb_ = [B_naq[c4][b]]
            for j in range(2):
                w = 2 * wp + 1 - j
                s_ = 2 * b + w
                first = (wp == 0 and j == 0)
                last = (wp == 2 and j == 1)
                pe.op(lambda e, o=psum[:, ob, 0:256], a=naV[:, s_, nvc0:nvc0 + 128], r=NPT[ii][:, j, :], st=first, sp_=last:
                      e.matmul(o, a, r, start=st, stop=sp_),
                      reads=[nptb[ii][j], B_nav], writes=[bank[ob]], signal=True)
            if wp == 2:
                dr0 = 64 - r0
                dve.op(lambda e, o=nrd[dr0:dr0 + 64, :], i=psum[dr0:dr0 + 64, ob, 0:256]: e.reciprocal(o, i), reads=[bank[ob]], writes=[nrdb])
                dve.op(lambda e, o=naq[r0:r0 + 64, c4, b * 256:(b + 1) * 256], a=psum[r0:r0 + 64, ob, 0:256], b_=nrd[dr0:dr0 + 64, :]:
                       e.tensor_tensor(o, a, b_, ALU.mult), reads=[bank[ob], nrdb], writes=qb_)

        for n in range(NS_ + LAG):
            if n < NS_:
                na_front(n)
            if n - LAG >= 0:
                na_back(n - LAG)
        pass
        if stop == (l, 'na'):
            return True

        proj_residual(w_out_d[l][0:512, :], 4,
                      lambda c, blk: naq[:, c, blk * 512:(blk + 1) * 512],
                      lambda c, blk: [B_naq[c][2 * blk], B_naq[c][2 * blk + 1]])
        k.barrier()
        if stop == (l, 'mixer'):
            return True

        qm = av(0, [4, T], BF16)
        memT = av(16 * KB, [8, 256], F32)
        memn = av(24 * KB, [8, 256], BF16)
        kmT = av(28 * KB, [4, 256], BF16)
        Vm = av(30 * KB, [2, 512], BF16)
        MT = 32 * KB
        B_memT = Buf()
        B_memn = Buf()
        B_km = Buf()
        B_vm = Buf()
        load_transpose(mem_d, 2, memT, lambda t, half: [B_memT], 40 * KB)
        pass
        rmsnorm_T(memT, 256, nbase + 16, lambda c, blk: memn[:, c, :], MT, lambda c, blk: [B_memT], lambda c, blk: [B_memn], nblk=1)
        norm_x_to_hT(nbase + 8, 56 * KB)
        k.barrier()
        w_kv_l = w_mkv_d[l]
        for hd in range(4):
            wt, wb = wload_cols(w_kv_l, hd * 128)
            for c in range(8):
                pe.op(lambda e, o=PS(hd % 2, 256), a=wt[:, c, :], r=memn[:, c, :], st=(c == 0), sp_=(c == 7):
                      e.matmul(o, a, r, start=st, stop=sp_), reads=[wb, B_memn], writes=[bank[hd % 2]], signal=(c == 7))
            act.op(lambda e, o=kmT[:, hd, :], i=PS(hd % 2, 256): e.copy(o, i), reads=[bank[hd % 2]], writes=[B_km])
        for c4 in range(4):
            wt, wb = wload_cols(w_kv_l, 512 + c4 * 128)
            for mt in range(2):
                pb = 2 + (c4 * 2 + mt) % 2
                for c in range(8):
                    pe.op(lambda e, o=PS(pb, 128), a=memn[:, c, mt * 128:(mt + 1) * 128], r=wt[:, c, :], st=(c == 0), sp_=(c == 7):
                          e.matmul(o, a, r, start=st, stop=sp_), reads=[wb, B_memn], writes=[bank[pb]], signal=(c == 7))
                dve.op(lambda e, o=Vm[:, mt, c4 * 128:(c4 + 1) * 128], i=PS(pb, 128): e.tensor_copy(o, i), reads=[bank[pb]], writes=[B_vm])
        B_qm = [[Buf() for _ in range(4)] for _ in range(4)]
        w_q_l = w_mq_d[l]
        for hd in range(4):
            wt, wb = wload_cols(w_q_l, hd * 128)
            for blk in range(4):
                pb = 4 + blk % 2
                for c in range(8):
                    pe.op(lambda e, o=PS(pb), a=wt[:, c, :], r=hblk(c, blk), st=(c == 0), sp_=(c == 7):
                          e.matmul(o, a, r, start=st, stop=sp_), reads=[wb, B_hT[blk]], writes=[bank[pb]], signal=(c == 7))
                dst = qm[:, hd, blk * 512:(blk + 1) * 512]
                if blk % 2 == 0:
                    act.op(lambda e, o=dst, i=PS(pb): e.copy(o, i), reads=[bank[pb]], writes=[B_qm[hd][blk]])
                else:
                    dve.op(lambda e, o=dst, i=PS(pb): e.tensor_copy(o, i), reads=[bank[pb]], writes=[B_qm[hd][blk]])
        MPT = [av(MT + i * 2 * KB, [2, 512], BF16) for i in range(2)]
        mptb = [Buf(), Buf()]
        mrd = av(MT + 4 * KB, [512], F32)
        mrdb = Buf()
        msc = float(128 ** -0.5)
        it = 0
        for blk in range(4):
            for hd in range(4):
                sbk = 2 * (it % 2)
                for mt in range(2):
                    pe.op(lambda e, o=PS(sbk + mt), a=kmT[:, hd, mt * 128:(mt + 1) * 128], r=qm[:, hd, blk * 512:(blk + 1) * 512]:
                          e.matmul(o, a, r, start=True, stop=True),
                          reads=[B_km, B_qm[hd][blk]], writes=[bank[sbk], bank[sbk + 1]], signal=(mt == 1))
                p = MPT[it % 2]
                pb_ = mptb[it % 2]
                act.op(lambda e, o=p, s_=psum[:, sbk:sbk + 2, :]: e.activation(o, s_, AF.Exp, scale=msc),
                       reads=[bank[sbk], bank[sbk + 1]], writes=[pb_])
                ob = 4 + 2 * (it % 2)
                it += 1
                for mt in range(2):
                    pe.op(lambda e, o=PS(ob), a=Vm[:, mt, hd * 128:(hd + 1) * 128], r=p[:, mt, :], st=(mt == 0), sp_=(mt == 1):
                          e.matmul(o, a, r, start=st, stop=sp_), reads=[pb_, B_vm], writes=[bank[ob]], signal=(mt == 1))
                for mt in range(2):
                    pe.op(lambda e, o=PS(ob + 1), r=p[:, mt, :], st=(mt == 0), sp_=(mt == 1):
                          e.matmul(o, ones128, r, start=st, stop=sp_), reads=[pb_, B_const], writes=[bank[ob + 1]], signal=(mt == 1))
                dve.op(lambda e, i=PS(ob + 1): e.reciprocal(mrd, i), reads=[bank[ob + 1]], writes=[mrdb])
                dve.op(lambda e, o=qm[:, hd, blk * 512:(blk + 1) * 512], a=PS(ob): e.tensor_tensor(o, a, mrd, ALU.mult),
                       reads=[bank[ob], mrdb], writes=[B_qm[hd][blk]])
        pass
        proj_residual(w_mo_d[l], 4, lambda c, blk: qm[:, c, blk * 512:(blk + 1) * 512], lambda c, blk: [B_qm[c][blk]])
        k.barrier()
        if stop == (l, 'mem'):
            return True

        norm_x_to_hT(nbase + 24, 64 * KB)
        hst = av(0, [2, 8], BF16)
        hstb = Buf()
        dve.op(lambda e: e.tensor_copy(hst[:, 0, :], hT[:, :, 1]), reads=[B_hT[0]], writes=[hstb])
        dve.op(lambda e: e.tensor_copy(hst[:, 1, :], hT[:, :, 2048]), reads=[B_hT[3]], writes=[hstb])
        d = sp.dma(e3_in[l][:, :].rearrange("p (a b) -> p a b", a=2), hst, reads=[hstb])
        eb3 = Buf()
        eb3.w = [d]
        e3_done = allgather(e3_in[l], e3_out[l], [eb3])
        hrx = av(64, [2, 8], BF16)
        hrxb = Buf()
        sp.dma(hrx[:, 0, :], e3_out[l][0:128, 8:16], reads=[e3_done], writes=[hrxb])
        sp.dma(hrx[:, 1, :], e3_out[l][128:256, 0:8], reads=[e3_done], writes=[hrxb])
        dve.op(lambda e: e.tensor_scalar(hT[:, :, 0], hrx[:, 0, :], pvc(PV_FLAGL), None, ALU.mult), reads=[hrxb, B_const], writes=[B_hhalo])
        dve.op(lambda e: e.tensor_scalar(hT[:, :, 2049], hrx[:, 1, :], pvc(PV_FLAGR), None, ALU.mult), reads=[hrxb, B_const], writes=[B_hhalo])
        k.barrier()
        if stop == (l, 'ffnhalo'):
            return True

        actT = av(1 * KB, [22, 1024], BF16)
        TG = [av(45 * KB + i * 4 * KB, [1024], F32) for i in range(2)]
        TV = [av(53 * KB + i * 4 * KB, [1024], F32) for i in range(2)]
        UC = [av(61 * KB + i * 4128, [1032], F32) for i in range(2)]
        tgb = [Buf(), Buf()]
        tvb = [Buf(), Buf()]
        ucb = [Buf(), Buf()]
        WD = [av(61 * KB + 8256 + i * 5632, [22, 128], BF16) for i in range(2)]
        wdb = [Buf(), Buf()]
        B_act = [Buf() for _ in range(22)]
        cwb = PV_CW + l * 176
        w_up_l = w_up_d[l]
        w_dn_l = w_dn_d[l]
        UHB = 7
        uhb = [bank[6], bank[7]]
        for hh in range(int(os.environ.get('FFN_HH', '2'))):
            base = 1 + hh * 1024
            uslot = 0
            order_ = [cp_ + 22 * g_ for cp_ in range(int(os.environ.get('FFN_CP', '22'))) for g_ in range(2)]
            loaded_ = {}
            PF_ = 5
            for s0 in range(min(PF_, len(order_))):
                loaded_[s0] = wload_cols(w_up_l, order_[s0] * 128)
            for cp in range(int(os.environ.get('FFN_CP', '22'))):
                for gv_ in range(2):
                    j = cp + 22 * gv_
                    si_ = 2 * cp + gv_
                    if si_ + PF_ < len(order_):
                        loaded_[si_ + PF_] = wload_cols(w_up_l, order_[si_ + PF_] * 128)
                    wt, wb = loaded_.pop(si_)
                    ub = 2 * (uslot % 3)
                    uc = UC[uslot % 2]
                    ucb_ = ucb[uslot % 2]
                    uslot += 1
                    for sb_ in range(2):
                        for c in range(8):
                            pe.op(lambda e, o=PS(ub + sb_), a=wt[:, c, :], r=hT[:, c, base + sb_ * 512:base + (sb_ + 1) * 512],
                                  st=(c == 0), sp_=(c == 7): e.matmul(o, a, r, start=st, stop=sp_),
                                  reads=[wb, B_hT[hh * 2 + sb_]], writes=[bank[ub], bank[ub + 1]], signal=(c == 7 and sb_ == 1))
                    uhi_ = si_ % 2
                    uh = psum[:, 6 + uhi_, 0:2]
                    for c in range(8):
                        pe.op(lambda e, o=uh, a=wt[:, c, :], r=hT[:, c, base - 1:base + 1025:1025], st=(c == 0), sp_=(c == 7):
                              e.matmul(o, a, r, start=st, stop=sp_),
                              reads=[wb, B_hT[0], B_hT[1], B_hT[2], B_hT[3], B_hhalo], writes=[uhb[uhi_]], signal=(c == 7))
                    U = psum[:, ub:ub + 2, :].rearrange("p a q -> p (a q)")
                    tt_ = (TG if gv_ == 0 else TV)[cp % 2]
                    ttb_ = (tgb if gv_ == 0 else tvb)[cp % 2]
                    ub_ = [bank[ub], bank[ub + 1]]
                    act.op(lambda e, o=uc[:, 1:1025], u=U: e.copy(o, u), reads=ub_, writes=[ucb_])
                    act.op(lambda e, o=uc[:, 0:1026:1025], u=uh: e.copy(o, u), reads=[uhb[uhi_]], writes=[ucb_])
                    act.op(lambda e, o=tt_, u=U, s_=pvc(cwb + 44 + j), b_=pvc(cwb + 132 + j):
                           e.activation(o, u, AF.Identity, bias=b_, scale=s_), reads=ub_ + [B_const], writes=[ttb_])
                    dve.op(lambda e, o=tt_, u=uc[:, 0:1024], s_=pvc(cwb + j):
                           e.scalar_tensor_tensor(o, u, s_, o, ALU.mult, ALU.add), reads=[ucb_, ttb_, B_const], writes=[ttb_])
                    dve.op(lambda e, o=tt_, u=uc[:, 2:1026], s_=pvc(cwb + 88 + j):
                           e.scalar_tensor_tensor(o, u, s_, o, ALU.mult, ALU.add), reads=[ucb_, ttb_, B_const], writes=[ttb_])
                act.op(lambda e, o=TG[cp % 2]: e.activation(o, o, AF.Silu), reads=[tgb[cp % 2]], writes=[tgb[cp % 2]])
                pool.op(lambda e, o=actT[:, cp, :], a=TG[cp % 2], b_=TV[cp % 2]: e.tensor_tensor(o, a, b_, ALU.mult),
                        reads=[tgb[cp % 2], tvb[cp % 2]], writes=[B_act[cp]])
            for oc in range(int(os.environ.get('FFN_OC', '8'))):
                wi = oc % 2
                pool.dma(WD[wi], w_dn_l[:, oc * 128:(oc + 1) * 128].rearrange("(kc p) n -> p kc n", p=128), writes=[wdb[wi]])
                for sb_ in range(2):
                    pb = (oc * 2 + sb_) % 6
                    for c in range(22):
                        pe.op(lambda e, o=PS(pb), a=WD[wi][:, c, :], r=actT[:, c, sb_ * 512:(sb_ + 1) * 512], st=(c == 0), sp_=(c == 21):
                              e.matmul(o, a, r, start=st, stop=sp_), reads=[wdb[wi], B_act[c]], writes=[bank[pb]], signal=(c == 21))
                    blk = hh * 2 + sb_
                    xs = xT[:, oc, blk * 512:(blk + 1) * 512]
                    dve.op(lambda e, o=xs, p_=PS(pb): e.tensor_tensor(o, o, p_, ALU.add), reads=[bank[pb], B_xT[oc][blk]], writes=[B_xT[oc][blk]])
        k.barrier()
        return False

    if stop != 'load':
        for l in range(nlayers):
            if layer_body(l):
                break

    yT = av(0, [8, 512], F32)
    yT2 = [av(i * 16 * KB, [8, 512], F32) for i in range(2)]
    ytb = [Buf(), Buf()]
    ost = [av(32 * KB + i * 4 * KB, [1024], F32) for i in range(4)]
    ostb = [Buf() for _ in range(4)]
    out_deps = []
    cnt = {"n": 0}

    def emit_out(blk, src_of, src_b):
        for tt in range(4):
            t = blk * 4 + tt
            oi = t % 4
            for half in range(2):
                pb = cnt["n"] % 8
                cnt["n"] += 1
                for j in range(4):
                    c = half * 4 + j
                    pe.op(lambda e, o=psum[:, pb, j * 128:(j + 1) * 128], i=src_of(c, blk, tt): e.transpose(o, i, ident),
                          reads=src_b(c, blk) + [B_const], writes=[bank[pb]], signal=(j == 3))
                dst = ost[oi][:, half * 512:(half + 1) * 512]
                if half == 0:
                    dve.op(lambda e, o=dst, i=PS(pb): e.tensor_copy(o, i), reads=[bank[pb]], writes=[ostb[oi]])
                else:
                    act.op(lambda e, o=dst, i=PS(pb): e.copy(o, i), reads=[bank[pb]], writes=[ostb[oi]])
            out_deps.append(sp.dma(out_d[t * 128:(t + 1) * 128, :], ost[oi], reads=[ostb[oi]]))

    if final_norm:
        rmsnorm_T(xT, T, PV_FINAL, lambda c, blk: yT2[blk % 2][:, c, :], 56 * KB,
                  lambda c, blk: [B_xT[c][blk]], lambda c, blk: [ytb[blk % 2]],
                  after_blk=lambda blk: emit_out(blk, lambda c, b_, tt: yT2[b_ % 2][:, c, tt * 128:(tt + 1) * 128],
                                                 lambda c, b_: [ytb[b_ % 2]]))
    else:
        for blk in range(4):
            emit_out(blk, lambda c, b_, tt: xT[:, c, b_ * 512 + tt * 128:b_ * 512 + (tt + 1) * 128],
                     lambda c, b_: [B_xT[c][b_]])
    for d in out_deps:
        sp._wait(d)
    k.barrier()

    if os.environ.get('SEMDBG'):
        print('SEMS', [(e.name, e.sem.cnt, len(e.items)) for e in k.engs], [(e.name, [d.cnt for d in e.dsems]) for e in k.engs])
    with nc.Block() as block:
        @block.sync
        def _(e):
            sp.replay(e)

        @block.gpsimd
        def _(e):
            pool.replay(e)

        @block.tensor
        def _(e):
            pe.replay(e)

        @block.vector
        def _(e):
            dve.replay(e)

        @block.scalar
        def _(e):
            act.replay(e)
    stack.close()
    return nc


def _consts(p):
    ident = np.eye(128, dtype=np.float32)
    R = np.zeros((128, 128), np.float32)
    for blk in range(2):
        for a in range(2):
            o = blk * 64 + a * 32
            for f in range(16):
                R[o + f, o + 16 + f] = -1.0
                R[o + 16 + f, o + f] = 1.0
    rmatT = np.ascontiguousarray(R.T)
    t = np.arange(T) + p * T
    pos = np.stack([t // 64, t % 64], 0).astype(np.float32)
    inv = (10000.0 ** (-np.arange(16, dtype=np.float32) / 16)).astype(np.float32)
    cosT = np.zeros((128, T), np.float32)
    sinT = np.zeros((128, T), np.float32)
    for d in range(128):
        dd = d % 64
        a = dd // 32
        f = dd % 16
        ang = (pos[a] * inv[f]).astype(np.float32)
        cosT[d] = np.cos(ang)
        sinT[d] = np.sin(ang)
    kr = (np.arange(128) // 64)[:, None, None]
    kc = (np.arange(128) % 64)[:, None, None]
    idx = np.arange(14)[None, :, None]
    qc = np.arange(64)[None, None, :]
    dr = 6 + kr - idx + 0 * qc
    cs = np.clip(qc - 8, 0, 48)
    colvalid = (kc >= cs) & (kc < cs + 16)
    band = (dr >= -4) & (dr <= 3)
    full = colvalid & (dr >= -7) & (dr <= 7)
    bandm = colvalid & band
    mA = full if p == 0 else bandm
    mC = bandm if p == 0 else full
    namask = np.stack([mA, bandm, mC], 0).astype(np.float32).reshape(3, 128, 896)
    return ident, rmatT, cosT, sinT, namask


def _na_gather(na_rpb):
    kr = (np.arange(128) // 64)[:, None, None]
    kc = (np.arange(128) % 64)[:, None, None]
    idx = np.arange(14)[None, :, None]
    qc = np.arange(64)[None, None, :]
    dr = np.clip(6 + kr - idx + 0 * qc, -7, 7) + 7
    dc = np.clip(kc - qc + 0 * idx, -15, 15) + 15
    g = na_rpb[:, :, dr, dc]
    return np.ascontiguousarray(g.reshape(L, 8, 128, 896)).astype(np.float32)


def _pv(p, inp):
    pv = np.zeros((128, PV_N), np.float32)

    def col8(v):
        return np.asarray(v, np.float32).reshape(8, 128).T

    for l in range(L):
        b = PV_NORM + l * 32
        pv[:, b + 0:b + 8] = col8(inp["norm_mix"][l])
        pv[:, b + 8:b + 16] = col8(inp["norm_mem_q"][l])
        pv[:, b + 16:b + 24] = col8(inp["norm_mem_kv"][l])
        pv[:, b + 24:b + 32] = col8(inp["norm_ffn"][l])
        pv[:, PV_GQ + l] = np.tile(np.asarray(inp["gqa_q_norm"][l], np.float32), 2)
        pv[:, PV_GK + l] = np.tile(np.asarray(inp["gqa_k_norm"][l], np.float32), 2)
        cw = np.asarray(inp["conv_w"][l], np.float32)
        cb = np.asarray(inp["conv_b"][l], np.float32)
        o = PV_CW + l * 176
        for kk in range(3):
            pv[:, o + kk * 44:o + (kk + 1) * 44] = cw[kk].reshape(44, 128).T
        pv[:, o + 132:o + 176] = cb.reshape(44, 128).T
    pv[:, PV_FINAL:PV_FINAL + 8] = col8(inp["norm_final"])
    pv[:, PV_FLAGL] = 1.0 if p == 1 else 0.0
    pv[:, PV_FLAGR] = 1.0 if p == 0 else 0.0
    return pv


_CACHE = {}


def kernel(**inputs):
    inp = {k_: np.asarray(v) for k_, v in inputs.items()}
    if "nc" not in _CACHE:
        _CACHE["nc"] = build_nc()
    nc = _CACHE["nc"]
    nag = _na_gather(np.asarray(inp["na_rpb"], np.float32))
    shared = {n_: np.ascontiguousarray(inp[n_], dtype=np.float32) for n_ in
              ("w_in", "w_out", "w_mem_q", "w_mem_kv", "w_mem_o", "w_up", "w_down")}
    in_maps = []
    for c in range(8):
        b, p = c // 2, c % 2
        ident, rmatT, cosT, sinT, namask = _consts(p)
        m = dict(shared)
        m["x"] = np.ascontiguousarray(inp["x"][b, p * T:(p + 1) * T, :], dtype=np.float32)
        m["mem"] = np.ascontiguousarray(inp["mem"][b], dtype=np.float32)
        m["pv"] = _pv(p, inp)
        m["ident"] = ident
        m["rmatT"] = rmatT
        m["cosT"] = cosT
        m["sinT"] = sinT
        m["nag"] = nag
        m["namask"] = namask
        in_maps.append(m)
    res = run_bass_kernel_spmd(nc, in_maps, core_ids=list(range(8)))
    out = np.zeros((4, 4096, D), np.float32)
    for c in range(8):
        b, p = c // 2, c % 2
        out[b, p * T:(p + 1) * T, :] = np.asarray(res.results[c]["out"], dtype=np.float32)
    return out
```

```python
import numpy as np
import concourse.bass as bass
import concourse.mybir as mybir
from concourse.bass_utils import run_bass_kernel_spmd

F32 = mybir.dt.float32
BF16 = mybir.dt.bfloat16
AF = mybir.ActivationFunctionType
ALU = mybir.AluOpType

L = 2
D = 1024
T = 2048
NCH = 8
DFF = 2816
NJ = 44
EPS = 1e-6
PAIRS = [[0, 1], [2, 3], [4, 5], [6, 7]]

PV_NORM = 0
PV_FINAL = 64
PV_GQ = 72
PV_GK = 74
PV_FLAGL = 76
PV_FLAGR = 77
PV_CW = 80
PV_N = 80 + 2 * 176


class Buf:
    __slots__ = ("w", "r")

    def __init__(self):
        self.w = []
        self.r = []


class Sem:
    def __init__(self, h):
        self.h = h
        self.cnt = 0


class Eng:
    def __init__(self, K, name):
        self.K = K
        self.name = name
        self.sem = K.new_sem(name)
        self.items = []
        self.seen = {}
        self.pend_r = []
        self.pend_w = []
        self.dsems = []
        self.dsi = 0

    def _wait(self, dep):
        s, v = dep
        if self.seen.get(s, 0) >= v:
            return
        self.seen[s] = v
        self.items.append(("w", s.h, v))

    def _deps(self, reads, writes):
        for b in reads:
            for d in b.w:
                self._wait(d)
        for b in writes:
            for d in b.w:
                self._wait(d)
            for d in b.r:
                self._wait(d)

    def _register(self, dep, reads, writes):
        for b in reads:
            b.r.append(dep)
            if len(b.r) > 12:
                m = {}
                for s, v in b.r:
                    if m.get(s, 0) < v:
                        m[s] = v
                b.r = list(m.items())
        for b in writes:
            b.w = [dep]
            b.r = []

    def op(self, fn, reads=(), writes=(), signal=True):
        self._deps(reads, writes)
        if signal:
            self.sem.cnt += 1
            dep = (self.sem, self.sem.cnt)
            self.items.append(("o", fn, self.sem.h, 1))
            self._register(dep, list(reads) + self.pend_r, list(writes) + self.pend_w)
            self.pend_r = []
            self.pend_w = []
            return dep
        self.items.append(("o", fn, None, 0))
        self.pend_r += list(reads)
        self.pend_w += list(writes)
        return None

    def dma(self, out, in_, reads=(), writes=()):
        self._deps(reads, writes)
        if not self.dsems:
            self.dsems = [self.K.new_sem(self.name + "_d%d" % i) for i in range(6)]
        s = self.dsems[self.dsi % len(self.dsems)]
        self.dsi += 1
        s.cnt += 16
        dep = (s, s.cnt)
        self.items.append(("o", lambda e, o=out, i=in_: e.dma_start(out=o, in_=i), s.h, 16))
        self._register(dep, reads, writes)
        return dep

    def replay(self, e):
        for it in self.items:
            if it[0] == "w":
                e.wait_ge(it[1], it[2])
            else:
                ins = it[1](e)
                if it[2] is not None:
                    ins.then_inc(it[2], it[3])


class K:
    def __init__(self, nc, stack):
        self.nc = nc
        self.stack = stack
        self.sems = []
        self.pe = Eng(self, "pe")
        self.act = Eng(self, "act")
        self.dve = Eng(self, "dve")
        self.pool = Eng(self, "pool")
        self.sp = Eng(self, "sp")
        self.engs = [self.pe, self.act, self.dve, self.pool, self.sp]

    def new_sem(self, name):
        h = self.stack.enter_context(self.nc.semaphore(name))
        s = Sem(h)
        self.sems.append(s)
        return s

    def barrier(self):
        for e in self.engs:
            if e.pend_r or e.pend_w:
                raise RuntimeError("pending unsignaled ops at barrier on " + e.name)
        for e in self.engs:
            if e is self.pool:
                continue
            for s in self.sems:
                if s.cnt > 0:
                    e._wait((s, s.cnt))


from contextlib import ExitStack


def build_nc(nlayers=L, final_norm=True, stop=None):
    nc = bass.Bass("TRN2", target_bir_lowering=False)
    stack = ExitStack()
    k = K(nc, stack)
    pe, act, dve, pool, sp = k.pe, k.act, k.dve, k.pool, k.sp

    def din(name, shape, dt=F32):
        return nc.dram_tensor(name, list(shape), dt, kind="ExternalInput").ap()

    x_d = din("x", [T, D])
    mem_d = din("mem", [256, D])
    w_in_d = din("w_in", [L, D, 2304])
    w_out_d = din("w_out", [L, D, D])
    w_mq_d = din("w_mem_q", [L, D, 512])
    w_mkv_d = din("w_mem_kv", [L, D, 1024])
    w_mo_d = din("w_mem_o", [L, 512, D])
    w_up_d = din("w_up", [L, D, 2 * DFF])
    w_dn_d = din("w_down", [L, DFF, D])
    pv_d = din("pv", [128, PV_N])
    ident_d = din("ident", [128, 128])
    rt_d = din("rmatT", [128, 128])
    cos_d = din("cosT", [128, T])
    sin_d = din("sinT", [128, T])
    nag_d = din("nag", [L, 8, 128, 896])
    namask_d = din("namask", [3, 128, 896])
    out_d = nc.dram_tensor("out", [T, D], F32, kind="ExternalOutput").ap()

    e1_in = [nc.dram_tensor("e1in%d" % l, [384, 2048], BF16) for l in range(L)]
    e1_out = [nc.dram_tensor("e1out%d" % l, [768, 2048], BF16) for l in range(L)]
    e2_in = [nc.dram_tensor("e2in%d" % l, [512, 1536], BF16) for l in range(L)]
    e2_out = [nc.dram_tensor("e2out%d" % l, [1024, 1536], BF16) for l in range(L)]
    e3_in = [nc.dram_tensor("e3in%d" % l, [128, 16], BF16) for l in range(L)]
    e3_out = [nc.dram_tensor("e3out%d" % l, [256, 16], BF16) for l in range(L)]
    cc_sems = [k.new_sem("cc%d" % i) for i in range(3 * L)]

    import os
    ARENA_B = int(os.environ.get('ARENA_KB', '196')) * 1024
    arena = stack.enter_context(nc.sbuf_tensor("arena", [128, ARENA_B // 2], BF16))
    psum = stack.enter_context(nc.psum_tensor("psum", [128, 8, 512], F32))

    def view(off_bytes, shape, dt):
        esz = 4 if dt == F32 else 2
        n = int(np.prod(shape))
        assert off_bytes % 4 == 0 and off_bytes + n * esz <= ARENA_B, (off_bytes, shape)
        ap = arena[:, off_bytes // 2: off_bytes // 2 + n * esz // 2]
        if dt == F32:
            ap = ap.bitcast(F32)
        if len(shape) == 2:
            return ap.rearrange("p (a b) -> p a b", a=shape[0])
        if len(shape) == 3:
            return ap.rearrange("p (a b c) -> p a b c", a=shape[0], b=shape[1])
        return ap

    KB = 1024
    off = 0

    def take(nbytes):
        nonlocal off
        o = off
        off += (nbytes + 31) // 32 * 32
        return o

    xT = view(take(64 * KB), [8, T], F32)
    hT = view(take(8 * 2050 * 2), [8, 2050], BF16)
    HT_OFF = off - (8 * 2050 * 2 + 31) // 32 * 32
    NW = 8
    wring = [view(take(2 * KB), [8, 128], BF16) for _ in range(NW)]
    wbuf = [Buf() for _ in range(NW)]
    ident = view(take(512), [128], F32)
    onesblk = view(take(256), [128], BF16)
    ones128 = view(take(256), [128], BF16)
    rmt = view(take(256), [128], BF16)
    pv = view(take(PV_N * 4), [PV_N], F32)
    A0 = off
    assert stop == 'load' or ARENA_B - A0 >= 80 * KB, (ARENA_B - A0)

    def av(o, shape, dt):
        return view(A0 + o, shape, dt)

    def hv(o, shape, dt):
        return view(HT_OFF + o, shape, dt)

    bank = [Buf() for _ in range(8)]

    def PS(b, n=512):
        return psum[:, b, 0:n]

    def PS2(b):
        return psum[:, b:b + 2, :]

    B_xT = [[Buf() for _ in range(4)] for _ in range(8)]
    B_hT = [Buf() for _ in range(4)]
    B_hhalo = Buf()
    B_const = Buf()

    wstate = {"i": 0}

    def wload(src_ap):
        i = wstate["i"] % NW
        wstate["i"] += 1
        pool.dma(wring[i], src_ap, writes=[wbuf[i]])
        return wring[i], wbuf[i]

    def wload_cols(w_l, c0, ncols=128):
        src = w_l[:, c0:c0 + ncols].rearrange("(kc p) n -> p kc n", p=128)
        i = wstate["i"] % NW
        wstate["i"] += 1
        dst = wring[i][:, :, 0:ncols]
        pool.dma(dst, src, writes=[wbuf[i]])
        return wring[i], wbuf[i]

    sp.dma(ident, ident_d, writes=[B_const])
    sp.dma(pv, pv_d, writes=[B_const])
    pool.dma(rmt, rt_d, writes=[B_const])
    dve.op(lambda e: e.memset(ones128, 1.0), writes=[B_const])
    dve.op(lambda e: e.memset(onesblk, 0.0), writes=[B_const])
    dve.op(lambda e: e.memset(onesblk[0:64, 0:64], 1.0), writes=[B_const])
    dve.op(lambda e: e.memset(onesblk[64:128, 64:128], 1.0), writes=[B_const])
    dve.op(lambda e: e.memset(hT[:, :, 0:1], 0.0), writes=[B_hhalo])
    dve.op(lambda e: e.memset(hT[:, :, 2049:2050], 0.0), writes=[B_hhalo])
    k.barrier()

    def pvc(c, n=1):
        return pv[:, c:c + n]

    def load_transpose(src_d, ntiles, dstT, dstbufs, stage_off):
        stg = [av(stage_off + i * 4 * KB, [1024], F32) for i in range(2)]
        sb = [Buf(), Buf()]
        for t in range(ntiles):
            s = stg[t % 2]
            sp.dma(s, src_d[t * 128:(t + 1) * 128, :], writes=[sb[t % 2]])
            for half in range(2):
                b = (2 * t + half) % 8
                for j in range(4):
                    c = half * 4 + j
                    pe.op(lambda e, o=psum[:, b, j * 128:(j + 1) * 128], i=s[:, c * 128:(c + 1) * 128]:
                          e.transpose(o, i, ident),
                          reads=[sb[t % 2], B_const], writes=[bank[b]], signal=(j == 3))
                wb = dstbufs(t, half)
                dst = dstT[:, half * 4:half * 4 + 4, t * 128:(t + 1) * 128]
                src = psum[:, b, :].rearrange("p (a q) -> p a q", a=4)
                eng = dve if half == 0 else act
                if eng is dve:
                    dve.op(lambda e, o=dst, i=src: e.tensor_copy(o, i), reads=[bank[b]], writes=wb)
                else:
                    act.op(lambda e, o=dst, i=src: e.copy(o, i), reads=[bank[b]], writes=wb)

    load_transpose(x_d, 16, xT, lambda t, half: [B_xT[half * 4 + j][t // 4] for j in range(4)], 0)
    k.barrier()

    def rmsnorm_T(src, ncols, gcol, dst_fn, tmp_off, src_bufs, dst_bufs, nblk=None, after_blk=None):
        sq = [av(tmp_off + i * KB, [512], BF16) for i in range(4)]
        sqb = [Buf() for _ in range(4)]
        rs = [av(tmp_off + 4 * KB + i * 2 * KB, [512], F32) for i in range(2)]
        rsb = [Buf(), Buf()]
        nb = ncols // 512 if nblk is None else nblk
        w = min(512, ncols)
        for blk in range(nb):
            pb = blk % 2
            for c in range(8):
                i = (blk * 8 + c) % 4
                act.op(lambda e, o=sq[i][:, 0:w], s=src[:, c, blk * w:(blk + 1) * w]: e.activation(o, s, AF.Square),
                       reads=src_bufs(c, blk), writes=[sqb[i]])
                pe.op(lambda e, o=PS(pb, w), r=sq[i][:, 0:w], st=(c == 0), sp_=(c == 7):
                      e.matmul(o, ones128, r, start=st, stop=sp_),
                      reads=[sqb[i], B_const], writes=[bank[pb]], signal=(c == 7))
            r = rs[blk % 2]
            act.op(lambda e, o=r[:, 0:w], s=PS(pb, w): e.activation(o, s, AF.Ln, bias=EPS, scale=1.0 / D),
                   reads=[bank[pb]], writes=[rsb[blk % 2]])
            act.op(lambda e, o=r[:, 0:w]: e.activation(o, o, AF.Exp, scale=-0.5),
                   reads=[rsb[blk % 2]], writes=[rsb[blk % 2]])
            for c in range(8):
                dve.op(lambda e, o=dst_fn(c, blk), s=src[:, c, blk * w:(blk + 1) * w], g=pvc(gcol + c), rr=r[:, 0:w]:
                       e.scalar_tensor_tensor(o, s, g, rr, ALU.mult, ALU.mult),
                       reads=src_bufs(c, blk) + [rsb[blk % 2], B_const], writes=dst_bufs(c, blk))
            if after_blk is not None:
                after_blk(blk)

    def norm_x_to_hT(gcol, tmp_off):
        rmsnorm_T(xT, T, gcol, lambda c, blk: hT[:, c, 1 + blk * 512:1 + (blk + 1) * 512], tmp_off,
                  lambda c, blk: [B_xT[c][blk]], lambda c, blk: [B_hT[blk]])

    def hblk(c, blk):
        return hT[:, c, 1 + blk * 512:1 + (blk + 1) * 512]

    cci = {"i": 0}

    def allgather(src_t, dst_t, reads):
        s = cc_sems[cci["i"]]
        cci["i"] += 1
        pool._deps(reads, [])
        s.cnt += 1
        pool.items.append(("o", lambda e: e.collective_compute(
            "AllGather", ALU.bypass, replica_groups=PAIRS,
            ins=[src_t.ap().opt()], outs=[dst_t.ap().opt()]), s.h, 1))
        b = Buf()
        b.w = [(s, 1)]
        return b

    def proj_residual(w_l, nk, rhs_fn, rhs_bufs):
        for oc in range(8):
            i = wstate["i"] % NW
            wstate["i"] += 1
            pool.dma(wring[i][:, 0:nk, :], w_l[:, oc * 128:(oc + 1) * 128].rearrange("(kc p) n -> p kc n", p=128), writes=[wbuf[i]])
            wt, wb = wring[i], wbuf[i]
            for blk in range(4):
                pb = (oc * 4 + blk) % 4
                for c in range(nk):
                    pe.op(lambda e, o=PS(pb), a=wt[:, c, :], r=rhs_fn(c, blk), st=(c == 0), sp_=(c == nk - 1):
                          e.matmul(o, a, r, start=st, stop=sp_),
                          reads=[wb] + rhs_bufs(c, blk), writes=[bank[pb]], signal=(c == nk - 1))
                xs = xT[:, oc, blk * 512:(blk + 1) * 512]
                dve.op(lambda e, o=xs, p_=PS(pb): e.tensor_tensor(o, o, p_, ALU.add), reads=[bank[pb], B_xT[oc][blk]], writes=[B_xT[oc][blk]])


    def layer_body(l):
        nbase = PV_NORM + l * 32
        w_in_l = w_in_d[l]
        norm_x_to_hT(nbase + 0, 64 * KB)
        k.barrier()
        if stop == (l, 'norm'):
            return True

        qT = av(0, [4, T], BF16)
        ropeC = av(16 * KB, [T], F32)
        ropeS = av(24 * KB, [T], F32)
        kfull = av(16 * KB, [2, 4096], BF16)
        vfull = av(32 * KB, [32, 320], BF16)
        TMP = 52 * KB
        B_rope = Buf()
        sp.dma(ropeC, cos_d, writes=[B_rope])
        sp.dma(ropeS, sin_d, writes=[B_rope])
        B_qT = [[Buf() for _ in range(4)] for _ in range(4)]
        nr_tb = [{n_: Buf() for n_ in ('sq', 'gv', 'rs', 'aa', 'bb')} for _ in range(2)]

        def normrope(ps_b, gcol, dst_ap, dst_bufs, blk, tmpo, ti):
            sl = ti % 2
            o = tmpo + sl * 8 * KB
            sqv = av(o, [512], BF16)
            gv = av(o + KB, [512], BF16)
            rsv = av(o + 2 * KB, [512], F32)
            aa = av(o + 4 * KB, [512], F32)
            bb = av(o + 6 * KB, [512], F32)
            B = nr_tb[sl]
            b2 = 4 + 2 * sl
            b3 = b2 + 1
            act.op(lambda e: e.activation(sqv, PS(ps_b), AF.Square), reads=[bank[ps_b]], writes=[B["sq"]])
            act.op(lambda e: e.activation(gv, PS(ps_b), AF.Identity, scale=pvc(gcol)), reads=[bank[ps_b], B_const], writes=[B["gv"]])
            pe.op(lambda e: e.matmul(PS(b2), onesblk, sqv, start=True, stop=True), reads=[B["sq"], B_const], writes=[bank[b2]])
            pe.op(lambda e: e.matmul(PS(b3), rmt, gv, start=True, stop=True), reads=[B["gv"], B_const], writes=[bank[b3]])
            act.op(lambda e: e.activation(rsv, PS(b2), AF.Ln, bias=EPS, scale=1.0 / 64), reads=[bank[b2]], writes=[B["rs"]])
            act.op(lambda e: e.activation(rsv, rsv, AF.Exp, scale=-0.5), reads=[B["rs"]], writes=[B["rs"]])
            cs = ropeC[:, blk * 512:(blk + 1) * 512]
            sn = ropeS[:, blk * 512:(blk + 1) * 512]
            dve.op(lambda e: e.tensor_tensor(aa, gv, cs, ALU.mult), reads=[B["gv"], B_rope], writes=[B["aa"]])
            dve.op(lambda e: e.tensor_tensor(bb, PS(b3), sn, ALU.mult), reads=[bank[b3], B_rope], writes=[B["bb"]])
            dve.op(lambda e: e.tensor_tensor(aa, aa, bb, ALU.add), reads=[B["aa"], B["bb"]], writes=[B["aa"]])
            dve.op(lambda e: e.tensor_tensor(dst_ap, aa, rsv, ALU.mult), reads=[B["aa"], B["rs"]], writes=dst_bufs)

        kst = [av(TMP + 16 * KB + i * KB, [512], BF16) for i in range(2)]
        kstb = [Buf(), Buf()]
        e1_deps = []
        ti = 0
        for g in range(2):
            i = wstate["i"] % NW
            wstate["i"] += 1
            for hf in range(2):
                pool.dma(wring[i][:, :, hf * 64:(hf + 1) * 64],
                         w_in_l[:, 2048 + g * 64:2048 + (g + 1) * 64].rearrange("(kc p) n -> p kc n", p=128),
                         writes=[wbuf[i]])
            wt, wb = wring[i], wbuf[i]
            for blk in range(4):
                pb = blk % 2
                for c in range(8):
                    pe.op(lambda e, o=PS(pb), a=wt[:, c, :], r=hblk(c, blk), st=(c == 0), sp_=(c == 7):
                          e.matmul(o, a, r, start=st, stop=sp_),
                          reads=[wb, B_hT[blk]], writes=[bank[pb]], signal=(c == 7))
                si = ti % 2
                normrope(pb, PV_GK + l, kst[si], [kstb[si]], blk, TMP, ti)
                ti += 1
                d = sp.dma(e1_in[l][g * 128:(g + 1) * 128, blk * 512:(blk + 1) * 512], kst[si], reads=[kstb[si]])
                e1_deps.append(d)
        wt, wb = wload_cols(w_in_l, 2176)
        vst = av(TMP + 18 * KB, [16, 128], BF16)
        vstb = Buf()
        for t4 in range(4):
            pb = 2 + t4 % 2
            for tt in range(4):
                t = t4 * 4 + tt
                for c in range(8):
                    pe.op(lambda e, o=psum[:, pb, tt * 128:(tt + 1) * 128], a=hT[:, c, 1 + t * 128:1 + (t + 1) * 128], r=wt[:, c, :],
                          st=(c == 0), sp_=(c == 7): e.matmul(o, a, r, start=st, stop=sp_),
                          reads=[wb, B_hT[t // 4]], writes=[bank[pb]], signal=(c == 7 and tt == 3))
            act.op(lambda e, o=vst[:, t4 * 4:(t4 + 1) * 4, :], i=psum[:, pb, :].rearrange("p (a q) -> p a q", a=4): e.copy(o, i),
                   reads=[bank[pb]], writes=[vstb])
        vdst = e1_in[l][256:384, :].rearrange("p (t n) -> p t n", t=16)
        d = sp.dma(vdst, vst, reads=[vstb])
        e1_deps.append(d)
        eb = Buf()
        eb.w = e1_deps
        e1_done = allgather(e1_in[l], e1_out[l], [eb])

        for c4 in range(4):
            wt, wb = wload_cols(w_in_l, 1536 + c4 * 128)
            for blk in range(4):
                pb = blk % 2
                for c in range(8):
                    pe.op(lambda e, o=PS(pb), a=wt[:, c, :], r=hblk(c, blk), st=(c == 0), sp_=(c == 7):
                          e.matmul(o, a, r, start=st, stop=sp_),
                          reads=[wb, B_hT[blk]], writes=[bank[pb]], signal=(c == 7))
                normrope(pb, PV_GQ + l, qT[:, c4, blk * 512:(blk + 1) * 512], [B_qT[c4][blk]], blk, TMP, ti)
                ti += 1
        k.barrier()
        if stop == (l, 'gqaproj'):
            return True

        B_kf = Buf()
        B_vf = Buf()
        vf5 = vfull.rearrange("p t (s d) -> p t s d", d=64)
        dve.op(lambda e: e.memset(vf5[:, :, 0:5:2, :], 1.0), writes=[B_vf])
        for r in range(2):
            for g in range(2):
                sp.dma(kfull[:, g, r * 2048:(r + 1) * 2048], e1_out[l][384 * r + 128 * g:384 * r + 128 * (g + 1), :],
                       reads=[e1_done], writes=[B_kf])
            vsrc = e1_out[l][384 * r + 256:384 * r + 384, :].rearrange("p (t n) -> p t n", t=16)
            sp.dma(vfull[:, r * 16:(r + 1) * 16, 64:128], vsrc[:, :, 0:64], reads=[e1_done], writes=[B_vf])
            sp.dma(vfull[:, r * 16:(r + 1) * 16, 192:256], vsrc[:, :, 64:128], reads=[e1_done], writes=[B_vf])

        PT = [av(TMP + i * 2 * KB, [1024], BF16) for i in range(2)]
        ptb = [Buf(), Buf()]
        rden = av(TMP + 4 * KB, [1024], F32)
        rdb = Buf()
        it = 0
        rden2 = av(TMP + 4 * KB, [512], F32)
        for hp in range(4):
            g = hp // 2
            c4 = hp
            for qb in range(4):
                oe = 4 + 2 * ((hp * 4 + qb) % 2)
                qbufs = [B_qT[c4][qb]]
                qs = slice(qb * 512, (qb + 1) * 512)

                def s_mm(kt, sb):
                    for odd in range(2):
                        r0 = 64 * odd
                        pe.op(lambda e, o=PS(sb + odd), a=kfull[r0:r0 + 64, g, kt * 128:(kt + 1) * 128], r=qT[r0:r0 + 64, c4, qs]:
                              e.matmul(o, a, r, start=True, stop=True),
                              reads=[B_kf] + qbufs, writes=[bank[sb], bank[sb + 1]], signal=(odd == 1))

                s_mm(0, 0)
                for kt in range(32):
                    sb = 2 * (kt % 2)
                    if kt + 1 < 32:
                        s_mm(kt + 1, 2 * ((kt + 1) % 2))
                    p = PT[it % 2]
                    pbf = ptb[it % 2]
                    it += 1
                    act.op(lambda e, o=p, s_=psum[:, sb:sb + 2, :].rearrange("p a q -> p (a q)"): e.activation(o, s_, AF.Exp, scale=0.125),
                           reads=[bank[sb], bank[sb + 1]], writes=[pbf])
                    for odd in range(2):
                        vc0 = (64 if not odd else 0) + 128 * g
                        pe.op(lambda e, o=PS(oe + odd), a=vfull[:, kt, vc0:vc0 + 128], r=p[:, odd * 512:(odd + 1) * 512], st=(kt == 0), sp_=(kt == 31):
                              e.matmul(o, a, r, start=st, stop=sp_),
                              reads=[pbf, B_vf], writes=[bank[oe], bank[oe + 1]], signal=(odd == 1))
                obs = [bank[oe], bank[oe + 1]]
                dve.op(lambda e, o=rden2[64:128, :], i=psum[64:128, oe, :]: e.reciprocal(o, i), reads=obs, writes=[rdb], signal=False)
                dve.op(lambda e, o=rden2[0:64, :], i=psum[0:64, oe + 1, :]: e.reciprocal(o, i), reads=obs, writes=[rdb])
                dve.op(lambda e, o=qT[0:64, c4, qs], a=psum[0:64, oe, :], b=rden2[64:128, :]:
                       e.tensor_tensor(o, a, b, ALU.mult), reads=obs + [rdb], writes=qbufs, signal=False)
                dve.op(lambda e, o=qT[64:128, c4, qs], a=psum[64:128, oe + 1, :], b=rden2[0:64, :]:
                       e.tensor_tensor(o, a, b, ALU.mult), reads=obs + [rdb], writes=qbufs)
        pass
        proj_residual(w_out_d[l][512:1024, :], 4,
                      lambda c, blk: qT[:, c, blk * 512:(blk + 1) * 512],
                      lambda c, blk: [B_qT[c][blk]])
        k.barrier()
        if stop == (l, 'gqa'):
            return True

        naq = av(0, [4, T], BF16)
        nakT = av(16 * KB, [4, 2560], BF16)
        naV = av(36 * KB, [20, 768], BF16)
        nav5 = naV.rearrange("p t (c s d) -> p t c s d", c=4, s=3)
        NTMP = 66 * KB
        B_naq = [[Buf() for _ in range(8)] for _ in range(4)]
        B_nak = Buf()
        B_nav = Buf()
        def na_qk_proj(which, base_c, dstT, coff):
            for c4 in range(4):
                wt, wb = wload_cols(w_in_l, base_c + c4 * 128)
                for blk in range(4):
                    pb = blk % 2
                    for c in range(8):
                        pe.op(lambda e, o=PS(pb), a=wt[:, c, :], r=hblk(c, blk), st=(c == 0), sp_=(c == 7):
                              e.matmul(o, a, r, start=st, stop=sp_),
                              reads=[wb, B_hT[blk]], writes=[bank[pb]], signal=(c == 7))
                    dst = dstT[:, c4, coff + blk * 512:coff + (blk + 1) * 512]
                    wbs = [B_naq[c4][2 * blk], B_naq[c4][2 * blk + 1]] if which == 0 else [B_nak]
                    if (c4 + blk) % 2 == 0:
                        act.op(lambda e, o=dst, i=PS(pb): e.copy(o, i), reads=[bank[pb]], writes=wbs)
                    else:
                        dve.op(lambda e, o=dst, i=PS(pb): e.tensor_copy(o, i), reads=[bank[pb]], writes=wbs)
        na_qk_proj(1, 512, nakT, 256)
        wv = []
        for c4 in range(4):
            wv.append(wload_cols(w_in_l, 1024 + c4 * 128))
        for t in range(16):
            pb = 2 + t % 2
            for c4 in range(4):
                wt, wb = wv[c4]
                for c in range(8):
                    pe.op(lambda e, o=psum[:, pb, c4 * 128:(c4 + 1) * 128], a=hT[:, c, 1 + t * 128:1 + (t + 1) * 128], r=wt[:, c, :],
                          st=(c == 0), sp_=(c == 7): e.matmul(o, a, r, start=st, stop=sp_),
                          reads=[wb, B_hT[t // 4]], writes=[bank[pb]], signal=(c == 7 and c4 == 3))
            vdst_ = nav5[:, 2 + t, :, 0:3:2, :]
            vsrc_ = psum[:, pb, :].rearrange("p (c e d) -> p c e d", c=4, e=2)
            if t % 2 == 0:
                act.op(lambda e, o=vdst_, i=vsrc_: e.copy(o, i), reads=[bank[pb]], writes=[B_nav])
            else:
                dve.op(lambda e, o=vdst_, i=vsrc_: e.tensor_copy(o, i), reads=[bank[pb]], writes=[B_nav])
        dve.op(lambda e: e.memset(nav5[:, 2:18, :, 1, :], 1.0), writes=[B_nav])
        e2d = []
        kview = lambda t_, r0_: t_[r0_:r0_ + 128, 0:1024].rearrange("p (c n) -> p c n", c=4)
        vview = lambda t_, r0_: t_[r0_:r0_ + 128, :].rearrange("p (s n) -> p s n", s=2)
        e2d.append(sp.dma(kview(e2_in[l], 0), nakT[:, :, 256:512], reads=[B_nak]))
        e2d.append(sp.dma(kview(e2_in[l], 128), nakT[:, :, 2048:2304], reads=[B_nak]))
        e2d.append(sp.dma(vview(e2_in[l], 256), naV[:, 2:4, :], reads=[B_nav]))
        e2d.append(sp.dma(vview(e2_in[l], 384), naV[:, 16:18, :], reads=[B_nav]))
        eb2 = Buf()
        eb2.w = e2d
        e2_done = allgather(e2_in[l], e2_out[l], [eb2])
        sp.dma(nakT[:, :, 0:256], kview(e2_out[l], 128), reads=[e2_done], writes=[B_nak])
        sp.dma(nakT[:, :, 2304:2560], kview(e2_out[l], 512), reads=[e2_done], writes=[B_nak])
        sp.dma(naV[:, 0:2, :], vview(e2_out[l], 384), reads=[e2_done], writes=[B_nav])
        sp.dma(naV[:, 18:20, :], vview(e2_out[l], 512 + 256), reads=[e2_done], writes=[B_nav])
        na_qk_proj(0, 0, naq, 0)
        dve.op(lambda e: e.tensor_scalar(naV[:, 0:2, :], naV[:, 0:2, :], pvc(PV_FLAGL), None, ALU.mult), reads=[B_nav, B_const], writes=[B_nav])
        dve.op(lambda e: e.tensor_scalar(naV[:, 18:20, :], naV[:, 18:20, :], pvc(PV_FLAGR), None, ALU.mult), reads=[B_nav, B_const], writes=[B_nav])
        k.barrier()

        nmask = [hv(i * 1792, [896], BF16) for i in range(3)]
        B_nmask = Buf()
        for i in range(3):
            pool.dma(nmask[i], namask_d[i], writes=[B_nmask] + B_hT)
        gst = [hv(6 * KB + i * 3584, [896], F32) for i in range(2)]
        gstb = [Buf(), Buf()]
        ttf = [hv(14 * KB + i * 1792, [896], BF16) for i in range(2)]
        ttfb = [Buf(), Buf()]
        ttab = [[hv(18 * KB + (i * 3 + m) * 1792, [896], BF16) for m in range(3)] for i in range(2)]
        ttabb = [Buf(), Buf()]
        NPT = [av(NTMP + i * KB, [2, 256], BF16) for i in range(4)]
        nptb = [[Buf(), Buf()] for _ in range(4)]
        NE = [av(NTMP + 4 * KB + i * KB, [2, 256], BF16) for i in range(4)]
        neb = [Buf() for _ in range(4)]
        nrd = av(NTMP + 8 * KB, [256], F32)
        nrdb = Buf()
        def na_tables(h):
            hi = h % 2
            sp.dma(gst[hi], nag_d[l, h], writes=[gstb[hi]])
            act.op(lambda e, o=ttf[hi], i=gst[hi]: e.activation(o, i, AF.Exp), reads=[gstb[hi]], writes=[ttfb[hi]])
            for m in range(3):
                dve.op(lambda e, o=ttab[hi][m], a=ttf[hi], b=nmask[m]: e.tensor_tensor(o, a, b, ALU.mult),
                       reads=[ttfb[hi], B_nmask], writes=[ttabb[hi]])

        steps = [(h, b, wp) for h in range(8) for b in range(8) for wp in range(3)]
        NS_ = len(steps)
        LAG = 3

        def na_front(n):
            h, b, wp = steps[n]
            c4 = h // 2
            r0 = 64 * (h % 2)
            hi = h % 2
            if b == 0 and wp == 0:
                na_tables(h)
            tab = ttab[hi][0 if b == 0 else (2 if b == 7 else 1)]
            qb_ = [B_naq[c4][b]]
            sbk = n % 4
            for j in range(2):
                w = 2 * wp + 1 - j
                s_ = 2 * b + w
                pe.op(lambda e, o=psum[:, sbk, j * 256:(j + 1) * 256], a=nakT[r0:r0 + 64, c4, s_ * 128:(s_ + 1) * 128],
                      r=naq[r0:r0 + 64, c4, b * 256:(b + 1) * 256]: e.matmul(o, a, r, start=True, stop=True),
                      reads=[B_nak] + qb_, writes=[bank[sbk]], signal=(j == 1))
            ii = n % 4
            act.op(lambda e, o=NE[ii], s2=psum[:, sbk, :].rearrange("p (a q) -> p a q", a=2): e.activation(o, s2, AF.Exp, scale=0.125),
                   reads=[bank[sbk]], writes=[neb[ii]])
            w_hi = 2 * wp + 1
            c0 = (10 - 2 * w_hi) * 64
            tsl = tab[:, c0:c0 + 256]
            tsl2 = tab[:, c0 + 128:c0 + 384]
            dve.op(lambda e, o=NPT[ii][:, 0, :], a=NE[ii][:, 0, :], b_=tsl: e.tensor_tensor(o, a, b_, ALU.mult),
                   reads=[neb[ii], ttabb[hi]], writes=[nptb[ii][0]])
            pool.op(lambda e, o=NPT[ii][:, 1, :], a=NE[ii][:, 1, :], b_=tsl2: e.tensor_tensor(o, a, b_, ALU.mult),
                    reads=[neb[ii], ttabb[hi]], writes=[nptb[ii][1]])

        def na_back(n):
            h, b, wp = steps[n]
            c4 = h // 2
            odd = h % 2
            r0 = 64 * odd
            nvc0 = c4 * 192 + (64 if odd else 0)
            ob = 4 + (h * 8 + b) % 4
            ii = n % 4
            qb_ = [B_naq[c4][b]]
            for j in range(2):
                w = 2 * wp + 1 - j
                s_ = 2 * b + w
                first = (wp == 0 and j == 0)
                last = (wp == 2 and j == 1)
                pe.op(lambda e, o=psum[:, ob, 0:256], a=naV[:, s_, nvc0:nvc0 + 128], r=NPT[ii][:, j, :], st=first, sp_=last:
                      e.matmul(o, a, r, start=st, stop=sp_),
                      reads=[nptb[ii][j], B_nav], writes=[bank[ob]], signal=True)
            if wp == 2:
                dr0 = 64 - r0
                dve.op(lambda e, o=nrd[dr0:dr0 + 64, :], i=psum[dr0:dr0 + 64, ob, 0:256]: e.reciprocal(o, i), reads=[bank[ob]], writes=[nrdb])
                dve.op(lambda e, o=naq[r0:r0 + 64, c4, b * 256:(b + 1) * 256], a=psum[r0:r0 + 64, ob, 0:256], b_=nrd[dr0:dr0 + 64, :]:
                       e.tensor_tensor(o, a, b_, ALU.mult), reads=[bank[ob], nrdb], writes=qb_)

        for n in range(NS_ + LAG):
            if n < NS_:
                na_front(n)
            if n - LAG >= 0:
                na_back(n - LAG)
        pass
        if stop == (l, 'na'):
            return True

        proj_residual(w_out_d[l][0:512, :], 4,
                      lambda c, blk: naq[:, c, blk * 512:(blk + 1) * 512],
                      lambda c, blk: [B_naq[c][2 * blk], B_naq[c][2 * blk + 1]])
        k.barrier()
        if stop == (l, 'mixer'):
            return True

        qm = av(0, [4, T], BF16)
        memT = av(16 * KB, [8, 256], F32)
        memn = av(24 * KB, [8, 256], BF16)
        kmT = av(28 * KB, [4, 256], BF16)
        Vm = av(30 * KB, [2, 512], BF16)
        MT = 32 * KB
        B_memT = Buf()
        B_memn = Buf()
        B_km = Buf()
        B_vm = Buf()
        load_transpose(mem_d, 2, memT, lambda t, half: [B_memT], 40 * KB)
        pass
        rmsnorm_T(memT, 256, nbase + 16, lambda c, blk: memn[:, c, :], MT, lambda c, blk: [B_memT], lambda c, blk: [B_memn], nblk=1)
        norm_x_to_hT(nbase + 8, 56 * KB)
        k.barrier()
        w_kv_l = w_mkv_d[l]
        for hd in range(4):
            wt, wb = wload_cols(w_kv_l, hd * 128)
            for c in range(8):
                pe.op(lambda e, o=PS(hd % 2, 256), a=wt[:, c, :], r=memn[:, c, :], st=(c == 0), sp_=(c == 7):
                      e.matmul(o, a, r, start=st, stop=sp_), reads=[wb, B_memn], writes=[bank[hd % 2]], signal=(c == 7))
            act.op(lambda e, o=kmT[:, hd, :], i=PS(hd % 2, 256): e.copy(o, i), reads=[bank[hd % 2]], writes=[B_km])
        for c4 in range(4):
            wt, wb = wload_cols(w_kv_l, 512 + c4 * 128)
            for mt in range(2):
                pb = 2 + (c4 * 2 + mt) % 2
                for c in range(8):
                    pe.op(lambda e, o=PS(pb, 128), a=memn[:, c, mt * 128:(mt + 1) * 128], r=wt[:, c, :], st=(c == 0), sp_=(c == 7):
                          e.matmul(o, a, r, start=st, stop=sp_), reads=[wb, B_memn], writes=[bank[pb]], signal=(c == 7))
                dve.op(lambda e, o=Vm[:, mt, c4 * 128:(c4 + 1) * 128], i=PS(pb, 128): e.tensor_copy(o, i), reads=[bank[pb]], writes=[B_vm])
        B_qm = [[Buf() for _ in range(4)] for _ in range(4)]
        w_q_l = w_mq_d[l]
        for hd in range(4):
            wt, wb = wload_cols(w_q_l, hd * 128)
            for blk in range(4):
                pb = 4 + blk % 2
                for c in range(8):
                    pe.op(lambda e, o=PS(pb), a=wt[:, c, :], r=hblk(c, blk), st=(c == 0), sp_=(c == 7):
                          e.matmul(o, a, r, start=st, stop=sp_), reads=[wb, B_hT[blk]], writes=[bank[pb]], signal=(c == 7))
                dst = qm[:, hd, blk * 512:(blk + 1) * 512]
                if blk % 2 == 0:
                    act.op(lambda e, o=dst, i=PS(pb): e.copy(o, i), reads=[bank[pb]], writes=[B_qm[hd][blk]])
                else:
                    dve.op(lambda e, o=dst, i=PS(pb): e.tensor_copy(o, i), reads=[bank[pb]], writes=[B_qm[hd][blk]])
        MPT = [av(MT + i * 2 * KB, [2, 512], BF16) for i in range(2)]
        mptb = [Buf(), Buf()]
        mrd = av(MT + 4 * KB, [512], F32)
        mrdb = Buf()
        msc = float(128 ** -0.5)
        msteps = [(blk, hd) for blk in range(4) for hd in range(4)]

        def m_front(it):
            blk, hd = msteps[it]
            sbk = 2 * (it % 2)
            for mt in range(2):
                pe.op(lambda e, o=PS(sbk + mt), a=kmT[:, hd, mt * 128:(mt + 1) * 128], r=qm[:, hd, blk * 512:(blk + 1) * 512]:
                      e.matmul(o, a, r, start=True, stop=True),
                      reads=[B_km, B_qm[hd][blk]], writes=[bank[sbk], bank[sbk + 1]], signal=(mt == 1))
            p = MPT[it % 2]
            pb_ = mptb[it % 2]
            act.op(lambda e, o=p, s_=psum[:, sbk:sbk + 2, :]: e.activation(o, s_, AF.Exp, scale=msc),
                   reads=[bank[sbk], bank[sbk + 1]], writes=[pb_])

        def m_back(it):
            blk, hd = msteps[it]
            p = MPT[it % 2]
            pb_ = mptb[it % 2]
            ob = 4 + 2 * (it % 2)
            for mt in range(2):
                pe.op(lambda e, o=PS(ob), a=Vm[:, mt, hd * 128:(hd + 1) * 128], r=p[:, mt, :], st=(mt == 0), sp_=(mt == 1):
                      e.matmul(o, a, r, start=st, stop=sp_), reads=[pb_, B_vm], writes=[bank[ob]], signal=(mt == 1))
            for mt in range(2):
                pe.op(lambda e, o=PS(ob + 1), r=p[:, mt, :], st=(mt == 0), sp_=(mt == 1):
                      e.matmul(o, ones128, r, start=st, stop=sp_), reads=[pb_, B_const], writes=[bank[ob + 1]], signal=(mt == 1))
            dve.op(lambda e, i=PS(ob + 1): e.reciprocal(mrd, i), reads=[bank[ob + 1]], writes=[mrdb])
            dve.op(lambda e, o=qm[:, hd, blk * 512:(blk + 1) * 512], a=PS(ob): e.tensor_tensor(o, a, mrd, ALU.mult),
                   reads=[bank[ob], mrdb], writes=[B_qm[hd][blk]])

        for it in range(len(msteps) + 1):
            if it < len(msteps):
                m_front(it)
            if it >= 1:
                m_back(it - 1)
        pass
        proj_residual(w_mo_d[l], 4, lambda c, blk: qm[:, c, blk * 512:(blk + 1) * 512], lambda c, blk: [B_qm[c][blk]])
        k.barrier()
        if stop == (l, 'mem'):
            return True

        norm_x_to_hT(nbase + 24, 64 * KB)
        hst = av(0, [2, 8], BF16)
        hstb = Buf()
        dve.op(lambda e: e.tensor_copy(hst[:, 0, :], hT[:, :, 1]), reads=[B_hT[0]], writes=[hstb])
        dve.op(lambda e: e.tensor_copy(hst[:, 1, :], hT[:, :, 2048]), reads=[B_hT[3]], writes=[hstb])
        d = sp.dma(e3_in[l][:, :].rearrange("p (a b) -> p a b", a=2), hst, reads=[hstb])
        eb3 = Buf()
        eb3.w = [d]
        e3_done = allgather(e3_in[l], e3_out[l], [eb3])
        hrx = av(64, [2, 8], BF16)
        hrxb = Buf()
        sp.dma(hrx[:, 0, :], e3_out[l][0:128, 8:16], reads=[e3_done], writes=[hrxb])
        sp.dma(hrx[:, 1, :], e3_out[l][128:256, 0:8], reads=[e3_done], writes=[hrxb])
        dve.op(lambda e: e.tensor_scalar(hT[:, :, 0], hrx[:, 0, :], pvc(PV_FLAGL), None, ALU.mult), reads=[hrxb, B_const], writes=[B_hhalo])
        dve.op(lambda e: e.tensor_scalar(hT[:, :, 2049], hrx[:, 1, :], pvc(PV_FLAGR), None, ALU.mult), reads=[hrxb, B_const], writes=[B_hhalo])
        k.barrier()
        if stop == (l, 'ffnhalo'):
            return True

        actT = av(1 * KB, [22, 1024], BF16)
        TG = [av(45 * KB + i * 4 * KB, [1024], F32) for i in range(2)]
        TV = [av(53 * KB + i * 4 * KB, [1024], F32) for i in range(2)]
        UC = [av(61 * KB + i * 4128, [1032], F32) for i in range(2)]
        tgb = [Buf(), Buf()]
        tvb = [Buf(), Buf()]
        ucb = [Buf(), Buf()]
        WD = [av(61 * KB + 8256 + i * 5632, [22, 128], BF16) for i in range(2)]
        wdb = [Buf(), Buf()]
        B_act = [Buf() for _ in range(22)]
        cwb = PV_CW + l * 176
        w_up_l = w_up_d[l]
        w_dn_l = w_dn_d[l]
        UHB = 7
        uhb = [bank[6], bank[7]]
        for hh in range(int(os.environ.get('FFN_HH', '2'))):
            base = 1 + hh * 1024
            uslot = 0
            order_ = [cp_ + 22 * g_ for cp_ in range(int(os.environ.get('FFN_CP', '22'))) for g_ in range(2)]
            loaded_ = {}
            PF_ = 5
            for s0 in range(min(PF_, len(order_))):
                loaded_[s0] = wload_cols(w_up_l, order_[s0] * 128)
            for cp in range(int(os.environ.get('FFN_CP', '22'))):
                for gv_ in range(2):
                    j = cp + 22 * gv_
                    si_ = 2 * cp + gv_
                    if si_ + PF_ < len(order_):
                        loaded_[si_ + PF_] = wload_cols(w_up_l, order_[si_ + PF_] * 128)
                    wt, wb = loaded_.pop(si_)
                    ub = 2 * (uslot % 3)
                    uc = UC[uslot % 2]
                    ucb_ = ucb[uslot % 2]
                    uslot += 1
                    for sb_ in range(2):
                        for c in range(8):
                            pe.op(lambda e, o=PS(ub + sb_), a=wt[:, c, :], r=hT[:, c, base + sb_ * 512:base + (sb_ + 1) * 512],
                                  st=(c == 0), sp_=(c == 7): e.matmul(o, a, r, start=st, stop=sp_),
                                  reads=[wb, B_hT[hh * 2 + sb_]], writes=[bank[ub], bank[ub + 1]], signal=(c == 7 and sb_ == 1))
                    uhi_ = si_ % 2
                    uh = psum[:, 6 + uhi_, 0:2]
                    for c in range(8):
                        pe.op(lambda e, o=uh, a=wt[:, c, :], r=hT[:, c, base - 1:base + 1025:1025], st=(c == 0), sp_=(c == 7):
                              e.matmul(o, a, r, start=st, stop=sp_),
                              reads=[wb, B_hT[0], B_hT[1], B_hT[2], B_hT[3], B_hhalo], writes=[uhb[uhi_]], signal=(c == 7))
                    U = psum[:, ub:ub + 2, :].rearrange("p a q -> p (a q)")
                    tt_ = (TG if gv_ == 0 else TV)[cp % 2]
                    ttb_ = (tgb if gv_ == 0 else tvb)[cp % 2]
                    ub_ = [bank[ub], bank[ub + 1]]
                    act.op(lambda e, o=uc[:, 1:1025], u=U: e.copy(o, u), reads=ub_, writes=[ucb_])
                    act.op(lambda e, o=uc[:, 0:1026:1025], u=uh: e.copy(o, u), reads=[uhb[uhi_]], writes=[ucb_])
                    act.op(lambda e, o=tt_, u=U, s_=pvc(cwb + 44 + j), b_=pvc(cwb + 132 + j):
                           e.activation(o, u, AF.Identity, bias=b_, scale=s_), reads=ub_ + [B_const], writes=[ttb_])
                    dve.op(lambda e, o=tt_, u=uc[:, 0:1024], s_=pvc(cwb + j):
                           e.scalar_tensor_tensor(o, u, s_, o, ALU.mult, ALU.add), reads=[ucb_, ttb_, B_const], writes=[ttb_])
                    dve.op(lambda e, o=tt_, u=uc[:, 2:1026], s_=pvc(cwb + 88 + j):
                           e.scalar_tensor_tensor(o, u, s_, o, ALU.mult, ALU.add), reads=[ucb_, ttb_, B_const], writes=[ttb_])
                act.op(lambda e, o=TG[cp % 2]: e.activation(o, o, AF.Silu), reads=[tgb[cp % 2]], writes=[tgb[cp % 2]])
                pool.op(lambda e, o=actT[:, cp, :], a=TG[cp % 2], b_=TV[cp % 2]: e.tensor_tensor(o, a, b_, ALU.mult),
                        reads=[tgb[cp % 2], tvb[cp % 2]], writes=[B_act[cp]])
            for oc in range(int(os.environ.get('FFN_OC', '8'))):
                wi = oc % 2
                pool.dma(WD[wi], w_dn_l[:, oc * 128:(oc + 1) * 128].rearrange("(kc p) n -> p kc n", p=128), writes=[wdb[wi]])
                for sb_ in range(2):
                    pb = (oc * 2 + sb_) % 6
                    for c in range(22):
                        pe.op(lambda e, o=PS(pb), a=WD[wi][:, c, :], r=actT[:, c, sb_ * 512:(sb_ + 1) * 512], st=(c == 0), sp_=(c == 21):
                              e.matmul(o, a, r, start=st, stop=sp_), reads=[wdb[wi], B_act[c]], writes=[bank[pb]], signal=(c == 21))
                    blk = hh * 2 + sb_
                    xs = xT[:, oc, blk * 512:(blk + 1) * 512]
                    dve.op(lambda e, o=xs, p_=PS(pb): e.tensor_tensor(o, o, p_, ALU.add), reads=[bank[pb], B_xT[oc][blk]], writes=[B_xT[oc][blk]])
        k.barrier()
        return False

    if stop != 'load':
        for l in range(nlayers):
            if layer_body(l):
                break

    yT = av(0, [8, 512], F32)
    yT2 = [av(i * 16 * KB, [8, 512], F32) for i in range(2)]
    ytb = [Buf(), Buf()]
    ost = [av(32 * KB + i * 4 * KB, [1024], F32) for i in range(4)]
    ostb = [Buf() for _ in range(4)]
    out_deps = []
    cnt = {"n": 0}

    def emit_out(blk, src_of, src_b):
        for tt in range(4):
            t = blk * 4 + tt
            oi = t % 4
            for half in range(2):
                pb = cnt["n"] % 8
                cnt["n"] += 1
                for j in range(4):
                    c = half * 4 + j
                    pe.op(lambda e, o=psum[:, pb, j * 128:(j + 1) * 128], i=src_of(c, blk, tt): e.transpose(o, i, ident),
                          reads=src_b(c, blk) + [B_const], writes=[bank[pb]], signal=(j == 3))
                dst = ost[oi][:, half * 512:(half + 1) * 512]
                if half == 0:
                    dve.op(lambda e, o=dst, i=PS(pb): e.tensor_copy(o, i), reads=[bank[pb]], writes=[ostb[oi]])
                else:
                    act.op(lambda e, o=dst, i=PS(pb): e.copy(o, i), reads=[bank[pb]], writes=[ostb[oi]])
            out_deps.append(sp.dma(out_d[t * 128:(t + 1) * 128, :], ost[oi], reads=[ostb[oi]]))

    if final_norm:
        rmsnorm_T(xT, T, PV_FINAL, lambda c, blk: yT2[blk % 2][:, c, :], 56 * KB,
                  lambda c, blk: [B_xT[c][blk]], lambda c, blk: [ytb[blk % 2]],
                  after_blk=lambda blk: emit_out(blk, lambda c, b_, tt: yT2[b_ % 2][:, c, tt * 128:(tt + 1) * 128],
                                                 lambda c, b_: [ytb[b_ % 2]]))
    else:
        for blk in range(4):
            emit_out(blk, lambda c, b_, tt: xT[:, c, b_ * 512 + tt * 128:b_ * 512 + (tt + 1) * 128],
                     lambda c, b_: [B_xT[c][b_]])
    for d in out_deps:
        sp._wait(d)
    k.barrier()

    if os.environ.get('SEMDBG'):
        print('SEMS', [(e.name, e.sem.cnt, len(e.items)) for e in k.engs], [(e.name, [d.cnt for d in e.dsems]) for e in k.engs])
    with nc.Block() as block:
        @block.sync
        def _(e):
            sp.replay(e)

        @block.gpsimd
        def _(e):
            pool.replay(e)

        @block.tensor
        def _(e):
            pe.replay(e)

        @block.vector
        def _(e):
            dve.replay(e)

        @block.scalar
        def _(e):
            act.replay(e)
    stack.close()
    return nc


def _consts(p):
    ident = np.eye(128, dtype=np.float32)
    R = np.zeros((128, 128), np.float32)
    for blk in range(2):
        for a in range(2):
            o = blk * 64 + a * 32
            for f in range(16):
                R[o + f, o + 16 + f] = -1.0
                R[o + 16 + f, o + f] = 1.0
    rmatT = np.ascontiguousarray(R.T)
    t = np.arange(T) + p * T
    pos = np.stack([t // 64, t % 64], 0).astype(np.float32)
    inv = (10000.0 ** (-np.arange(16, dtype=np.float32) / 16)).astype(np.float32)
    cosT = np.zeros((128, T), np.float32)
    sinT = np.zeros((128, T), np.float32)
    for d in range(128):
        dd = d % 64
        a = dd // 32
        f = dd % 16
        ang = (pos[a] * inv[f]).astype(np.float32)
        cosT[d] = np.cos(ang)
        sinT[d] = np.sin(ang)
    kr = (np.arange(128) // 64)[:, None, None]
    kc = (np.arange(128) % 64)[:, None, None]
    idx = np.arange(14)[None, :, None]
    qc = np.arange(64)[None, None, :]
    dr = 6 + kr - idx + 0 * qc
    cs = np.clip(qc - 8, 0, 48)
    colvalid = (kc >= cs) & (kc < cs + 16)
    band = (dr >= -4) & (dr <= 3)
    full = colvalid & (dr >= -7) & (dr <= 7)
    bandm = colvalid & band
    mA = full if p == 0 else bandm
    mC = bandm if p == 0 else full
    namask = np.stack([mA, bandm, mC], 0).astype(np.float32).reshape(3, 128, 896)
    return ident, rmatT, cosT, sinT, namask


def _na_gather(na_rpb):
    kr = (np.arange(128) // 64)[:, None, None]
    kc = (np.arange(128) % 64)[:, None, None]
    idx = np.arange(14)[None, :, None]
    qc = np.arange(64)[None, None, :]
    dr = np.clip(6 + kr - idx + 0 * qc, -7, 7) + 7
    dc = np.clip(kc - qc + 0 * idx, -15, 15) + 15
    g = na_rpb[:, :, dr, dc]
    return np.ascontiguousarray(g.reshape(L, 8, 128, 896)).astype(np.float32)


def _pv(p, inp):
    pv = np.zeros((128, PV_N), np.float32)

    def col8(v):
        return np.asarray(v, np.float32).reshape(8, 128).T

    for l in range(L):
        b = PV_NORM + l * 32
        pv[:, b + 0:b + 8] = col8(inp["norm_mix"][l])
        pv[:, b + 8:b + 16] = col8(inp["norm_mem_q"][l])
        pv[:, b + 16:b + 24] = col8(inp["norm_mem_kv"][l])
        pv[:, b + 24:b + 32] = col8(inp["norm_ffn"][l])
        pv[:, PV_GQ + l] = np.tile(np.asarray(inp["gqa_q_norm"][l], np.float32), 2)
        pv[:, PV_GK + l] = np.tile(np.asarray(inp["gqa_k_norm"][l], np.float32), 2)
        cw = np.asarray(inp["conv_w"][l], np.float32)
        cb = np.asarray(inp["conv_b"][l], np.float32)
        o = PV_CW + l * 176
        for kk in range(3):
            pv[:, o + kk * 44:o + (kk + 1) * 44] = cw[kk].reshape(44, 128).T
        pv[:, o + 132:o + 176] = cb.reshape(44, 128).T
    pv[:, PV_FINAL:PV_FINAL + 8] = col8(inp["norm_final"])
    pv[:, PV_FLAGL] = 1.0 if p == 1 else 0.0
    pv[:, PV_FLAGR] = 1.0 if p == 0 else 0.0
    return pv


_CACHE = {}


def kernel(**inputs):
    inp = {k_: np.asarray(v) for k_, v in inputs.items()}
    if "nc" not in _CACHE:
        _CACHE["nc"] = build_nc()
    nc = _CACHE["nc"]
    nag = _na_gather(np.asarray(inp["na_rpb"], np.float32))
    shared = {n_: np.ascontiguousarray(inp[n_], dtype=np.float32) for n_ in
              ("w_in", "w_out", "w_mem_q", "w_mem_kv", "w_mem_o", "w_up", "w_down")}
    in_maps = []
    for c in range(8):
        b, p = c // 2, c % 2
        ident, rmatT, cosT, sinT, namask = _consts(p)
        m = dict(shared)
        m["x"] = np.ascontiguousarray(inp["x"][b, p * T:(p + 1) * T, :], dtype=np.float32)
        m["mem"] = np.ascontiguousarray(inp["mem"][b], dtype=np.float32)
        m["pv"] = _pv(p, inp)
        m["ident"] = ident
        m["rmatT"] = rmatT
        m["cosT"] = cosT
        m["sinT"] = sinT
        m["nag"] = nag
        m["namask"] = namask
        in_maps.append(m)
    res = run_bass_kernel_spmd(nc, in_maps, core_ids=list(range(8)))
    out = np.zeros((4, 4096, D), np.float32)
    for c in range(8):
        b, p = c // 2, c % 2
        out[b, p * T:(p + 1) * T, :] = np.asarray(res.results[c]["out"], dtype=np.float32)
    return out
```

```python
import numpy as np
import concourse.bass as bass
import concourse.mybir as mybir
from concourse.bass_utils import run_bass_kernel_spmd

F32 = mybir.dt.float32
BF16 = mybir.dt.bfloat16
AF = mybir.ActivationFunctionType
ALU = mybir.AluOpType

L = 2
D = 1024
T = 2048
NCH = 8
DFF = 2816
NJ = 44
EPS = 1e-6
PAIRS = [[0, 1], [2, 3], [4, 5], [6, 7]]

PV_NORM = 0
PV_FINAL = 64
PV_GQ = 72
PV_GK = 74
PV_FLAGL = 76
PV_FLAGR = 77
PV_CW = 80
PV_N = 80 + 2 * 176


class Buf:
    __slots__ = ("w", "r")

    def __init__(self):
        self.w = []
        self.r = []


class Sem:
    def __init__(self, h):
        self.h = h
        self.cnt = 0


class Eng:
    def __init__(self, K, name):
        self.K = K
        self.name = name
        self.sem = K.new_sem(name)
        self.items = []
        self.seen = {}
        self.pend_r = []
        self.pend_w = []
        self.dsems = []
        self.dsi = 0

    def _wait(self, dep):
        s, v = dep
        if self.seen.get(s, 0) >= v:
            return
        self.seen[s] = v
        self.items.append(("w", s.h, v))

    def _deps(self, reads, writes):
        for b in reads:
            for d in b.w:
                self._wait(d)
        for b in writes:
            for d in b.w:
                self._wait(d)
            for d in b.r:
                self._wait(d)

    def _register(self, dep, reads, writes):
        for b in reads:
            b.r.append(dep)
            if len(b.r) > 12:
                m = {}
                for s, v in b.r:
                    if m.get(s, 0) < v:
                        m[s] = v
                b.r = list(m.items())
        for b in writes:
            b.w = [dep]
            b.r = []

    def op(self, fn, reads=(), writes=(), signal=True):
        self._deps(reads, writes)
        if signal:
            self.sem.cnt += 1
            dep = (self.sem, self.sem.cnt)
            self.items.append(("o", fn, self.sem.h, 1))
            self._register(dep, list(reads) + self.pend_r, list(writes) + self.pend_w)
            self.pend_r = []
            self.pend_w = []
            return dep
        self.items.append(("o", fn, None, 0))
        self.pend_r += list(reads)
        self.pend_w += list(writes)
        return None

    def dma(self, out, in_, reads=(), writes=()):
        self._deps(reads, writes)
        if not self.dsems:
            self.dsems = [self.K.new_sem(self.name + "_d%d" % i) for i in range(6)]
        s = self.dsems[self.dsi % len(self.dsems)]
        self.dsi += 1
        s.cnt += 16
        dep = (s, s.cnt)
        self.items.append(("o", lambda e, o=out, i=in_: e.dma_start(out=o, in_=i), s.h, 16))
        self._register(dep, reads, writes)
        return dep

    def replay(self, e):
        for it in self.items:
            if it[0] == "w":
                e.wait_ge(it[1], it[2])
            else:
                ins = it[1](e)
                if it[2] is not None:
                    ins.then_inc(it[2], it[3])


class K:
    def __init__(self, nc, stack):
        self.nc = nc
        self.stack = stack
        self.sems = []
        self.pe = Eng(self, "pe")
        self.act = Eng(self, "act")
        self.dve = Eng(self, "dve")
        self.pool = Eng(self, "pool")
        self.sp = Eng(self, "sp")
        self.engs = [self.pe, self.act, self.dve, self.pool, self.sp]

    def new_sem(self, name):
        h = self.stack.enter_context(self.nc.semaphore(name))
        s = Sem(h)
        self.sems.append(s)
        return s

    def barrier(self):
        for e in self.engs:
            if e.pend_r or e.pend_w:
                raise RuntimeError("pending unsignaled ops at barrier on " + e.name)
        for e in self.engs:
            if e is self.pool:
                continue
            for s in self.sems:
                if s.cnt > 0:
                    e._wait((s, s.cnt))


from contextlib import ExitStack


def build_nc(nlayers=L, final_norm=True, stop=None):
    nc = bass.Bass("TRN2", target_bir_lowering=False)
    stack = ExitStack()
    k = K(nc, stack)
    pe, act, dve, pool, sp = k.pe, k.act, k.dve, k.pool, k.sp

    def din(name, shape, dt=F32):
        return nc.dram_tensor(name, list(shape), dt, kind="ExternalInput").ap()

    x_d = din("x", [T, D])
    mem_d = din("mem", [256, D])
    w_in_d = din("w_in", [L, D, 2304])
    w_out_d = din("w_out", [L, D, D])
    w_mq_d = din("w_mem_q", [L, D, 512])
    w_mkv_d = din("w_mem_kv", [L, D, 1024])
    w_mo_d = din("w_mem_o", [L, 512, D])
    w_up_d = din("w_up", [L, D, 2 * DFF])
    w_dn_d = din("w_down", [L, DFF, D])
    pv_d = din("pv", [128, PV_N])
    ident_d = din("ident", [128, 128])
    rt_d = din("rmatT", [128, 128])
    cos_d = din("cosT", [128, T])
    sin_d = din("sinT", [128, T])
    nag_d = din("nag", [L, 8, 128, 896])
    namask_d = din("namask", [3, 128, 896])
    out_d = nc.dram_tensor("out", [T, D], F32, kind="ExternalOutput").ap()

    e1_in = [nc.dram_tensor("e1in%d" % l, [384, 2048], BF16) for l in range(L)]
    e1_out = [nc.dram_tensor("e1out%d" % l, [768, 2048], BF16) for l in range(L)]
    e2_in = [nc.dram_tensor("e2in%d" % l, [512, 1536], BF16) for l in range(L)]
    e2_out = [nc.dram_tensor("e2out%d" % l, [1024, 1536], BF16) for l in range(L)]
    e3_in = [nc.dram_tensor("e3in%d" % l, [128, 16], BF16) for l in range(L)]
    e3_out = [nc.dram_tensor("e3out%d" % l, [256, 16], BF16) for l in range(L)]
    cc_sems = [k.new_sem("cc%d" % i) for i in range(3 * L)]

    import os
    ARENA_B = int(os.environ.get('ARENA_KB', '196')) * 1024
    arena = stack.enter_context(nc.sbuf_tensor("arena", [128, ARENA_B // 2], BF16))
    psum = stack.enter_context(nc.psum_tensor("psum", [128, 8, 512], F32))

    def view(off_bytes, shape, dt):
        esz = 4 if dt == F32 else 2
        n = int(np.prod(shape))
        assert off_bytes % 4 == 0 and off_bytes + n * esz <= ARENA_B, (off_bytes, shape)
        ap = arena[:, off_bytes // 2: off_bytes // 2 + n * esz // 2]
        if dt == F32:
            ap = ap.bitcast(F32)
        if len(shape) == 2:
            return ap.rearrange("p (a b) -> p a b", a=shape[0])
        if len(shape) == 3:
            return ap.rearrange("p (a b c) -> p a b c", a=shape[0], b=shape[1])
        return ap

    KB = 1024
    off = 0

    def take(nbytes):
        nonlocal off
        o = off
        off += (nbytes + 31) // 32 * 32
        return o

    xT = view(take(64 * KB), [8, T], F32)
    hT = view(take(8 * 2050 * 2), [8, 2050], BF16)
    HT_OFF = off - (8 * 2050 * 2 + 31) // 32 * 32
    NW = 8
    wring = [view(take(2 * KB), [8, 128], BF16) for _ in range(NW)]
    wbuf = [Buf() for _ in range(NW)]
    ident = view(take(512), [128], F32)
    onesblk = view(take(256), [128], BF16)
    ones128 = view(take(256), [128], BF16)
    rmt = view(take(256), [128], BF16)
    pv = view(take(PV_N * 4), [PV_N], F32)
    A0 = off
    assert stop == 'load' or ARENA_B - A0 >= 80 * KB, (ARENA_B - A0)

    def av(o, shape, dt):
        return view(A0 + o, shape, dt)

    def hv(o, shape, dt):
        return view(HT_OFF + o, shape, dt)

    bank = [Buf() for _ in range(8)]

    def PS(b, n=512):
        return psum[:, b, 0:n]

    def PS2(b):
        return psum[:, b:b + 2, :]

    B_xT = [[Buf() for _ in range(4)] for _ in range(8)]
    B_hT = [Buf() for _ in range(4)]
    B_hhalo = Buf()
    B_const = Buf()

    wstate = {"i": 0}

    def wload(src_ap):
        i = wstate["i"] % NW
        wstate["i"] += 1
        pool.dma(wring[i], src_ap, writes=[wbuf[i]])
        return wring[i], wbuf[i]

    def wload_cols(w_l, c0, ncols=128):
        src = w_l[:, c0:c0 + ncols].rearrange("(kc p) n -> p kc n", p=128)
        i = wstate["i"] % NW
        wstate["i"] += 1
        dst = wring[i][:, :, 0:ncols]
        pool.dma(dst, src, writes=[wbuf[i]])
        return wring[i], wbuf[i]

    sp.dma(ident, ident_d, writes=[B_const])
    sp.dma(pv, pv_d, writes=[B_const])
    pool.dma(rmt, rt_d, writes=[B_const])
    dve.op(lambda e: e.memset(ones128, 1.0), writes=[B_const])
    dve.op(lambda e: e.memset(onesblk, 0.0), writes=[B_const])
    dve.op(lambda e: e.memset(onesblk[0:64, 0:64], 1.0), writes=[B_const])
    dve.op(lambda e: e.memset(onesblk[64:128, 64:128], 1.0), writes=[B_const])
    dve.op(lambda e: e.memset(hT[:, :, 0:1], 0.0), writes=[B_hhalo])
    dve.op(lambda e: e.memset(hT[:, :, 2049:2050], 0.0), writes=[B_hhalo])
    k.barrier()

    def pvc(c, n=1):
        return pv[:, c:c + n]

    def load_transpose(src_d, ntiles, dstT, dstbufs, stage_off):
        stg = [av(stage_off + i * 4 * KB, [1024], F32) for i in range(2)]
        sb = [Buf(), Buf()]
        for t in range(ntiles):
            s = stg[t % 2]
            sp.dma(s, src_d[t * 128:(t + 1) * 128, :], writes=[sb[t % 2]])
            for half in range(2):
                b = (2 * t + half) % 8
                for j in range(4):
                    c = half * 4 + j
                    pe.op(lambda e, o=psum[:, b, j * 128:(j + 1) * 128], i=s[:, c * 128:(c + 1) * 128]:
                          e.transpose(o, i, ident),
                          reads=[sb[t % 2], B_const], writes=[bank[b]], signal=(j == 3))
                wb = dstbufs(t, half)
                dst = dstT[:, half * 4:half * 4 + 4, t * 128:(t + 1) * 128]
                src = psum[:, b, :].rearrange("p (a q) -> p a q", a=4)
                eng = dve if half == 0 else act
                if eng is dve:
                    dve.op(lambda e, o=dst, i=src: e.tensor_copy(o, i), reads=[bank[b]], writes=wb)
                else:
                    act.op(lambda e, o=dst, i=src: e.copy(o, i), reads=[bank[b]], writes=wb)

    load_transpose(x_d, 16, xT, lambda t, half: [B_xT[half * 4 + j][t // 4] for j in range(4)], 0)
    k.barrier()

    def rmsnorm_T(src, ncols, gcol, dst_fn, tmp_off, src_bufs, dst_bufs, nblk=None, after_blk=None):
        sq = [av(tmp_off + i * KB, [512], BF16) for i in range(4)]
        sqb = [Buf() for _ in range(4)]
        rs = [av(tmp_off + 4 * KB + i * 2 * KB, [512], F32) for i in range(2)]
        rsb = [Buf(), Buf()]
        nb = ncols // 512 if nblk is None else nblk
        w = min(512, ncols)
        for blk in range(nb):
            pb = blk % 2
            for c in range(8):
                i = (blk * 8 + c) % 4
                act.op(lambda e, o=sq[i][:, 0:w], s=src[:, c, blk * w:(blk + 1) * w]: e.activation(o, s, AF.Square),
                       reads=src_bufs(c, blk), writes=[sqb[i]])
                pe.op(lambda e, o=PS(pb, w), r=sq[i][:, 0:w], st=(c == 0), sp_=(c == 7):
                      e.matmul(o, ones128, r, start=st, stop=sp_),
                      reads=[sqb[i], B_const], writes=[bank[pb]], signal=(c == 7))
            r = rs[blk % 2]
            act.op(lambda e, o=r[:, 0:w], s=PS(pb, w): e.activation(o, s, AF.Ln, bias=EPS, scale=1.0 / D),
                   reads=[bank[pb]], writes=[rsb[blk % 2]])
            act.op(lambda e, o=r[:, 0:w]: e.activation(o, o, AF.Exp, scale=-0.5),
                   reads=[rsb[blk % 2]], writes=[rsb[blk % 2]])
            for c in range(8):
                dve.op(lambda e, o=dst_fn(c, blk), s=src[:, c, blk * w:(blk + 1) * w], g=pvc(gcol + c), rr=r[:, 0:w]:
                       e.scalar_tensor_tensor(o, s, g, rr, ALU.mult, ALU.mult),
                       reads=src_bufs(c, blk) + [rsb[blk % 2], B_const], writes=dst_bufs(c, blk))
            if after_blk is not None:
                after_blk(blk)

    def norm_x_to_hT(gcol, tmp_off):
        rmsnorm_T(xT, T, gcol, lambda c, blk: hT[:, c, 1 + blk * 512:1 + (blk + 1) * 512], tmp_off,
                  lambda c, blk: [B_xT[c][blk]], lambda c, blk: [B_hT[blk]])

    def hblk(c, blk):
        return hT[:, c, 1 + blk * 512:1 + (blk + 1) * 512]

    cci = {"i": 0}

    def allgather(src_t, dst_t, reads):
        s = cc_sems[cci["i"]]
        cci["i"] += 1
        pool._deps(reads, [])
        s.cnt += 1
        pool.items.append(("o", lambda e: e.collective_compute(
            "AllGather", ALU.bypass, replica_groups=PAIRS,
            ins=[src_t.ap().opt()], outs=[dst_t.ap().opt()]), s.h, 1))
        b = Buf()
        b.w = [(s, 1)]
        return b

    def proj_residual(w_l, nk, rhs_fn, rhs_bufs):
        for oc in range(8):
            i = wstate["i"] % NW
            wstate["i"] += 1
            pool.dma(wring[i][:, 0:nk, :], w_l[:, oc * 128:(oc + 1) * 128].rearrange("(kc p) n -> p kc n", p=128), writes=[wbuf[i]])
            wt, wb = wring[i], wbuf[i]
            for blk in range(4):
                pb = (oc * 4 + blk) % 4
                for c in range(nk):
                    pe.op(lambda e, o=PS(pb), a=wt[:, c, :], r=rhs_fn(c, blk), st=(c == 0), sp_=(c == nk - 1):
                          e.matmul(o, a, r, start=st, stop=sp_),
                          reads=[wb] + rhs_bufs(c, blk), writes=[bank[pb]], signal=(c == nk - 1))
                xs = xT[:, oc, blk * 512:(blk + 1) * 512]
                dve.op(lambda e, o=xs, p_=PS(pb): e.tensor_tensor(o, o, p_, ALU.add), reads=[bank[pb], B_xT[oc][blk]], writes=[B_xT[oc][blk]])


    def layer_body(l):
        nbase = PV_NORM + l * 32
        w_in_l = w_in_d[l]
        norm_x_to_hT(nbase + 0, 64 * KB)
        k.barrier()
        if stop == (l, 'norm'):
            return True

        qT = av(0, [4, T], BF16)
        ropeC = av(16 * KB, [T], F32)
        ropeS = av(24 * KB, [T], F32)
        kfull = av(16 * KB, [2, 4096], BF16)
        vfull = av(32 * KB, [32, 320], BF16)
        TMP = 52 * KB
        B_rope = Buf()
        sp.dma(ropeC, cos_d, writes=[B_rope])
        sp.dma(ropeS, sin_d, writes=[B_rope])
        B_qT = [[Buf() for _ in range(4)] for _ in range(4)]
        nr_tb = [{n_: Buf() for n_ in ('sq', 'gv', 'rs', 'aa', 'bb')} for _ in range(2)]

        def normrope(ps_b, gcol, dst_ap, dst_bufs, blk, tmpo, ti):
            sl = ti % 2
            o = tmpo + sl * 8 * KB
            sqv = av(o, [512], BF16)
            gv = av(o + KB, [512], BF16)
            rsv = av(o + 2 * KB, [512], F32)
            aa = av(o + 4 * KB, [512], F32)
            bb = av(o + 6 * KB, [512], F32)
            B = nr_tb[sl]
            b2 = 4 + 2 * sl
            b3 = b2 + 1
            act.op(lambda e: e.activation(sqv, PS(ps_b), AF.Square), reads=[bank[ps_b]], writes=[B["sq"]])
            act.op(lambda e: e.activation(gv, PS(ps_b), AF.Identity, scale=pvc(gcol)), reads=[bank[ps_b], B_const], writes=[B["gv"]])
            pe.op(lambda e: e.matmul(PS(b2), onesblk, sqv, start=True, stop=True), reads=[B["sq"], B_const], writes=[bank[b2]])
            pe.op(lambda e: e.matmul(PS(b3), rmt, gv, start=True, stop=True), reads=[B["gv"], B_const], writes=[bank[b3]])
            act.op(lambda e: e.activation(rsv, PS(b2), AF.Ln, bias=EPS, scale=1.0 / 64), reads=[bank[b2]], writes=[B["rs"]])
            act.op(lambda e: e.activation(rsv, rsv, AF.Exp, scale=-0.5), reads=[B["rs"]], writes=[B["rs"]])
            cs = ropeC[:, blk * 512:(blk + 1) * 512]
            sn = ropeS[:, blk * 512:(blk + 1) * 512]
            dve.op(lambda e: e.tensor_tensor(aa, gv, cs, ALU.mult), reads=[B["gv"], B_rope], writes=[B["aa"]])
            dve.op(lambda e: e.tensor_tensor(bb, PS(b3), sn, ALU.mult), reads=[bank[b3], B_rope], writes=[B["bb"]])
            dve.op(lambda e: e.tensor_tensor(aa, aa, bb, ALU.add), reads=[B["aa"], B["bb"]], writes=[B["aa"]])
            dve.op(lambda e: e.tensor_tensor(dst_ap, aa, rsv, ALU.mult), reads=[B["aa"], B["rs"]], writes=dst_bufs)

        kst = [av(TMP + 16 * KB + i * KB, [512], BF16) for i in range(2)]
        kstb = [Buf(), Buf()]
        e1_deps = []
        ti = 0
        for g in range(2):
            i = wstate["i"] % NW
            wstate["i"] += 1
            for hf in range(2):
                pool.dma(wring[i][:, :, hf * 64:(hf + 1) * 64],
                         w_in_l[:, 2048 + g * 64:2048 + (g + 1) * 64].rearrange("(kc p) n -> p kc n", p=128),
                         writes=[wbuf[i]])
            wt, wb = wring[i], wbuf[i]
            for blk in range(4):
                pb = blk % 2
                for c in range(8):
                    pe.op(lambda e, o=PS(pb), a=wt[:, c, :], r=hblk(c, blk), st=(c == 0), sp_=(c == 7):
                          e.matmul(o, a, r, start=st, stop=sp_),
                          reads=[wb, B_hT[blk]], writes=[bank[pb]], signal=(c == 7))
                si = ti % 2
                normrope(pb, PV_GK + l, kst[si], [kstb[si]], blk, TMP, ti)
                ti += 1
                d = sp.dma(e1_in[l][g * 128:(g + 1) * 128, blk * 512:(blk + 1) * 512], kst[si], reads=[kstb[si]])
                e1_deps.append(d)
        wt, wb = wload_cols(w_in_l, 2176)
        vst = av(TMP + 18 * KB, [16, 128], BF16)
        vstb = Buf()
        for t4 in range(4):
            pb = 2 + t4 % 2
            for tt in range(4):
                t = t4 * 4 + tt
                for c in range(8):
                    pe.op(lambda e, o=psum[:, pb, tt * 128:(tt + 1) * 128], a=hT[:, c, 1 + t * 128:1 + (t + 1) * 128], r=wt[:, c, :],
                          st=(c == 0), sp_=(c == 7): e.matmul(o, a, r, start=st, stop=sp_),
                          reads=[wb, B_hT[t // 4]], writes=[bank[pb]], signal=(c == 7 and tt == 3))
            act.op(lambda e, o=vst[:, t4 * 4:(t4 + 1) * 4, :], i=psum[:, pb, :].rearrange("p (a q) -> p a q", a=4): e.copy(o, i),
                   reads=[bank[pb]], writes=[vstb])
        vdst = e1_in[l][256:384, :].rearrange("p (t n) -> p t n", t=16)
        d = sp.dma(vdst, vst, reads=[vstb])
        e1_deps.append(d)
        eb = Buf()
        eb.w = e1_deps
        e1_done = allgather(e1_in[l], e1_out[l], [eb])

        for c4 in range(4):
            wt, wb = wload_cols(w_in_l, 1536 + c4 * 128)
            for blk in range(4):
                pb = blk % 2
                for c in range(8):
                    pe.op(lambda e, o=PS(pb), a=wt[:, c, :], r=hblk(c, blk), st=(c == 0), sp_=(c == 7):
                          e.matmul(o, a, r, start=st, stop=sp_),
                          reads=[wb, B_hT[blk]], writes=[bank[pb]], signal=(c == 7))
                normrope(pb, PV_GQ + l, qT[:, c4, blk * 512:(blk + 1) * 512], [B_qT[c4][blk]], blk, TMP, ti)
                ti += 1
        k.barrier()
        if stop == (l, 'gqaproj'):
            return True

        B_kf = Buf()
        B_vf = Buf()
        vf5 = vfull.rearrange("p t (s d) -> p t s d", d=64)
        dve.op(lambda e: e.memset(vf5[:, :, 0:5:2, :], 1.0), writes=[B_vf])
        for r in range(2):
            for g in range(2):
                sp.dma(kfull[:, g, r * 2048:(r + 1) * 2048], e1_out[l][384 * r + 128 * g:384 * r + 128 * (g + 1), :],
                       reads=[e1_done], writes=[B_kf])
            vsrc = e1_out[l][384 * r + 256:384 * r + 384, :].rearrange("p (t n) -> p t n", t=16)
            sp.dma(vfull[:, r * 16:(r + 1) * 16, 64:128], vsrc[:, :, 0:64], reads=[e1_done], writes=[B_vf])
            sp.dma(vfull[:, r * 16:(r + 1) * 16, 192:256], vsrc[:, :, 64:128], reads=[e1_done], writes=[B_vf])

        PT = [av(TMP + i * 2 * KB, [1024], BF16) for i in range(2)]
        ptb = [Buf(), Buf()]
        rden = av(TMP + 4 * KB, [1024], F32)
        rdb = Buf()
        it = 0
        rden2 = av(TMP + 4 * KB, [512], F32)
        gsteps = [(hp, qb, kt) for hp in range(4) for qb in range(4) for kt in range(32)]

        def g_front(n):
            hp, qb, kt = gsteps[n]
            g = hp // 2
            c4 = hp
            qs = slice(qb * 512, (qb + 1) * 512)
            sb = 2 * (n % 2)
            for odd in range(2):
                r0 = 64 * odd
                pe.op(lambda e, o=PS(sb + odd), a=kfull[r0:r0 + 64, g, kt * 128:(kt + 1) * 128], r=qT[r0:r0 + 64, c4, qs]:
                      e.matmul(o, a, r, start=True, stop=True),
                      reads=[B_kf, B_qT[c4][qb]], writes=[bank[sb], bank[sb + 1]], signal=(odd == 1))
            act.op(lambda e, o=PT[n % 2], s_=psum[:, sb:sb + 2, :].rearrange("p a q -> p (a q)"): e.activation(o, s_, AF.Exp, scale=0.125),
                   reads=[bank[sb], bank[sb + 1]], writes=[ptb[n % 2]])

        def g_back(n):
            hp, qb, kt = gsteps[n]
            g = hp // 2
            c4 = hp
            qs = slice(qb * 512, (qb + 1) * 512)
            oe = 4 + 2 * ((hp * 4 + qb) % 2)
            p = PT[n % 2]
            for odd in range(2):
                vc0 = (64 if not odd else 0) + 128 * g
                pe.op(lambda e, o=PS(oe + odd), a=vfull[:, kt, vc0:vc0 + 128], r=p[:, odd * 512:(odd + 1) * 512], st=(kt == 0), sp_=(kt == 31):
                      e.matmul(o, a, r, start=st, stop=sp_),
                      reads=[ptb[n % 2], B_vf], writes=[bank[oe], bank[oe + 1]], signal=(odd == 1))
            if kt == 31:
                qbufs = [B_qT[c4][qb]]
                obs = [bank[oe], bank[oe + 1]]
                dve.op(lambda e, o=rden2[64:128, :], i=psum[64:128, oe, :]: e.reciprocal(o, i), reads=obs, writes=[rdb], signal=False)
                dve.op(lambda e, o=rden2[0:64, :], i=psum[0:64, oe + 1, :]: e.reciprocal(o, i), reads=obs, writes=[rdb])
                dve.op(lambda e, o=qT[0:64, c4, qs], a=psum[0:64, oe, :], b=rden2[64:128, :]:
                       e.tensor_tensor(o, a, b, ALU.mult), reads=obs + [rdb], writes=qbufs, signal=False)
                dve.op(lambda e, o=qT[64:128, c4, qs], a=psum[64:128, oe + 1, :], b=rden2[0:64, :]:
                       e.tensor_tensor(o, a, b, ALU.mult), reads=obs + [rdb], writes=qbufs)

        for n in range(len(gsteps) + 1):
            if n < len(gsteps):
                g_front(n)
            if n >= 1:
                g_back(n - 1)
        pass
        proj_residual(w_out_d[l][512:1024, :], 4,
                      lambda c, blk: qT[:, c, blk * 512:(blk + 1) * 512],
                      lambda c, blk: [B_qT[c][blk]])
        k.barrier()
        if stop == (l, 'gqa'):
            return True

        naq = av(0, [4, T], BF16)
        nakT = av(16 * KB, [4, 2560], BF16)
        naV = av(36 * KB, [20, 768], BF16)
        nav5 = naV.rearrange("p t (c s d) -> p t c s d", c=4, s=3)
        NTMP = 66 * KB
        B_naq = [[Buf() for _ in range(8)] for _ in range(4)]
        B_nak = Buf()
        B_nav = Buf()
        def na_qk_proj(which, base_c, dstT, coff):
            for c4 in range(4):
                wt, wb = wload_cols(w_in_l, base_c + c4 * 128)
                for blk in range(4):
                    pb = blk % 2
                    for c in range(8):
                        pe.op(lambda e, o=PS(pb), a=wt[:, c, :], r=hblk(c, blk), st=(c == 0), sp_=(c == 7):
                              e.matmul(o, a, r, start=st, stop=sp_),
                              reads=[wb, B_hT[blk]], writes=[bank[pb]], signal=(c == 7))
                    dst = dstT[:, c4, coff + blk * 512:coff + (blk + 1) * 512]
                    wbs = [B_naq[c4][2 * blk], B_naq[c4][2 * blk + 1]] if which == 0 else [B_nak]
                    if (c4 + blk) % 2 == 0:
                        act.op(lambda e, o=dst, i=PS(pb): e.copy(o, i), reads=[bank[pb]], writes=wbs)
                    else:
                        dve.op(lambda e, o=dst, i=PS(pb): e.tensor_copy(o, i), reads=[bank[pb]], writes=wbs)
        na_qk_proj(1, 512, nakT, 256)
        wv = []
        for c4 in range(4):
            wv.append(wload_cols(w_in_l, 1024 + c4 * 128))
        for t in range(16):
            pb = 2 + t % 2
            for c4 in range(4):
                wt, wb = wv[c4]
                for c in range(8):
                    pe.op(lambda e, o=psum[:, pb, c4 * 128:(c4 + 1) * 128], a=hT[:, c, 1 + t * 128:1 + (t + 1) * 128], r=wt[:, c, :],
                          st=(c == 0), sp_=(c == 7): e.matmul(o, a, r, start=st, stop=sp_),
                          reads=[wb, B_hT[t // 4]], writes=[bank[pb]], signal=(c == 7 and c4 == 3))
            vdst_ = nav5[:, 2 + t, :, 0:3:2, :]
            vsrc_ = psum[:, pb, :].rearrange("p (c e d) -> p c e d", c=4, e=2)
            if t % 2 == 0:
                act.op(lambda e, o=vdst_, i=vsrc_: e.copy(o, i), reads=[bank[pb]], writes=[B_nav])
            else:
                dve.op(lambda e, o=vdst_, i=vsrc_: e.tensor_copy(o, i), reads=[bank[pb]], writes=[B_nav])
        dve.op(lambda e: e.memset(nav5[:, 2:18, :, 1, :], 1.0), writes=[B_nav])
        e2d = []
        kview = lambda t_, r0_: t_[r0_:r0_ + 128, 0:1024].rearrange("p (c n) -> p c n", c=4)
        vview = lambda t_, r0_: t_[r0_:r0_ + 128, :].rearrange("p (s n) -> p s n", s=2)
        e2d.append(sp.dma(kview(e2_in[l], 0), nakT[:, :, 256:512], reads=[B_nak]))
        e2d.append(sp.dma(kview(e2_in[l], 128), nakT[:, :, 2048:2304], reads=[B_nak]))
        e2d.append(sp.dma(vview(e2_in[l], 256), naV[:, 2:4, :], reads=[B_nav]))
        e2d.append(sp.dma(vview(e2_in[l], 384), naV[:, 16:18, :], reads=[B_nav]))
        eb2 = Buf()
        eb2.w = e2d
        e2_done = allgather(e2_in[l], e2_out[l], [eb2])
        sp.dma(nakT[:, :, 0:256], kview(e2_out[l], 128), reads=[e2_done], writes=[B_nak])
        sp.dma(nakT[:, :, 2304:2560], kview(e2_out[l], 512), reads=[e2_done], writes=[B_nak])
        sp.dma(naV[:, 0:2, :], vview(e2_out[l], 384), reads=[e2_done], writes=[B_nav])
        sp.dma(naV[:, 18:20, :], vview(e2_out[l], 512 + 256), reads=[e2_done], writes=[B_nav])
        na_qk_proj(0, 0, naq, 0)
        dve.op(lambda e: e.tensor_scalar(naV[:, 0:2, :], naV[:, 0:2, :], pvc(PV_FLAGL), None, ALU.mult), reads=[B_nav, B_const], writes=[B_nav])
        dve.op(lambda e: e.tensor_scalar(naV[:, 18:20, :], naV[:, 18:20, :], pvc(PV_FLAGR), None, ALU.mult), reads=[B_nav, B_const], writes=[B_nav])
        k.barrier()

        nmask = [hv(i * 1792, [896], BF16) for i in range(3)]
        B_nmask = Buf()
        for i in range(3):
            pool.dma(nmask[i], namask_d[i], writes=[B_nmask] + B_hT)
        gst = [hv(6 * KB + i * 3584, [896], F32) for i in range(2)]
        gstb = [Buf(), Buf()]
        ttf = [hv(14 * KB + i * 1792, [896], BF16) for i in range(2)]
        ttfb = [Buf(), Buf()]
        ttab = [[hv(18 * KB + (i * 3 + m) * 1792, [896], BF16) for m in range(3)] for i in range(2)]
        ttabb = [Buf(), Buf()]
        NPT = [av(NTMP + i * KB, [2, 256], BF16) for i in range(4)]
        nptb = [[Buf(), Buf()] for _ in range(4)]
        NE = [av(NTMP + 4 * KB + i * KB, [2, 256], BF16) for i in range(4)]
        neb = [Buf() for _ in range(4)]
        nrd = av(NTMP + 8 * KB, [256], F32)
        nrdb = Buf()
        def na_tables(h):
            hi = h % 2
            sp.dma(gst[hi], nag_d[l, h], writes=[gstb[hi]])
            act.op(lambda e, o=ttf[hi], i=gst[hi]: e.activation(o, i, AF.Exp), reads=[gstb[hi]], writes=[ttfb[hi]])
            for m in range(3):
                dve.op(lambda e, o=ttab[hi][m], a=ttf[hi], b=nmask[m]: e.tensor_tensor(o, a, b, ALU.mult),
                       reads=[ttfb[hi], B_nmask], writes=[ttabb[hi]])

        steps = [(h, b, wp) for h in range(8) for b in range(8) for wp in range(3)]
        NS_ = len(steps)
        LAG = 3

        def na_front(n):
            h, b, wp = steps[n]
            c4 = h // 2
            r0 = 64 * (h % 2)
            hi = h % 2
            if b == 0 and wp == 0:
                na_tables(h)
            tab = ttab[hi][0 if b == 0 else (2 if b == 7 else 1)]
            qb_ = [B_naq[c4][b]]
            sbk = n % 4
            for j in range(2):
                w = 2 * wp + 1 - j
                s_ = 2 * b + w
                pe.op(lambda e, o=psum[:, sbk, j * 256:(j + 1) * 256], a=nakT[r0:r0 + 64, c4, s_ * 128:(s_ + 1) * 128],
                      r=naq[r0:r0 + 64, c4, b * 256:(b + 1) * 256]: e.matmul(o, a, r, start=True, stop=True),
                      reads=[B_nak] + qb_, writes=[bank[sbk]], signal=(j == 1))
            ii = n % 4
            act.op(lambda e, o=NE[ii], s2=psum[:, sbk, :].rearrange("p (a q) -> p a q", a=2): e.activation(o, s2, AF.Exp, scale=0.125),
                   reads=[bank[sbk]], writes=[neb[ii]])
            w_hi = 2 * wp + 1
            c0 = (10 - 2 * w_hi) * 64
            tsl = tab[:, c0:c0 + 256]
            tsl2 = tab[:, c0 + 128:c0 + 384]
            dve.op(lambda e, o=NPT[ii][:, 0, :], a=NE[ii][:, 0, :], b_=tsl: e.tensor_tensor(o, a, b_, ALU.mult),
                   reads=[neb[ii], ttabb[hi]], writes=[nptb[ii][0]])
            pool.op(lambda e, o=NPT[ii][:, 1, :], a=NE[ii][:, 1, :], b_=tsl2: e.tensor_tensor(o, a, b_, ALU.mult),
                    reads=[neb[ii], ttabb[hi]], writes=[nptb[ii][1]])

        def na_back(n):
            h, b, wp = steps[n]
            c4 = h // 2
            odd = h % 2
            r0 = 64 * odd
            nvc0 = c4 * 192 + (64 if odd else 0)
            ob = 4 + (h * 8 + b) % 4
            ii = n % 4
            qb_ = [B_naq[c4][b]]
            for j in range(2):
                w = 2 * wp + 1 - j
                s_ = 2 * b + w
                first = (wp == 0 and j == 0)
                last = (wp == 2 and j == 1)
                pe.op(lambda e, o=psum[:, ob, 0:256], a=naV[:, s_, nvc0:nvc0 + 128], r=NPT[ii][:, j, :], st=first, sp_=last:
                      e.matmul(o, a, r, start=st, stop=sp_),
                      reads=[nptb[ii][j], B_nav], writes=[bank[ob]], signal=True)
            if wp == 2:
                dr0 = 64 - r0
                dve.op(lambda e, o=nrd[dr0:dr0 + 64, :], i=psum[dr0:dr0 + 64, ob, 0:256]: e.reciprocal(o, i), reads=[bank[ob]], writes=[nrdb])
                dve.op(lambda e, o=naq[r0:r0 + 64, c4, b * 256:(b + 1) * 256], a=psum[r0:r0 + 64, ob, 0:256], b_=nrd[dr0:dr0 + 64, :]:
                       e.tensor_tensor(o, a, b_, ALU.mult), reads=[bank[ob], nrdb], writes=qb_)

        for n in range(NS_ + LAG):
            if n < NS_:
                na_front(n)
            if n - LAG >= 0:
                na_back(n - LAG)
        pass
        if stop == (l, 'na'):
            return True

        proj_residual(w_out_d[l][0:512, :], 4,
                      lambda c, blk: naq[:, c, blk * 512:(blk + 1) * 512],
                      lambda c, blk: [B_naq[c][2 * blk], B_naq[c][2 * blk + 1]])
        k.barrier()
        if stop == (l, 'mixer'):
            return True

        qm = av(0, [4, T], BF16)
        memT = av(16 * KB, [8, 256], F32)
        memn = av(24 * KB, [8, 256], BF16)
        kmT = av(28 * KB, [4, 256], BF16)
        Vm = av(30 * KB, [2, 512], BF16)
        MT = 32 * KB
        B_memT = Buf()
        B_memn = Buf()
        B_km = Buf()
        B_vm = Buf()
        load_transpose(mem_d, 2, memT, lambda t, half: [B_memT], 40 * KB)
        pass
        rmsnorm_T(memT, 256, nbase + 16, lambda c, blk: memn[:, c, :], MT, lambda c, blk: [B_memT], lambda c, blk: [B_memn], nblk=1)
        norm_x_to_hT(nbase + 8, 56 * KB)
        k.barrier()
        w_kv_l = w_mkv_d[l]
        for hd in range(4):
            wt, wb = wload_cols(w_kv_l, hd * 128)
            for c in range(8):
                pe.op(lambda e, o=PS(hd % 2, 256), a=wt[:, c, :], r=memn[:, c, :], st=(c == 0), sp_=(c == 7):
                      e.matmul(o, a, r, start=st, stop=sp_), reads=[wb, B_memn], writes=[bank[hd % 2]], signal=(c == 7))
            act.op(lambda e, o=kmT[:, hd, :], i=PS(hd % 2, 256): e.copy(o, i), reads=[bank[hd % 2]], writes=[B_km])
        for c4 in range(4):
            wt, wb = wload_cols(w_kv_l, 512 + c4 * 128)
            for mt in range(2):
                pb = 2 + (c4 * 2 + mt) % 2
                for c in range(8):
                    pe.op(lambda e, o=PS(pb, 128), a=memn[:, c, mt * 128:(mt + 1) * 128], r=wt[:, c, :], st=(c == 0), sp_=(c == 7):
                          e.matmul(o, a, r, start=st, stop=sp_), reads=[wb, B_memn], writes=[bank[pb]], signal=(c == 7))
                dve.op(lambda e, o=Vm[:, mt, c4 * 128:(c4 + 1) * 128], i=PS(pb, 128): e.tensor_copy(o, i), reads=[bank[pb]], writes=[B_vm])
        B_qm = [[Buf() for _ in range(4)] for _ in range(4)]
        w_q_l = w_mq_d[l]
        for hd in range(4):
            wt, wb = wload_cols(w_q_l, hd * 128)
            for blk in range(4):
                pb = 4 + blk % 2
                for c in range(8):
                    pe.op(lambda e, o=PS(pb), a=wt[:, c, :], r=hblk(c, blk), st=(c == 0), sp_=(c == 7):
                          e.matmul(o, a, r, start=st, stop=sp_), reads=[wb, B_hT[blk]], writes=[bank[pb]], signal=(c == 7))
                dst = qm[:, hd, blk * 512:(blk + 1) * 512]
                if blk % 2 == 0:
                    act.op(lambda e, o=dst, i=PS(pb): e.copy(o, i), reads=[bank[pb]], writes=[B_qm[hd][blk]])
                else:
                    dve.op(lambda e, o=dst, i=PS(pb): e.tensor_copy(o, i), reads=[bank[pb]], writes=[B_qm[hd][blk]])
        MPT = [av(MT + i * 2 * KB, [2, 512], BF16) for i in range(2)]
        mptb = [Buf(), Buf()]
        mrd = av(MT + 4 * KB, [512], F32)
        mrdb = Buf()
        msc = float(128 ** -0.5)
        it = 0
        for blk in range(4):
            for hd in range(4):
                sbk = 2 * (it % 2)
                for mt in range(2):
                    pe.op(lambda e, o=PS(sbk + mt), a=kmT[:, hd, mt * 128:(mt + 1) * 128], r=qm[:, hd, blk * 512:(blk + 1) * 512]:
                          e.matmul(o, a, r, start=True, stop=True),
                          reads=[B_km, B_qm[hd][blk]], writes=[bank[sbk], bank[sbk + 1]], signal=(mt == 1))
                p = MPT[it % 2]
                pb_ = mptb[it % 2]
                act.op(lambda e, o=p, s_=psum[:, sbk:sbk + 2, :]: e.activation(o, s_, AF.Exp, scale=msc),
                       reads=[bank[sbk], bank[sbk + 1]], writes=[pb_])
                ob = 4 + 2 * (it % 2)
                it += 1
                for mt in range(2):
                    pe.op(lambda e, o=PS(ob), a=Vm[:, mt, hd * 128:(hd + 1) * 128], r=p[:, mt, :], st=(mt == 0), sp_=(mt == 1):
                          e.matmul(o, a, r, start=st, stop=sp_), reads=[pb_, B_vm], writes=[bank[ob]], signal=(mt == 1))
                for mt in range(2):
                    pe.op(lambda e, o=PS(ob + 1), r=p[:, mt, :], st=(mt == 0), sp_=(mt == 1):
                          e.matmul(o, ones128, r, start=st, stop=sp_), reads=[pb_, B_const], writes=[bank[ob + 1]], signal=(mt == 1))
                dve.op(lambda e, i=PS(ob + 1): e.reciprocal(mrd, i), reads=[bank[ob + 1]], writes=[mrdb])
                dve.op(lambda e, o=qm[:, hd, blk * 512:(blk + 1) * 512], a=PS(ob): e.tensor_tensor(o, a, mrd, ALU.mult),
                       reads=[bank[ob], mrdb], writes=[B_qm[hd][blk]])
        pass
        proj_residual(w_mo_d[l], 4, lambda c, blk: qm[:, c, blk * 512:(blk + 1) * 512], lambda c, blk: [B_qm[c][blk]])
        k.barrier()
        if stop == (l, 'mem'):
            return True

        norm_x_to_hT(nbase + 24, 64 * KB)
        hst = av(0, [2, 8], BF16)
        hstb = Buf()
        dve.op(lambda e: e.tensor_copy(hst[:, 0, :], hT[:, :, 1]), reads=[B_hT[0]], writes=[hstb])
        dve.op(lambda e: e.tensor_copy(hst[:, 1, :], hT[:, :, 2048]), reads=[B_hT[3]], writes=[hstb])
        d = sp.dma(e3_in[l][:, :].rearrange("p (a b) -> p a b", a=2), hst, reads=[hstb])
        eb3 = Buf()
        eb3.w = [d]
        e3_done = allgather(e3_in[l], e3_out[l], [eb3])
        hrx = av(64, [2, 8], BF16)
        hrxb = Buf()
        sp.dma(hrx[:, 0, :], e3_out[l][0:128, 8:16], reads=[e3_done], writes=[hrxb])
        sp.dma(hrx[:, 1, :], e3_out[l][128:256, 0:8], reads=[e3_done], writes=[hrxb])
        dve.op(lambda e: e.tensor_scalar(hT[:, :, 0], hrx[:, 0, :], pvc(PV_FLAGL), None, ALU.mult), reads=[hrxb, B_const], writes=[B_hhalo])
        dve.op(lambda e: e.tensor_scalar(hT[:, :, 2049], hrx[:, 1, :], pvc(PV_FLAGR), None, ALU.mult), reads=[hrxb, B_const], writes=[B_hhalo])
        k.barrier()
        if stop == (l, 'ffnhalo'):
            return True

        actT = av(1 * KB, [22, 1024], BF16)
        TG = [av(45 * KB + i * 4 * KB, [1024], F32) for i in range(2)]
        TV = [av(53 * KB + i * 4 * KB, [1024], F32) for i in range(2)]
        UC = [av(61 * KB + i * 4128, [1032], F32) for i in range(2)]
        tgb = [Buf(), Buf()]
        tvb = [Buf(), Buf()]
        ucb = [Buf(), Buf()]
        WD = [av(61 * KB + 8256 + i * 5632, [22, 128], BF16) for i in range(2)]
        wdb = [Buf(), Buf()]
        B_act = [Buf() for _ in range(22)]
        cwb = PV_CW + l * 176
        w_up_l = w_up_d[l]
        w_dn_l = w_dn_d[l]
        UHB = 7
        uhb = [bank[6], bank[7]]
        for hh in range(int(os.environ.get('FFN_HH', '2'))):
            base = 1 + hh * 1024
            uslot = 0
            order_ = [cp_ + 22 * g_ for cp_ in range(int(os.environ.get('FFN_CP', '22'))) for g_ in range(2)]
            loaded_ = {}
            PF_ = 5
            for s0 in range(min(PF_, len(order_))):
                loaded_[s0] = wload_cols(w_up_l, order_[s0] * 128)
            for cp in range(int(os.environ.get('FFN_CP', '22'))):
                for gv_ in range(2):
                    j = cp + 22 * gv_
                    si_ = 2 * cp + gv_
                    if si_ + PF_ < len(order_):
                        loaded_[si_ + PF_] = wload_cols(w_up_l, order_[si_ + PF_] * 128)
                    wt, wb = loaded_.pop(si_)
                    ub = 2 * (uslot % 3)
                    uc = UC[uslot % 2]
                    ucb_ = ucb[uslot % 2]
                    uslot += 1
                    for sb_ in range(2):
                        for c in range(8):
                            pe.op(lambda e, o=PS(ub + sb_), a=wt[:, c, :], r=hT[:, c, base + sb_ * 512:base + (sb_ + 1) * 512],
                                  st=(c == 0), sp_=(c == 7): e.matmul(o, a, r, start=st, stop=sp_),
                                  reads=[wb, B_hT[hh * 2 + sb_]], writes=[bank[ub], bank[ub + 1]], signal=(c == 7 and sb_ == 1))
                    uhi_ = si_ % 2
                    uh = psum[:, 6 + uhi_, 0:2]
                    for c in range(8):
                        pe.op(lambda e, o=uh, a=wt[:, c, :], r=hT[:, c, base - 1:base + 1025:1025], st=(c == 0), sp_=(c == 7):
                              e.matmul(o, a, r, start=st, stop=sp_),
                              reads=[wb, B_hT[0], B_hT[1], B_hT[2], B_hT[3], B_hhalo], writes=[uhb[uhi_]], signal=(c == 7))
                    U = psum[:, ub:ub + 2, :].rearrange("p a q -> p (a q)")
                    tt_ = (TG if gv_ == 0 else TV)[cp % 2]
                    ttb_ = (tgb if gv_ == 0 else tvb)[cp % 2]
                    ub_ = [bank[ub], bank[ub + 1]]
                    act.op(lambda e, o=uc[:, 1:1025], u=U: e.copy(o, u), reads=ub_, writes=[ucb_])
                    act.op(lambda e, o=uc[:, 0:1026:1025], u=uh: e.copy(o, u), reads=[uhb[uhi_]], writes=[ucb_])
                    act.op(lambda e, o=tt_, u=U, s_=pvc(cwb + 44 + j), b_=pvc(cwb + 132 + j):
                           e.activation(o, u, AF.Identity, bias=b_, scale=s_), reads=ub_ + [B_const], writes=[ttb_])
                    dve.op(lambda e, o=tt_, u=uc[:, 0:1024], s_=pvc(cwb + j):
                           e.scalar_tensor_tensor(o, u, s_, o, ALU.mult, ALU.add), reads=[ucb_, ttb_, B_const], writes=[ttb_])
                    dve.op(lambda e, o=tt_, u=uc[:, 2:1026], s_=pvc(cwb + 88 + j):
                           e.scalar_tensor_tensor(o, u, s_, o, ALU.mult, ALU.add), reads=[ucb_, ttb_, B_const], writes=[ttb_])
                act.op(lambda e, o=TG[cp % 2]: e.activation(o, o, AF.Silu), reads=[tgb[cp % 2]], writes=[tgb[cp % 2]])
                pool.op(lambda e, o=actT[:, cp, :], a=TG[cp % 2], b_=TV[cp % 2]: e.tensor_tensor(o, a, b_, ALU.mult),
                        reads=[tgb[cp % 2], tvb[cp % 2]], writes=[B_act[cp]])
            for oc in range(int(os.environ.get('FFN_OC', '8'))):
                wi = oc % 2
                pool.dma(WD[wi], w_dn_l[:, oc * 128:(oc + 1) * 128].rearrange("(kc p) n -> p kc n", p=128), writes=[wdb[wi]])
                for sb_ in range(2):
                    pb = (oc * 2 + sb_) % 6
                    for c in range(22):
                        pe.op(lambda e, o=PS(pb), a=WD[wi][:, c, :], r=actT[:, c, sb_ * 512:(sb_ + 1) * 512], st=(c == 0), sp_=(c == 21):
                              e.matmul(o, a, r, start=st, stop=sp_), reads=[wdb[wi], B_act[c]], writes=[bank[pb]], signal=(c == 21))
                    blk = hh * 2 + sb_
                    xs = xT[:, oc, blk * 512:(blk + 1) * 512]
                    dve.op(lambda e, o=xs, p_=PS(pb): e.tensor_tensor(o, o, p_, ALU.add), reads=[bank[pb], B_xT[oc][blk]], writes=[B_xT[oc][blk]])
        k.barrier()
        return False

    if stop != 'load':
        for l in range(nlayers):
            if layer_body(l):
                break

    yT = av(0, [8, 512], F32)
    yT2 = [av(i * 16 * KB, [8, 512], F32) for i in range(2)]
    ytb = [Buf(), Buf()]
    ost = [av(32 * KB + i * 4 * KB, [1024], F32) for i in range(4)]
    ostb = [Buf() for _ in range(4)]
    out_deps = []
    cnt = {"n": 0}

    def emit_out(blk, src_of, src_b):
        for tt in range(4):
            t = blk * 4 + tt
            oi = t % 4
            for half in range(2):
                pb = cnt["n"] % 8
                cnt["n"] += 1
                for j in range(4):
                    c = half * 4 + j
                    pe.op(lambda e, o=psum[:, pb, j * 128:(j + 1) * 128], i=src_of(c, blk, tt): e.transpose(o, i, ident),
                          reads=src_b(c, blk) + [B_const], writes=[bank[pb]], signal=(j == 3))
                dst = ost[oi][:, half * 512:(half + 1) * 512]
                if half == 0:
                    dve.op(lambda e, o=dst, i=PS(pb): e.tensor_copy(o, i), reads=[bank[pb]], writes=[ostb[oi]])
                else:
                    act.op(lambda e, o=dst, i=PS(pb): e.copy(o, i), reads=[bank[pb]], writes=[ostb[oi]])
            out_deps.append(sp.dma(out_d[t * 128:(t + 1) * 128, :], ost[oi], reads=[ostb[oi]]))

    if final_norm:
        rmsnorm_T(xT, T, PV_FINAL, lambda c, blk: yT2[blk % 2][:, c, :], 56 * KB,
                  lambda c, blk: [B_xT[c][blk]], lambda c, blk: [ytb[blk % 2]],
                  after_blk=lambda blk: emit_out(blk, lambda c, b_, tt: yT2[b_ % 2][:, c, tt * 128:(tt + 1) * 128],
                                                 lambda c, b_: [ytb[b_ % 2]]))
    else:
        for blk in range(4):
            emit_out(blk, lambda c, b_, tt: xT[:, c, b_ * 512 + tt * 128:b_ * 512 + (tt + 1) * 128],
                     lambda c, b_: [B_xT[c][b_]])
    for d in out_deps:
        sp._wait(d)
    k.barrier()

    if os.environ.get('SEMDBG'):
        print('SEMS', [(e.name, e.sem.cnt, len(e.items)) for e in k.engs], [(e.name, [d.cnt for d in e.dsems]) for e in k.engs])
    with nc.Block() as block:
        @block.sync
        def _(e):
            sp.replay(e)

        @block.gpsimd
        def _(e):
            pool.replay(e)

        @block.tensor
        def _(e):
            pe.replay(e)

        @block.vector
        def _(e):
            dve.replay(e)

        @block.scalar
        def _(e):
            act.replay(e)
    stack.close()
    return nc


def _consts(p):
    ident = np.eye(128, dtype=np.float32)
    R = np.zeros((128, 128), np.float32)
    for blk in range(2):
        for a in range(2):
            o = blk * 64 + a * 32
            for f in range(16):
                R[o + f, o + 16 + f] = -1.0
                R[o + 16 + f, o + f] = 1.0
    rmatT = np.ascontiguousarray(R.T)
    t = np.arange(T) + p * T
    pos = np.stack([t // 64, t % 64], 0).astype(np.float32)
    inv = (10000.0 ** (-np.arange(16, dtype=np.float32) / 16)).astype(np.float32)
    cosT = np.zeros((128, T), np.float32)
    sinT = np.zeros((128, T), np.float32)
    for d in range(128):
        dd = d % 64
        a = dd // 32
        f = dd % 16
        ang = (pos[a] * inv[f]).astype(np.float32)
        cosT[d] = np.cos(ang)
        sinT[d] = np.sin(ang)
    kr = (np.arange(128) // 64)[:, None, None]
    kc = (np.arange(128) % 64)[:, None, None]
    idx = np.arange(14)[None, :, None]
    qc = np.arange(64)[None, None, :]
    dr = 6 + kr - idx + 0 * qc
    cs = np.clip(qc - 8, 0, 48)
    colvalid = (kc >= cs) & (kc < cs + 16)
    band = (dr >= -4) & (dr <= 3)
    full = colvalid & (dr >= -7) & (dr <= 7)
    bandm = colvalid & band
    mA = full if p == 0 else bandm
    mC = bandm if p == 0 else full
    namask = np.stack([mA, bandm, mC], 0).astype(np.float32).reshape(3, 128, 896)
    return ident, rmatT, cosT, sinT, namask


def _na_gather(na_rpb):
    kr = (np.arange(128) // 64)[:, None, None]
    kc = (np.arange(128) % 64)[:, None, None]
    idx = np.arange(14)[None, :, None]
    qc = np.arange(64)[None, None, :]
    dr = np.clip(6 + kr - idx + 0 * qc, -7, 7) + 7
    dc = np.clip(kc - qc + 0 * idx, -15, 15) + 15
    g = na_rpb[:, :, dr, dc]
    return np.ascontiguousarray(g.reshape(L, 8, 128, 896)).astype(np.float32)


def _pv(p, inp):
    pv = np.zeros((128, PV_N), np.float32)

    def col8(v):
        return np.asarray(v, np.float32).reshape(8, 128).T

    for l in range(L):
        b = PV_NORM + l * 32
        pv[:, b + 0:b + 8] = col8(inp["norm_mix"][l])
        pv[:, b + 8:b + 16] = col8(inp["norm_mem_q"][l])
        pv[:, b + 16:b + 24] = col8(inp["norm_mem_kv"][l])
        pv[:, b + 24:b + 32] = col8(inp["norm_ffn"][l])
        pv[:, PV_GQ + l] = np.tile(np.asarray(inp["gqa_q_norm"][l], np.float32), 2)
        pv[:, PV_GK + l] = np.tile(np.asarray(inp["gqa_k_norm"][l], np.float32), 2)
        cw = np.asarray(inp["conv_w"][l], np.float32)
        cb = np.asarray(inp["conv_b"][l], np.float32)
        o = PV_CW + l * 176
        for kk in range(3):
            pv[:, o + kk * 44:o + (kk + 1) * 44] = cw[kk].reshape(44, 128).T
        pv[:, o + 132:o + 176] = cb.reshape(44, 128).T
    pv[:, PV_FINAL:PV_FINAL + 8] = col8(inp["norm_final"])
    pv[:, PV_FLAGL] = 1.0 if p == 1 else 0.0
    pv[:, PV_FLAGR] = 1.0 if p == 0 else 0.0
    return pv


_CACHE = {}


def kernel(**inputs):
    inp = {k_: np.asarray(v) for k_, v in inputs.items()}
    if "nc" not in _CACHE:
        _CACHE["nc"] = build_nc()
    nc = _CACHE["nc"]
    nag = _na_gather(np.asarray(inp["na_rpb"], np.float32))
    shared = {n_: np.ascontiguousarray(inp[n_], dtype=np.float32) for n_ in
              ("w_in", "w_out", "w_mem_q", "w_mem_kv", "w_mem_o", "w_up", "w_down")}
    in_maps = []
    for c in range(8):
        b, p = c // 2, c % 2
        ident, rmatT, cosT, sinT, namask = _consts(p)
        m = dict(shared)
        m["x"] = np.ascontiguousarray(inp["x"][b, p * T:(p + 1) * T, :], dtype=np.float32)
        m["mem"] = np.ascontiguousarray(inp["mem"][b], dtype=np.float32)
        m["pv"] = _pv(p, inp)
        m["ident"] = ident
        m["rmatT"] = rmatT
        m["cosT"] = cosT
        m["sinT"] = sinT
        m["nag"] = nag
        m["namask"] = namask
        in_maps.append(m)
    res = run_bass_kernel_spmd(nc, in_maps, core_ids=list(range(8)))
    out = np.zeros((4, 4096, D), np.float32)
    for c in range(8):
        b, p = c // 2, c % 2
        out[b, p * T:(p + 1) * T, :] = np.asarray(res.results[c]["out"], dtype=np.float32)
    return out
```
